# Optimizing a Trainium2 kernel written in Bass

```python
import math
import jax, jax.numpy as jnp
from jax import lax
import numpy as np

D_MODEL = 4096
BATCH = 8
SEQ = 2048
DEPTH = 2

GRID_W = 64
CTX_LEN = 256
HEAD_DIM = 128
A_WIDTH = (3 * D_MODEL) // 8
A_HEADS = A_WIDTH // HEAD_DIM
A_QK_DIM = HEAD_DIM // 2
A_V_DIM = HEAD_DIM
B_WIDTH = (3 * D_MODEL) // 8
B_HEADS = B_WIDTH // HEAD_DIM
B_HEAD_DIM = HEAD_DIM
C_WIDTH = D_MODEL - A_WIDTH - B_WIDTH
C_BLOCKS = 16
C_BLOCK_DIM = C_WIDTH // C_BLOCKS
MIX_WIDTH = A_WIDTH + B_WIDTH + C_WIDTH
SPLIT_SIZES = (A_WIDTH, A_WIDTH, A_WIDTH, A_WIDTH,
               B_WIDTH, B_WIDTH, B_WIDTH, B_WIDTH,
               C_WIDTH, C_WIDTH)
IN_WIDTH = 4 * A_WIDTH + 4 * B_WIDTH + 2 * C_WIDTH

NA_KH = 8
NA_KW = 16
ROPE_THETA = 10000.0
RGLRU_C = 8.0
CONV_W = 4
CONV_LEFT = 2
Q_BLOCK = 128
NORM_EPS = 1e-6
SUBLN_EPS = 1e-5
NEG_INF = -1e30

kernel_name = "hybrid_diffattn_natten_rglru_dit"


def rms_norm(x, w, eps=NORM_EPS):
    x32 = x.astype(jnp.float32)
    y = x32 * lax.rsqrt(jnp.mean(x32 * x32, axis=-1, keepdims=True) + eps)
    return (y * w.astype(jnp.float32)).astype(x.dtype)


def split_columns(p):
    points = [int(v) for v in np.cumsum(SPLIT_SIZES)[:-1]]
    return jnp.split(p, points, axis=-1)


def _rope_1d(x, pos):
    n = x.shape[-1] // 2
    freqs = ROPE_THETA ** (-jnp.arange(n, dtype=jnp.float32) / n)
    ang = pos.astype(jnp.float32)[:, None] * freqs[None, :]
    cos = jnp.cos(ang)[None, :, None, None, :]
    sin = jnp.sin(ang)[None, :, None, None, :]
    x32 = x.astype(jnp.float32)
    x1, x2 = x32[..., :n], x32[..., n:]
    return jnp.concatenate([x1 * cos - x2 * sin, x2 * cos + x1 * sin], axis=-1).astype(x.dtype)


def axial_rope(x, rows, cols):
    half = x.shape[-1] // 2
    return jnp.concatenate([_rope_1d(x[..., :half], rows), _rope_1d(x[..., half:], cols)], axis=-1)


def diff_softmax_attn(q, k, v, lam):
    s = jnp.einsum('bqhnd,bkhnd->nbhqk', q, k).astype(jnp.float32) * (A_QK_DIM ** -0.5)
    p = jax.nn.softmax(s, axis=-1)
    attn = (p[0] - lam * p[1]).astype(v.dtype)
    return jnp.einsum('bhqk,bkhe->bqhe', attn, v)


def diff_attention_latent(q, k_all, v_all, lam):
    bn, s = q.shape[:2]
    nqb = s // Q_BLOCK
    qb = q.reshape(bn, nqb, Q_BLOCK, *q.shape[2:]).swapaxes(0, 1)
    o = lax.map(lambda q_blk: diff_softmax_attn(q_blk, k_all, v_all, lam), qb)
    return o.swapaxes(0, 1).reshape(bn, s, *o.shape[3:])


def dense_attn(q, k, v):
    s = jnp.einsum('bqhd,bkhd->bhqk', q, k).astype(jnp.float32) * (q.shape[-1] ** -0.5)
    p = jax.nn.softmax(s, axis=-1).astype(v.dtype)
    return jnp.einsum('bhqk,bkhd->bqhd', p, v)


def neighbourhood_attention(q, k, v, k_ctx, v_ctx, rpb):
    bn, s, h, dh = q.shape
    rows_n = s // GRID_W
    kh = min(NA_KH, rows_n)
    w = GRID_W
    scale = dh ** -0.5
    n_ctx = k_ctx.shape[1]
    q_rows = q.reshape(bn, rows_n, w, h, dh).swapaxes(0, 1)
    k_grid = k.reshape(bn, rows_n, w, h, dh)
    v_grid = v.reshape(bn, rows_n, w, h, dh)
    cq = jnp.arange(w)
    c0 = jnp.clip(cq - NA_KW // 2, 0, w - NA_KW)
    ck = jnp.arange(w)
    col_valid = (ck[None, :] >= c0[:, None]) & (ck[None, :] < c0[:, None] + NA_KW)
    dc_idx = jnp.clip(ck[None, :] - cq[:, None] + NA_KW - 1, 0, 2 * NA_KW - 2)
    rpb_col = rpb[:, :, dc_idx].astype(jnp.float32)

    def row_block(args):
        r, q_row = args
        r0 = jnp.clip(r - kh // 2, 0, rows_n - kh)
        k_slab = lax.dynamic_slice_in_dim(k_grid, r0, kh, axis=1)
        v_slab = lax.dynamic_slice_in_dim(v_grid, r0, kh, axis=1)
        dr_idx = r0 + jnp.arange(kh) - r + NA_KH - 1
        bias = jnp.transpose(rpb_col[:, dr_idx], (0, 2, 1, 3))
        s_nb = jnp.einsum('bqhd,bikhd->bhqik', q_row, k_slab).astype(jnp.float32) * scale + bias[None]
        s_nb = jnp.where(col_valid[:, None, :], s_nb, NEG_INF)
        s_ctx = jnp.einsum('bqhd,bkhd->bhqk', q_row, k_ctx).astype(jnp.float32) * scale
        p = jax.nn.softmax(jnp.concatenate([s_ctx, s_nb.reshape(bn, h, w, kh * w)], axis=-1),
                           axis=-1).astype(v.dtype)
        p_nb = p[..., n_ctx:].reshape(bn, h, w, kh, w)
        return (jnp.einsum('bhqk,bkhd->bqhd', p[..., :n_ctx], v_ctx)
                + jnp.einsum('bhqik,bikhd->bqhd', p_nb, v_slab))

    o = lax.map(row_block, (jnp.arange(rows_n), q_rows))
    return o.swapaxes(0, 1).reshape(bn, s, h * dh)


def dwconv_centred(u, w, b):
    s = u.shape[1]
    up = jnp.pad(u, ((0, 0), (CONV_LEFT, CONV_W - 1 - CONV_LEFT), (0, 0)))
    acc = b
    for j in range(CONV_W):
        acc = acc + w[j] * up[:, j:j + s]
    return acc


def linear_scan(a, b, h0, reverse):
    if reverse:
        a = jnp.flip(a, axis=1)
        b = jnp.flip(b, axis=1)
    b = b.at[:, 0].add(a[:, 0] * h0)

    def combine(e1, e2):
        a1, b1 = e1
        a2, b2 = e2
        return a1 * a2, a2 * b1 + b2

    _, h = lax.associative_scan(combine, (a, b), axis=1)
    final = h[:, -1]
    if reverse:
        h = jnp.flip(h, axis=1)
    return h, final


def rglru_gates(u32, wa, ba, wx, bx, lam):
    bn, s, _ = u32.shape
    ub = u32.reshape(bn, s, C_BLOCKS, C_BLOCK_DIM)
    r = jax.nn.sigmoid(jnp.einsum('bsni,nij->bsnj', ub, wa.astype(jnp.float32)).reshape(bn, s, C_WIDTH)
                       + ba.astype(jnp.float32))
    i = jax.nn.sigmoid(jnp.einsum('bsni,nij->bsnj', ub, wx.astype(jnp.float32)).reshape(bn, s, C_WIDTH)
                       + bx.astype(jnp.float32))
    log_a = -RGLRU_C * r * jax.nn.softplus(-lam.astype(jnp.float32))
    a = jnp.exp(log_a)
    b = jnp.sqrt(-jnp.expm1(2.0 * log_a)) * (i * u32)
    return a, b


def rglru_bidirectional(u_lat, u_ctx, wa, ba, wx, bx, lam, need_ctx):
    bn = u_lat.shape[0]
    ul32 = u_lat.astype(jnp.float32)
    uc32 = u_ctx.astype(jnp.float32)
    y_lat = []
    y_ctx = []
    for d in range(2):
        rev = d == 1
        a_c, b_c = rglru_gates(uc32, wa[d], ba[d], wx[d], bx[d], lam[d])
        h_c, fin_c = linear_scan(a_c, b_c, jnp.zeros((bn, C_WIDTH), jnp.float32), rev)
        a_l, b_l = rglru_gates(ul32, wa[d], ba[d], wx[d], bx[d], lam[d])
        h_l, _ = linear_scan(a_l, b_l, fin_c, rev)
        y_lat.append(h_l)
        y_ctx.append(h_c)
    out_lat = (y_lat[0] + y_lat[1]).astype(u_lat.dtype)
    out_ctx = (y_ctx[0] + y_ctx[1]).astype(u_ctx.dtype) if need_ctx else None
    return out_lat, out_ctx


def hybrid_layer(l, x, ctx, c, c_ctx, ada_w, ada_b, norm_w, w_in, w_out, lambda_qk, subln_w, rpb,
                 conv_w, conv_b, rg_wa, rg_ba, rg_wx, rg_bx, rg_lambda, need_ctx):
    bn, s, d = x.shape
    n_ctx = ctx.shape[1]
    t = jnp.arange(s)
    rows = t // GRID_W
    cols = t % GRID_W

    mod = jax.nn.silu(c) @ ada_w + ada_b
    shift, scale, gate = jnp.split(mod[:, None, :], 3, axis=-1)
    mod_c = jax.nn.silu(c_ctx) @ ada_w + ada_b
    shift_c, scale_c, gate_c = jnp.split(mod_c, 3, axis=-1)
    hx = rms_norm(x, norm_w) * (1.0 + scale) + shift
    hc = rms_norm(ctx, norm_w) * (1.0 + scale_c) + shift_c

    qa, ka, va, ga, qb, kb, vb, gb, xc, gc = split_columns(hx @ w_in)
    qa_c, ka_c, va_c, ga_c, qb_c, kb_c, vb_c, gb_c, xc_c, gc_c = split_columns(hc @ w_in)

    lam_init = 0.8 - 0.6 * math.exp(-0.3 * l)
    lq = lambda_qk.astype(jnp.float32)
    lam = jnp.exp(jnp.sum(lq[0] * lq[1])) - jnp.exp(jnp.sum(lq[2] * lq[3])) + lam_init
    qa_l = axial_rope(qa.reshape(bn, s, A_HEADS, 2, A_QK_DIM), rows, cols)
    ka_l = axial_rope(ka.reshape(bn, s, A_HEADS, 2, A_QK_DIM), rows, cols)
    va_l = va.reshape(bn, s, A_HEADS, A_V_DIM)
    ka_cx = ka_c.reshape(bn, n_ctx, A_HEADS, 2, A_QK_DIM)
    va_cx = va_c.reshape(bn, n_ctx, A_HEADS, A_V_DIM)
    k_all = jnp.concatenate([ka_cx, ka_l], axis=1)
    v_all = jnp.concatenate([va_cx, va_l], axis=1)
    oa = diff_attention_latent(qa_l, k_all, v_all, lam)
    oa = (rms_norm(oa, subln_w, SUBLN_EPS) * (1.0 - lam_init)).reshape(bn, s, A_WIDTH)

    kb_cx = kb_c.reshape(bn, n_ctx, B_HEADS, B_HEAD_DIM)
    vb_cx = vb_c.reshape(bn, n_ctx, B_HEADS, B_HEAD_DIM)
    ob = neighbourhood_attention(qb.reshape(bn, s, B_HEADS, B_HEAD_DIM),
                                 kb.reshape(bn, s, B_HEADS, B_HEAD_DIM),
                                 vb.reshape(bn, s, B_HEADS, B_HEAD_DIM), kb_cx, vb_cx, rpb)

    u_l = dwconv_centred(xc, conv_w, conv_b)
    u_c = dwconv_centred(xc_c, conv_w, conv_b)
    oc, oc_c = rglru_bidirectional(u_l, u_c, rg_wa, rg_ba, rg_wx, rg_bx, rg_lambda, need_ctx)

    merged = jnp.concatenate([oa * jax.nn.silu(ga), ob * jax.nn.silu(gb), oc * jax.nn.silu(gc)], axis=-1)
    x = x + gate * (merged @ w_out)

    if need_ctx:
        oa_c = diff_softmax_attn(qa_c.reshape(bn, n_ctx, A_HEADS, 2, A_QK_DIM), ka_cx, va_cx, lam)
        oa_c = (rms_norm(oa_c, subln_w, SUBLN_EPS) * (1.0 - lam_init)).reshape(bn, n_ctx, A_WIDTH)
        ob_c = dense_attn(qb_c.reshape(bn, n_ctx, B_HEADS, B_HEAD_DIM), kb_cx, vb_cx).reshape(bn, n_ctx, B_WIDTH)
        merged_c = jnp.concatenate([oa_c * jax.nn.silu(ga_c), ob_c * jax.nn.silu(gb_c),
                                    oc_c * jax.nn.silu(gc_c)], axis=-1)
        ctx = ctx + gate_c * (merged_c @ w_out)
    return x, ctx


def setup_inputs(seed: int = 0) -> dict:
    key = jax.random.key(seed)
    ks = jax.random.split(key, 20)
    f32 = jnp.float32
    x = jax.random.normal(ks[0], (BATCH, SEQ, D_MODEL), f32)
    c = jax.random.normal(ks[1], (BATCH, D_MODEL), f32)
    ctx = jax.random.normal(ks[2], (BATCH, CTX_LEN, D_MODEL), f32)
    c_ctx = jax.random.normal(ks[3], (D_MODEL,), f32)
    ada_w = jax.random.normal(ks[4], (DEPTH, D_MODEL, 3 * D_MODEL), f32) * (0.5 * D_MODEL ** -0.5)
    ada_b = jax.random.normal(ks[5], (DEPTH, 3 * D_MODEL), f32) * 0.02
    norm_w = 1.0 + 0.02 * jax.random.normal(ks[6], (DEPTH, D_MODEL), f32)
    w_in = jax.random.normal(ks[7], (DEPTH, D_MODEL, IN_WIDTH), f32) * (D_MODEL ** -0.5)
    w_out = jax.random.normal(ks[8], (DEPTH, MIX_WIDTH, D_MODEL), f32) * (MIX_WIDTH ** -0.5)
    lambda_qk = jax.random.normal(ks[9], (DEPTH, 4, A_QK_DIM), f32) * 0.1
    subln_w = 1.0 + 0.02 * jax.random.normal(ks[10], (DEPTH, A_V_DIM), f32)
    rpb = jax.random.normal(ks[11], (DEPTH, B_HEADS, 2 * NA_KH - 1, 2 * NA_KW - 1), f32) * 0.1
    conv_w = jax.random.normal(ks[12], (DEPTH, CONV_W, C_WIDTH), f32) * (CONV_W ** -0.5)
    conv_b = jax.random.normal(ks[13], (DEPTH, C_WIDTH), f32) * 0.02
    rg_wa = jax.random.normal(ks[14], (DEPTH, 2, C_BLOCKS, C_BLOCK_DIM, C_BLOCK_DIM), f32) * (C_BLOCK_DIM ** -0.5)
    rg_ba = jax.random.normal(ks[15], (DEPTH, 2, C_WIDTH), f32) * 0.02
    rg_wx = jax.random.normal(ks[16], (DEPTH, 2, C_BLOCKS, C_BLOCK_DIM, C_BLOCK_DIM), f32) * (C_BLOCK_DIM ** -0.5)
    rg_bx = jax.random.normal(ks[17], (DEPTH, 2, C_WIDTH), f32) * 0.02
    a_base = jax.random.uniform(ks[18], (DEPTH, 2, C_WIDTH), f32, minval=0.9, maxval=0.999) ** (1.0 / RGLRU_C)
    rg_lambda = jnp.log(a_base) - jnp.log1p(-a_base)
    final_norm_w = 1.0 + 0.02 * jax.random.normal(ks[19], (D_MODEL,), f32)
    return {"x": x, "c": c, "ctx": ctx, "c_ctx": c_ctx, "ada_w": ada_w, "ada_b": ada_b,
            "norm_w": norm_w, "w_in": w_in, "w_out": w_out, "lambda_qk": lambda_qk,
            "subln_w": subln_w, "rpb": rpb, "conv_w": conv_w, "conv_b": conv_b,
            "rg_wa": rg_wa, "rg_ba": rg_ba, "rg_wx": rg_wx, "rg_bx": rg_bx,
            "rg_lambda": rg_lambda, "final_norm_w": final_norm_w}


def reference(x, c, ctx, c_ctx, ada_w, ada_b, norm_w, w_in, w_out, lambda_qk, subln_w, rpb,
              conv_w, conv_b, rg_wa, rg_ba, rg_wx, rg_bx, rg_lambda, final_norm_w):
    for l in range(DEPTH):
        x, ctx = hybrid_layer(l, x, ctx, c, c_ctx, ada_w[l], ada_b[l], norm_w[l], w_in[l], w_out[l],
                              lambda_qk[l], subln_w[l], rpb[l], conv_w[l], conv_b[l],
                              rg_wa[l], rg_ba[l], rg_wx[l], rg_bx[l], rg_lambda[l],
                              need_ctx=(l < DEPTH - 1))
    return rms_norm(x, final_norm_w)
```

```python
import math
from contextlib import ExitStack

import numpy as np
import ml_dtypes
import concourse.bass as bass
import concourse.mybir as mybir
from concourse.bass_utils import run_bass_kernel_spmd

F32 = mybir.dt.float32
BF16 = mybir.dt.bfloat16
ALU = mybir.AluOpType
AF = mybir.ActivationFunctionType
AX = mybir.AxisListType

D = 4096
SEQ = 2048
NCTX = 256
T = SEQ + NCTX
KC = D // 128
INW = 14336
DEPTH = 2
NG_IN = INW // 128
TOKCH = [(0, 256), (256, 512), (768, 512), (1280, 512), (1792, 512)]
LAT_TOKCH = TOKCH[1:]
NORM_EPS = 1e-6
SUBLN_EPS = 1e-5

PP_CC = 0
PP_ADAB = PP_CC + 64
PP_NORMW = PP_ADAB + 192
PP_FNW = PP_NORMW + 64
PP_LAMQ = PP_FNW + 32
PP_SUBLN = PP_LAMQ + 512
PP_CONVW = PP_SUBLN + 2
PP_CONVB = PP_CONVW + 64
PP_RGBA = PP_CONVB + 16
PP_RGBX = PP_RGBA + 32
PP_RGLAM = PP_RGBX + 32
PP_N = PP_RGLAM + 32


class Buf:
    __slots__ = ("t", "w", "r")

    def __init__(self, t):
        self.t = t
        self.w = None
        self.r = {}

    def __getitem__(self, k):
        return self.t[k]


class DSem:
    __slots__ = ("sem", "cnt")

    def __init__(self, sem):
        self.sem = sem
        self.cnt = 0


class KB:
    def __init__(self, nc, es):
        self.nc = nc
        self.es = es
        self.eng = {"pe": nc.tensor, "dve": nc.vector, "act": nc.scalar, "pool": nc.gpsimd, "sp": nc.sync}
        self.esem = {k: es.enter_context(nc.semaphore("e_" + k)) for k in ("pe", "dve", "act", "pool")}
        self.ecnt = {k: 0 for k in self.esem}
        self.waited = {k: {} for k in self.eng}
        self.dsems = []
        self.nsem = 0

    def dsem(self):
        s = DSem(self.es.enter_context(self.nc.semaphore("d%d" % self.nsem)))
        self.nsem += 1
        self.dsems.append(s)
        return s

    def sb(self, stack, name, shape, dt):
        self.nsb = getattr(self, "nsb", 0) + 1
        return Buf(stack.enter_context(self.nc.sbuf_tensor("s%d_%s" % (self.nsb, name), list(shape), dt)))

    def _deps(self, e, reads, writes):
        w = self.waited[e]
        need = {}
        own = self.esem.get(e)
        for b in reads:
            if b.w is not None:
                s, v = b.w
                if need.get(s, 0) < v:
                    need[s] = v
        for b in writes:
            if b.w is not None:
                s, v = b.w
                if need.get(s, 0) < v:
                    need[s] = v
            for s, v in b.r.items():
                if need.get(s, 0) < v:
                    need[s] = v
        for s, v in need.items():
            if e == "pe" and s is own:
                continue
            if w.get(s, 0) < v:
                self.eng[e].wait_ge(s, v)
                w[s] = v

    def _mark(self, tok, reads, writes):
        s, v = tok
        for b in reads:
            if b.r.get(s, 0) < v:
                b.r[s] = v
        for b in writes:
            b.w = tok
            b.r = {}

    def op(self, e, reads, writes, fn):
        self._deps(e, reads, writes)
        ins = fn(self.eng[e])
        self.ecnt[e] += 1
        tok = (self.esem[e], self.ecnt[e])
        ins.then_inc(tok[0], 1)
        self._mark(tok, reads, writes)
        return tok

    def dma(self, q, ds, out_ap, in_ap, reads, writes):
        self._deps(q, reads, writes)
        ins = self.eng[q].dma_start(out=out_ap, in_=in_ap)
        ds.cnt += 16
        ins.then_inc(ds.sem, 16)
        self._mark((ds.sem, ds.cnt), reads, writes)

    def dma_group(self, q, ds, items):
        for (o, i_, r, w) in items:
            self._deps(q, r, w)
        for (o, i_, r, w) in items:
            ins = self.eng[q].dma_start(out=o, in_=i_)
            ds.cnt += 16
            ins.then_inc(ds.sem, 16)
        tok = (ds.sem, ds.cnt)
        for (o, i_, r, w) in items:
            self._mark(tok, r, w)

    def barrier(self):
        for e in self.eng:
            w = self.waited[e]
            for k, s in self.esem.items():
                v = self.ecnt[k]
                if k == e and e == "pe":
                    continue
                if v > 0 and w.get(s, 0) < v:
                    self.eng[e].wait_ge(s, v)
                    w[s] = v
            for ds in self.dsems:
                if ds.cnt > 0 and w.get(ds.sem, 0) < ds.cnt:
                    self.eng[e].wait_ge(ds.sem, ds.cnt)
                    w[ds.sem] = ds.cnt


class Rot:
    def __init__(self, items):
        self.items = items
        self.i = 0

    def next(self):
        it = self.items[self.i % len(self.items)]
        self.i += 1
        return it


def build_program(debug=None, nlayers=DEPTH):
    debug = debug or set()
    nc = bass.Bass("TRN2", target_bir_lowering=False)

    def din(name, shape, dt=F32):
        return nc.dram_tensor(name, list(shape), dt, kind="ExternalInput").ap()

    def dscr(name, shape, dt=F32):
        kind = "ExternalOutput" if name in debug else "Internal"
        return nc.dram_tensor(name, list(shape), dt, kind=kind).ap()

    x_in = din("x", [SEQ, D])
    ctx_in = din("ctx", [NCTX, D])
    pp_in = din("pp", [128, PP_N])
    cf_in = din("cf", [128, 384])
    rope_in = din("rope", [128, 2, SEQ])
    ada_w = din("ada_w", [DEPTH, D, 3 * D])
    w_in = din("w_in", [DEPTH, D, INW])
    w_out = din("w_out", [DEPTH, D, D])
    rgw_in = din("rgw", [DEPTH, 8, 128, 4, 128])
    bias_in = din("biasb", [DEPTH, 12, 128, 25, 128])
    out_d = nc.dram_tensor("out", [SEQ, D], F32, kind="ExternalOutput").ap()

    xT = dscr("xT", [D, T])
    proj = dscr("proj", [13312, T], BF16)
    xcT = dscr("xcT", [1024, T])
    mrg = dscr("mrg", [D, T], BF16)
    xT_b, proj_b, xcT_b, mrg_b = Buf(xT), Buf(proj), Buf(xcT), Buf(mrg)
    xT_v = xT.rearrange("(c p) t -> p c t", p=128)
    mrg_v = mrg.rearrange("(c p) t -> p c t", p=128)

    es = ExitStack()
    with es:
        kb = KB(nc, es)
        op, dma = kb.op, kb.dma

        pp = kb.sb(es, "pp", [128, PP_N], F32)
        cf = kb.sb(es, "cf", [128, 384], F32)
        identb = kb.sb(es, "identb", [128, 128], BF16)
        rotTb = kb.sb(es, "rotTb", [128, 128], BF16)
        onesb = kb.sb(es, "onesb", [128, 128], BF16)
        modTs = [kb.sb(es, "modT%d" % i, [128, 96, 2], F32) for i in range(DEPTH)]
        Amods = [kb.sb(es, "Amod%d" % i, [128, KC, 2], F32) for i in range(DEPTH)]
        scb = kb.sb(es, "scb", [128, KC, 2], BF16)
        epsc = kb.sb(es, "epsc", [128, 4], F32)
        psb = [Buf(es.enter_context(nc.psum_tensor("ps%d" % i, [128, 512], F32))) for i in range(8)]
        PS = Rot(psb)
        d_misc = kb.dsem()
        NW = 3
        wsems = [kb.dsem() for _ in range(8)]
        dpool = [kb.dsem() for _ in range(12)]

        ident = cf.t[:, 0:128]
        onesf = cf.t[:, 256:384]
        kb.dma_group("sp", d_misc, [(pp.t[:], pp_in[:, :], [], [pp]), (cf.t[:], cf_in[:, :], [], [cf])])
        op("dve", [cf], [identb], lambda e: e.tensor_copy(out=identb.t[:], in_=cf.t[:, 0:128]))
        op("dve", [cf], [rotTb], lambda e: e.tensor_copy(out=rotTb.t[:], in_=cf.t[:, 128:256]))
        op("dve", [cf], [onesb], lambda e: e.tensor_copy(out=onesb.t[:], in_=cf.t[:, 256:384]))
        op("dve", [], [epsc], lambda e: e.memset(epsc.t[:, 0:1], NORM_EPS))
        op("dve", [epsc], [epsc], lambda e: e.memset(epsc.t[:, 1:2], SUBLN_EPS))
        op("dve", [epsc], [epsc], lambda e: e.memset(epsc.t[:, 2:3], 1.0))
        op("dve", [epsc], [epsc], lambda e: e.memset(epsc.t[:, 3:4], 0.0))

        def rstd_from(bk, n, dst, inv_n, eps_col):
            op("act", [bk, epsc], [dst], lambda e: e.activation(out=dst.t[:, 0:n], in_=bk.t[:, 0:n], func=AF.Ln,
                                                               scale=inv_n, bias=epsc.t[:, eps_col:eps_col + 1]))
            op("act", [dst], [dst], lambda e: e.activation(out=dst.t[:, 0:n], in_=dst.t[:, 0:n], func=AF.Exp, scale=-0.5))

        def mk_wpool(stack, n=NW):
            slots = [kb.sb(stack, "w%d" % i, [128, KC, 128], BF16) for i in range(n)]
            return {"slots": slots, "i": 0}

        def load_w(wp, wd, g):
            i = wp["i"] % len(wp["slots"])
            wp["i"] += 1
            src = wd.rearrange("(c p) n -> p c n", p=128)
            sl = wp["slots"][i]
            for q in range(4):
                dma("pool", wsems[i], sl.t[:, q * 8:(q + 1) * 8, :], src[:, q * 8:(q + 1) * 8, g * 128:(g + 1) * 128], [], [sl])
            return sl

        def gen_items(wp, items):
            pend = []
            nxt = 0
            n = len(items)
            prefetch = len(wp["slots"])
            for idx in range(n):
                while nxt < n and nxt <= idx + prefetch - 1:
                    pend.append(load_w(wp, items[nxt]["wd"], items[nxt]["g"]))
                    nxt += 1
                it = items[idx]
                if it.get("pre") is not None:
                    it["pre"](it["g"])
                ws = pend.pop(0)
                banks = []
                for (t0, tn) in it["tokch"]:
                    bk = PS.next()
                    for k in range(KC):
                        rb = it["rhs_bufs"](k) if callable(it["rhs_bufs"]) else it["rhs_bufs"]
                        if it.get("swap"):
                            op("pe", [ws] + rb, [bk], lambda e, k=k, bk=bk, ws=ws, it=it: e.matmul(
                                bk.t[0:2, 0:128], lhsT=it["rhs_fn"](k, 0, 2), rhs=ws.t[:, k, :], start=(k == 0), stop=(k == KC - 1)))
                            continue
                        op("pe", [ws] + rb, [bk], lambda e, k=k, bk=bk, t0=t0, tn=tn, ws=ws, it=it: e.matmul(
                            bk.t[:, 0:tn], lhsT=ws.t[:, k, :], rhs=it["rhs_fn"](k, t0, tn), start=(k == 0), stop=(k == KC - 1)))
                    banks.append((bk, t0, tn))
                it["evac"](it["g"], banks)
                yield

        def linear(wp, wd, ngroups, rhs_fn, rhs_bufs, tokch, evac, pre=None):
            for _ in gen_items(wp, [dict(wd=wd, g=g, rhs_fn=rhs_fn, rhs_bufs=rhs_bufs, tokch=tokch, evac=evac, pre=pre) for g in range(ngroups)]):
                pass

        def ada_items(l):
            def evac_mod(g, banks, l=l):
                bk, _, _ = banks[0]
                op("dve", [bk, pp], [modTs[l]], lambda e: e.tensor_scalar(
                    out=modTs[l].t[:, g, :], in0=bk.t[:, 0:2], scalar1=pp.t[:, PP_ADAB + l * 96 + g:PP_ADAB + l * 96 + g + 1],
                    scalar2=None, op0=ALU.add))
            return [dict(wd=ada_w[l], g=g, rhs_fn=lambda k, t0, tn: scb.t[:, k, 0:2], rhs_bufs=[scb], tokch=[(0, 2)], evac=evac_mod, pre=None)
                    for g in range(96)]

        def gen_ada_wide(l, stack, nslots, ngw=24):
            slots = [kb.sb(stack, "ww%d" % i, [128, KC, 512], BF16) for i in range(nslots)]
            src = ada_w[l].rearrange("(c p) n -> p c n", p=128)
            evac_mod = ada_items(l)[0]["evac"]

            def load(gw):
                sl = slots[gw % nslots]
                for q in range(4):
                    dma("pool", wsems[gw % nslots], sl.t[:, q * 8:(q + 1) * 8, :], src[:, q * 8:(q + 1) * 8, gw * 512:(gw + 1) * 512], [], [sl])
                return sl
            pend = []
            nxt = 0
            for gw in range(ngw):
                while nxt < ngw and nxt <= gw + nslots - 1:
                    pend.append(load(nxt))
                    nxt += 1
                ws = pend.pop(0)
                for j in range(4):
                    bk = PS.next()
                    for k in range(KC):
                        op("pe", [ws, scb], [bk], lambda e, k=k, bk=bk, j=j, ws=ws: e.matmul(
                            bk.t[:, 0:2], lhsT=ws.t[:, k, j * 128:(j + 1) * 128], rhs=scb.t[:, k, 0:2], start=(k == 0), stop=(k == KC - 1)))
                    evac_mod(gw * 4 + j, [(bk, 0, 2)])
                    yield

        adarow = Rot([kb.sb(es, "adarow%d" % i, [2, 128], F32) for i in range(2)])

        def ada_items_swapped(l):
            def evac_sw(g, banks, l=l):
                bk, _, _ = banks[0]
                row = adarow.next()
                op("dve", [bk], [row], lambda e: e.tensor_copy(out=row.t[0:2, :], in_=bk.t[0:2, 0:128]))
                bk2 = PS.next()
                op("pe", [row, cf], [bk2], lambda e: e.transpose(out=bk2.t[:, 0:2], in_=row.t[0:2, :], identity=cf.t[0:2, 0:2]))
                op("dve", [bk2, pp], [modTs[l]], lambda e: e.tensor_scalar(
                    out=modTs[l].t[:, g, :], in0=bk2.t[:, 0:2], scalar1=pp.t[:, PP_ADAB + l * 96 + g:PP_ADAB + l * 96 + g + 1],
                    scalar2=None, op0=ALU.add))
            return [dict(wd=ada_w[l], g=g, rhs_fn=lambda k, t0, tn: scb.t[:, k, 0:2], rhs_bufs=[scb], tokch=[(0, 2)], evac=evac_sw, pre=None, swap=True)
                    for g in range(96)]

        def ada_finish(l):
            op("dve", [modTs[l], pp], [Amods[l]], lambda e: e.scalar_tensor_tensor(
                out=Amods[l].t[:], in0=modTs[l].t[:, 32:64, :], scalar=1.0,
                in1=pp.t[:, PP_NORMW + l * 32:PP_NORMW + (l + 1) * 32].unsqueeze(2).to_broadcast([128, KC, 2]),
                op0=ALU.add, op1=ALU.mult))

        op("act", [pp], [scb], lambda e: e.activation(out=scb.t[:], in_=pp.t[:, PP_CC:PP_CC + 64].rearrange("p (c t) -> p c t", t=2), func=AF.Silu))

        with ExitStack() as ph:
            xin = [kb.sb(ph, "xin%d" % i, [128, D], F32) for i in range(2)]
            xst = [kb.sb(ph, "xst%d" % i, [128, KC, 256], F32) for i in range(2)]
            ada_gen = gen_ada_wide(0, ph, 3, 24)
            for tt in range(T // 128):
                i = tt % 2
                src = ctx_in[tt * 128:(tt + 1) * 128, :] if tt < 2 else x_in[(tt - 2) * 128:(tt - 1) * 128, :]
                dma("sp", dpool[i], xin[i].t[:], src, [], [xin[i]])
                for c4 in range(KC // 4):
                    bk = PS.next()
                    for j in range(4):
                        c = c4 * 4 + j
                        op("pe", [xin[i], cf], [bk], lambda e, c=c, j=j, bk=bk, i=i: e.transpose(
                            out=bk.t[:, j * 128:(j + 1) * 128], in_=xin[i].t[:, c * 128:(c + 1) * 128], identity=ident))
                    xs = xst[(tt // 2) % 2]
                    o_ = (tt % 2) * 128
                    dst = xs.t[:, c4 * 4:(c4 + 1) * 4, o_:o_ + 128]
                    srcp = bk.t[:, :].rearrange("p (c t) -> p c t", c=4)
                    if c4 % 2 == 0:
                        op("act", [bk], [xs], lambda e, dst=dst, srcp=srcp: e.copy(out=dst, in_=srcp))
                    else:
                        op("dve", [bk], [xs], lambda e, dst=dst, srcp=srcp: e.tensor_copy(out=dst, in_=srcp))
                if tt % 2 == 1:
                    xs = xst[(tt // 2) % 2]
                    dma("sp", dpool[2 + (tt // 2) % 2], xT_v[:, :, (tt - 1) * 128:(tt + 1) * 128], xs.t[:], [xs], [xT_b])
                for _ in range(6):
                    next(ada_gen, None)
            for _ in ada_gen:
                pass
            ada_finish(0)
            kb.barrier()

        for l in range(nlayers):
            last = (l == DEPTH - 1)
            need_ctx = not last
            lam_init = 0.8 - 0.6 * math.exp(-0.3 * l)
            modT, Amod = modTs[l], Amods[l]
            if "modT" in debug:
                dbg = nc.dram_tensor("dbg_modT", [128, 192], F32, kind="ExternalOutput").ap()
                dma("sp", d_misc, dbg[:, :], modT.t[:].rearrange("p a b -> p (a b)"), [modT], [])
            if "stop_mod" in debug:
                break

            with ExitStack() as phx:
                XT = kb.sb(phx, "XT", [128, KC, T], BF16)
                with ExitStack() as ph:
                    xch = [kb.sb(ph, "xch%d" % i, [128, KC, 128], F32) for i in range(2)]
                    sq = [kb.sb(ph, "sq%d" % i, [128, 8, 128], F32) for i in range(2)]
                    rstd = [kb.sb(ph, "rstd%d" % i, [128, 128], F32) for i in range(2)]
                    tmp8 = [kb.sb(ph, "tmp8%d" % i, [128, 8, 128], F32) for i in range(2)]
                    SQ, TM = Rot(sq), Rot(tmp8)
                    bks = {}

                    def n_dma(tt):
                        i = tt % 2
                        dma("sp", dpool[i], xch[i].t[:], xT_v[:, :, tt * 128:(tt + 1) * 128], [xT_b], [xch[i]])

                    def n_stageA(tt):
                        i = tt % 2
                        bk = PS.next()
                        bks[tt] = bk
                        for c8 in range(4):
                            s = SQ.next()
                            op("dve", [xch[i]], [s], lambda e, s=s, c8=c8, i=i: e.tensor_tensor(
                                out=s.t[:], in0=xch[i].t[:, c8 * 8:(c8 + 1) * 8, :], in1=xch[i].t[:, c8 * 8:(c8 + 1) * 8, :], op=ALU.mult))
                            for j in range(8):
                                op("pe", [s, cf], [bk], lambda e, s=s, j=j, bk=bk, c8=c8: e.matmul(
                                    bk.t[:, 0:128], lhsT=onesf, rhs=s.t[:, j, :], start=(c8 == 0 and j == 0), stop=(c8 == 3 and j == 7)))

                    def n_stageC(tt):
                        rstd_from(bks[tt], 128, rstd[tt % 2], 1.0 / D, 0)

                    def n_stageB(tt):
                        i = tt % 2
                        m = 1 if tt < 2 else 0
                        r = rstd[i]
                        for c8 in range(4):
                            tm = TM.next()
                            op("dve", [xch[i], r], [tm], lambda e, r=r, tm=tm, i=i, c8=c8: e.tensor_tensor(
                                out=tm.t[:], in0=xch[i].t[:, c8 * 8:(c8 + 1) * 8, :], in1=r.t[:].unsqueeze(1).to_broadcast([128, 8, 128]), op=ALU.mult))
                            for j in range(8):
                                c = c8 * 8 + j
                                op("act", [tm, Amod, modT], [XT], lambda e, c=c, j=j, tm=tm, m=m, tt=tt: e.activation(
                                    out=XT.t[:, c, tt * 128:(tt + 1) * 128], in_=tm.t[:, j, :], func=AF.Identity,
                                    scale=Amod.t[:, c, m:m + 1], bias=modT.t[:, c, m:m + 1]))

                    NT_ = T // 128
                    n_dma(0)
                    n_stageA(0)
                    n_stageC(0)
                    for tt in range(NT_):
                        if tt + 1 < NT_:
                            n_dma(tt + 1)
                        n_stageB(tt)
                        if tt + 1 < NT_:
                            n_stageA(tt + 1)
                            n_stageC(tt + 1)
                    kb.barrier()
                if "XT" in debug:
                    dbg_xt = nc.dram_tensor("dbg_XT", [128, KC, T], BF16, kind="ExternalOutput").ap()
                    dma("sp", d_misc, dbg_xt[:, :, :], XT.t[:], [XT], [])
                if "stop_norm" in debug:
                    kb.barrier()
                    break

                with ExitStack() as ph:
                    wp = mk_wpool(ph)
                    stg = [kb.sb(ph, "stg%d" % i, [128, T], BF16) for i in range(2)]
                    stgf = [kb.sb(ph, "stgf%d" % i, [128, T], F32) for i in range(1)]
                    st = {"i": 0}

                    def evac_in(g, banks):
                        fam = g // 12 if g < 96 else 8 + (g - 96) // 8
                        if fam == 8:
                            for (bk, t0, tn) in banks:
                                op("dve", [bk], [stgf[0]], lambda e, bk=bk, t0=t0, tn=tn: e.tensor_copy(out=stgf[0].t[:, t0:t0 + tn], in_=bk.t[:, 0:tn]))
                            r0 = (g - 96) * 128
                            dma("sp", dpool[2], xcT[r0:r0 + 128, :], stgf[0].t[:], [stgf[0]], [xcT_b])
                            return
                        i = st["i"] % 2
                        st["i"] += 1
                        gate = fam in (3, 7, 9)
                        for (bk, t0, tn) in banks:
                            if gate:
                                op("act", [bk], [stg[i]], lambda e, bk=bk, t0=t0, tn=tn, i=i: e.activation(out=stg[i].t[:, t0:t0 + tn], in_=bk.t[:, 0:tn], func=AF.Silu))
                            else:
                                op("dve", [bk], [stg[i]], lambda e, bk=bk, t0=t0, tn=tn, i=i: e.tensor_copy(out=stg[i].t[:, t0:t0 + tn], in_=bk.t[:, 0:tn]))
                        r0 = g * 128 if g < 96 else (g - 8) * 128
                        dma("sp", dpool[i], proj[r0:r0 + 128, :], stg[i].t[:], [stg[i]], [proj_b])

                    def fam_of(g):
                        return g // 12 if g < 96 else 8 + (g - 96) // 8
                    in_items = [dict(wd=w_in[l], g=g, rhs_fn=lambda k, t0, tn: XT.t[:, k, t0:t0 + tn], rhs_bufs=[XT],
                                     tokch=(LAT_TOKCH if (last and fam_of(g) in (0, 3, 4, 7, 9)) else TOKCH), evac=evac_in, pre=None)
                                for g in range(NG_IN)]
                    ad = []
                    if l + 1 < nlayers:
                        ad += ada_items_swapped(l + 1)[0:64]
                    if l > 0:
                        ad += ada_items_swapped(l)[64:96]
                    items = []
                    for idx in range(NG_IN):
                        items.append(in_items[idx])
                        if idx < len(ad):
                            items.append(ad[idx])
                    items += ad[NG_IN:]
                    for _ in gen_items(wp, items):
                        pass
                    if l + 1 < nlayers:
                        ada_finish(l + 1)
                    kb.barrier()
            if "stop_proj" in debug:
                break

            with ExitStack() as ph:
                rope = kb.sb(ph, "rope", [128, 2, SEQ], F32)
                dma("sp", d_misc, rope.t[:], rope_in[:, :, :], [], [rope])
                lt = kb.sb(ph, "lamtmp", [128, 136], F32)
                nlam = kb.sb(ph, "nlam", [128, 1], F32)
                sw = kb.sb(ph, "sw", [128, 1], F32)
                lq = pp.t[:, PP_LAMQ + l * 256:PP_LAMQ + (l + 1) * 256]
                op("dve", [pp], [lt], lambda e: e.tensor_tensor(out=lt.t[:, 0:64], in0=lq[:, 0:64], in1=lq[:, 64:128], op=ALU.mult))
                op("dve", [pp, lt], [lt], lambda e: e.tensor_tensor(out=lt.t[:, 64:128], in0=lq[:, 128:192], in1=lq[:, 192:256], op=ALU.mult))
                op("dve", [lt], [lt], lambda e: e.reduce_sum(out=lt.t[:, 128:129], in_=lt.t[:, 0:64], axis=AX.X))
                op("dve", [lt], [lt], lambda e: e.reduce_sum(out=lt.t[:, 129:130], in_=lt.t[:, 64:128], axis=AX.X))
                op("act", [lt], [lt], lambda e: e.activation(out=lt.t[:, 130:132], in_=lt.t[:, 128:130], func=AF.Exp))
                op("dve", [lt], [lt], lambda e: e.tensor_tensor(out=lt.t[:, 132:133], in0=lt.t[:, 131:132], in1=lt.t[:, 130:131], op=ALU.subtract))
                op("dve", [lt], [nlam], lambda e: e.tensor_scalar(out=nlam.t[:], in0=lt.t[:, 132:133], scalar1=-lam_init, scalar2=None, op0=ALU.add))
                op("dve", [pp], [sw], lambda e: e.tensor_scalar(out=sw.t[:], in0=pp.t[:, PP_SUBLN + l:PP_SUBLN + l + 1], scalar1=1.0 - lam_init, scalar2=None, op0=ALU.mult))

                ld = [[kb.sb(ph, "ald%d_%d" % (i, j), [128, T], BF16) for j in range(4)] for i in range(2)]
                qrs = [kb.sb(ph, "qr%d" % i, [128, T], BF16) for i in range(2)]
                krs = [kb.sb(ph, "kr%d" % i, [128, T], BF16) for i in range(2)]
                vtoks = [kb.sb(ph, "vtok%d" % i, [128, 18, 128], BF16) for i in range(2)]
                mst = [kb.sb(ph, "amst%d" % i, [128, T], BF16) for i in range(2)]
                pts = [kb.sb(ph, "pt%d" % i, [128, 2, 2, 256], BF16) for i in range(3)]
                f = [kb.sb(ph, "af%d" % i, [128, 512], F32) for i in range(8)]
                fpost = [[kb.sb(ph, "afp%d_%d" % (i, j), [128, 256], F32) for j in range(2)] for i in range(2)]
                SC = Rot([(psb[4], psb[5]), (psb[6], psb[7])])
                OB = Rot([psb[0], psb[1]])
                ZB = psb[2]
                AUX = psb[3]
                PTS = Rot(pts)
                ZS = Rot([kb.sb(ph, "azs%d" % i, [128, 512], BF16) for i in range(3)])
                qscale = 64 ** -0.5

                def a_load(h):
                    i = h % 2
                    kb.dma_group("sp", dpool[i], [(ld[i][j].t[:], proj[(j * 12 + h) * 128:(j * 12 + h + 1) * 128, :], [proj_b], [ld[i][j]]) for j in range(4)])

                def gen_prologue(h):
                    qT, kT, vT, gT = ld[h % 2]
                    qr, kr, vtok = qrs[h % 2], krs[h % 2], vtoks[h % 2]
                    for (src, dst) in ((qT, qr), (kT, kr)):
                        op("dve", [src], [dst], lambda e, src=src, dst=dst: e.tensor_copy(out=dst.t[:, 0:256], in_=src.t[:, 0:256]))
                        for ci, (t0, tn) in enumerate(LAT_TOKCH):
                            bk = AUX
                            op("pe", [src, rotTb], [bk], lambda e, bk=bk, src=src, t0=t0: e.matmul(
                                bk.t[:, 0:512], lhsT=rotTb.t[:], rhs=src.t[:, t0:t0 + 512], start=True, stop=True))
                            op("dve", [src, rope], [f[6]], lambda e, src=src, t0=t0: e.tensor_tensor(
                                out=f[6].t[:], in0=src.t[:, t0:t0 + 512], in1=rope.t[:, 0, t0 - 256:t0 + 256], op=ALU.mult))
                            op("dve", [bk, rope], [f[7]], lambda e, bk=bk, t0=t0: e.tensor_tensor(
                                out=f[7].t[:], in0=bk.t[:, 0:512], in1=rope.t[:, 1, t0 - 256:t0 + 256], op=ALU.mult))
                            op("dve", [f[6], f[7]], [dst], lambda e, dst=dst, t0=t0: e.tensor_tensor(
                                out=dst.t[:, t0:t0 + 512], in0=f[6].t[:], in1=f[7].t[:], op=ALU.add))
                            yield
                    for k0 in range(0, 18, 8):
                        nk = min(8, 18 - k0)
                        bk = AUX
                        bkb = bk.t[:, :].bitcast(BF16)
                        for j in range(nk):
                            op("pe", [vT, identb], [bk], lambda e, j=j, k0=k0, bkb=bkb: e.transpose(
                                out=bkb[:, j * 128:(j + 1) * 128], in_=vT.t[:, (k0 + j) * 128:(k0 + j + 1) * 128], identity=identb.t[:]))
                        op("dve", [bk], [vtok], lambda e, bkb=bkb, k0=k0, nk=nk, vtok=vtok: e.tensor_copy(
                            out=vtok.t[:, k0:k0 + nk, :], in_=bkb[:, 0:nk * 128].rearrange("p (k d) -> p k d", k=nk)))
                        yield

                a_load(0)
                for _ in gen_prologue(0):
                    pass
                pending = []
                qranges = [(256 + 256 * i, 18) for i in range(8)]
                if need_ctx:
                    qranges.append((0, 2))
                tasks = [(h, qi, q0, nkc, p) for h in range(12) for qi, (q0, nkc) in enumerate(qranges) for p in range(nkc // 2)]
                it_ob = {}
                fcount = [0]

                def scores(task):
                    h, qi, q0, nkc, p = task
                    qr, kr = qrs[h % 2], krs[h % 2]
                    bA, bB = SC.next()
                    for slot in range(2):
                        kc = 2 * p + slot
                        for n, bkd in enumerate((bA, bB)):
                            op("pe", [kr, qr], [bkd], lambda e, n=n, kc=kc, bkd=bkd, slot=slot: e.matmul(
                                bkd.t[:, slot * 256:(slot + 1) * 256], lhsT=kr.t[n * 64:(n + 1) * 64, kc * 128:(kc + 1) * 128],
                                rhs=qr.t[n * 64:(n + 1) * 64, q0:q0 + 256], start=True, stop=True))
                    return (bA, bB)

                def drain_pending():
                    while pending:
                        pp2 = pending.pop(0)
                        pp2(0)
                        pp2(1)

                def finish_iteration(task):
                    h, qi, q0, nkc, p = task
                    Ob = it_ob[(h, qi)]
                    gT = ld[h % 2][3]
                    ms = mst[h % 2]
                    fo, fq = fpost[fcount[0] % 2]
                    fcount[0] += 1
                    op("dve", [ZB], [f[4]], lambda e: e.tensor_copy(out=f[4].t[:], in_=ZB.t[:, 0:512]))
                    op("dve", [f[4]], [f[0]], lambda e: e.reciprocal(out=f[0].t[:], in_=f[4].t[:]))
                    op("dve", [Ob, f[0]], [f[2]], lambda e: e.tensor_tensor(out=f[2].t[:], in0=Ob.t[:, 0:512], in1=f[0].t[:], op=ALU.mult))
                    op("dve", [f[2], nlam], [fo], lambda e: e.scalar_tensor_tensor(
                        out=fo.t[:], in0=f[2].t[:, 256:512], scalar=nlam.t[:, 0:1], in1=f[2].t[:, 0:256], op0=ALU.mult, op1=ALU.add))
                    op("dve", [fo], [fq], lambda e: e.tensor_tensor(out=fq.t[:], in0=fo.t[:], in1=fo.t[:], op=ALU.mult))
                    st = {}

                    def post2(stage):
                        if stage == 0:
                            if "done0" in st:
                                return
                            st["done0"] = True
                            op("pe", [fq, cf], [AUX], lambda e: e.matmul(AUX.t[:, 0:256], lhsT=onesf, rhs=fq.t[:], start=True, stop=True))
                            return
                        rstd_from(AUX, 256, f[3], 1.0 / 128, 1)
                        op("dve", [fo, f[3]], [fo], lambda e: e.tensor_tensor(out=fo.t[:], in0=fo.t[:], in1=f[3].t[:, 0:256], op=ALU.mult))
                        op("dve", [fo, sw, gT], [ms], lambda e: e.scalar_tensor_tensor(
                            out=ms.t[:, q0:q0 + 256], in0=fo.t[:], scalar=sw.t[:, 0:1], in1=gT.t[:, q0:q0 + 256], op0=ALU.mult, op1=ALU.mult))
                        if qi == len(qranges) - 1:
                            c0 = 0 if need_ctx else 256
                            dma("sp", dpool[2 + h % 2], mrg[h * 128:(h + 1) * 128, c0:T], ms.t[:, c0:T], [ms], [mrg_b])
                    pending.append(post2)

                def emit_z(zp):
                    zs_, task_ = zp
                    h_, qi_, q0_, nkc_, p_ = task_
                    last = (p_ == nkc_ // 2 - 1)
                    op("pe", [onesb, zs_], [ZB], lambda e: e.matmul(ZB.t[:, 0:512], lhsT=onesb.t[:], rhs=zs_.t[:], start=(p_ == 0), stop=last))
                    if last:
                        finish_iteration(task_)

                pro = iter(())
                zprev = None
                cur = scores(tasks[0])
                for k, task in enumerate(tasks):
                    h, qi, q0, nkc, p = task
                    npair = nkc // 2
                    vtok = vtoks[h % 2]
                    if p == 0:
                        it_ob[(h, qi)] = OB.next()
                    Ob = it_ob[(h, qi)]
                    if k + 1 < len(tasks):
                        if tasks[k + 1][0] != h:
                            for _ in pro:
                                pass
                        nxt_s = scores(tasks[k + 1])
                    else:
                        nxt_s = None
                    pt = PTS.next()
                    for n in range(2):
                        op("act", [cur[n]], [pt], lambda e, n=n, cur=cur, pt=pt: e.activation(
                            out=pt.t[:, :, n, :], in_=cur[n].t[:, 0:512].rearrange("p (s q) -> p s q", s=2), func=AF.Exp, scale=qscale))
                    zs = ZS.next()
                    op("pool", [pt], [zs], lambda e, pt=pt, zs=zs: e.tensor_tensor(
                        out=zs.t[:], in0=pt.t[:, 0, :, :].rearrange("p n q -> p (n q)"),
                        in1=pt.t[:, 1, :, :].rearrange("p n q -> p (n q)"), op=ALU.add))
                    for slot in range(2):
                        kc = 2 * p + slot
                        rhs = pt.t[:, slot, :, :].rearrange("p n q -> p (n q)")
                        op("pe", [vtok, pt], [Ob], lambda e, kc=kc, rhs=rhs: e.matmul(
                            Ob.t[:, 0:512], lhsT=vtok.t[:, kc, :], rhs=rhs, start=(kc == 0), stop=(kc == nkc - 1)))
                    if zprev is not None:
                        emit_z(zprev)
                    zprev = (zs, task)
                    cur = nxt_s
                    if p == 0 and qi == 1 and h + 1 < 12:
                        a_load(h + 1)
                    if p == 0 and qi == 2:
                        pro = gen_prologue(h + 1) if h + 1 < 12 else iter(())
                    if npair >= 9:
                        if p in (5, 6):
                            next(pro, None)
                        if p == 7 and pending:
                            pending[0](0)
                        if p == 8:
                            drain_pending()
                emit_z(zprev)
                drain_pending()
                kb.barrier()
            if "stop_A" in debug:
                break

            with ExitStack() as ph:
                ld = [[kb.sb(ph, "bld%d_%d" % (i, j), [128, T], BF16) for j in range(4)] for i in range(2)]
                bias = [kb.sb(ph, "bbias%d" % i, [128, 25, 128], F32) for i in range(2)]
                vtok = kb.sb(ph, "bvtok", [128, 18, 128], BF16)
                mst = [kb.sb(ph, "bmst%d" % i, [128, T], BF16) for i in range(2)]
                pts = [kb.sb(ph, "bpt%d" % i, [128, 896], BF16) for i in range(3)]
                tb = [kb.sb(ph, "btb%d" % i, [128, 640], F32) for i in range(2)]
                f = [kb.sb(ph, "bf%d" % i, [128, 512], F32) for i in range(3)]
                SC = Rot([(psb[4], psb[5]), (psb[6], psb[7])])
                OZ = Rot([(psb[0], psb[1]), (psb[2], psb[3])])
                PTS, TB = Rot(pts), Rot(tb)
                bscale = 128 ** -0.5

                def b_load(h):
                    i = h % 2
                    kb.dma_group("sp", dpool[i], [(ld[i][j].t[:], proj[((4 + j) * 12 + h) * 128:((4 + j) * 12 + h + 1) * 128, :], [proj_b], [ld[i][j]]) for j in range(4)]
                                 + [(bias[i].t[:], bias_in[l, h], [], [bias[i]])])

                b_load(0)
                for h in range(12):
                    if h + 1 < 12:
                        b_load(h + 1)
                    qT, kT, vT, gT = ld[h % 2]
                    bi = bias[h % 2]
                    for k0 in range(0, 18, 8):
                        nk = min(8, 18 - k0)
                        bk = SC.next()[0]
                        bkb = bk.t[:, :].bitcast(BF16)
                        for j in range(nk):
                            op("pe", [vT, identb], [bk], lambda e, j=j, k0=k0, bkb=bkb: e.transpose(
                                out=bkb[:, j * 128:(j + 1) * 128], in_=vT.t[:, (k0 + j) * 128:(k0 + j + 1) * 128], identity=identb.t[:]))
                        op("act", [bk], [vtok], lambda e, bkb=bkb, k0=k0, nk=nk: e.copy(
                            out=vtok.t[:, k0:k0 + nk, :], in_=bkb[:, 0:nk * 128].rearrange("p (k d) -> p k d", k=nk)))
                    ms = mst[h % 2]
                    groups = [[(256 + 128 * (g4 * 4 + j), 2 * (g4 * 4 + j)) for j in range(4)] for g4 in range(4)]
                    if need_ctx:
                        groups.append([(0, None), (128, None)])
                    def scores(blk):
                        qt0, r = blk
                        s0, s1 = SC.next()
                        qa = qT.t[:, qt0:qt0 + 128]
                        if r is not None:
                            jb = min(max((r - 4) // 2, 0), 11)
                            for j in range(5):
                                kt0 = (2 + jb + j) * 128
                                bkd = s0 if j < 4 else s1
                                col = (j % 4) * 128
                                op("pe", [kT, qT], [bkd], lambda e, kt0=kt0, bkd=bkd, col=col: e.matmul(
                                    bkd.t[:, col:col + 128], lhsT=kT.t[:, kt0:kt0 + 128], rhs=qa, start=True, stop=True))
                        for j in range(2):
                            col = 128 + j * 128
                            op("pe", [kT, qT], [s1], lambda e, j=j, col=col: e.matmul(
                                s1.t[:, col:col + 128], lhsT=kT.t[:, j * 128:(j + 1) * 128], rhs=qa, start=True, stop=True))
                        return (s0, s1)

                    blocks = [(gi, bi_, blk) for gi, grp in enumerate(groups) for bi_, blk in enumerate(grp)]
                    pend_post = []
                    cur = scores(blocks[0][2])
                    for idx, (gi, bi_, blk) in enumerate(blocks):
                        grp = groups[gi]
                        if bi_ == 0:
                            Ob, Zb = OZ.next()
                        nxt_s = scores(blocks[idx + 1][2]) if idx + 1 < len(blocks) else None
                        qt0, r = blk
                        s0, s1 = cur
                        pt = PTS.next()
                        chunks = []
                        if r is not None:
                            jb = min(max((r - 4) // 2, 0), 11)
                            cls = {0: 0, 2: 1, 28: 2, 30: 3}.get(r, 4)
                            t_ = TB.next()
                            op("dve", [s0, bi], [t_], lambda e, t_=t_, s0=s0, cls=cls: e.scalar_tensor_tensor(
                                out=t_.t[:, 0:512], in0=s0.t[:, 0:512], scalar=bscale,
                                in1=bi.t[:, cls * 5:cls * 5 + 4, :].rearrange("p a b -> p (a b)"), op0=ALU.mult, op1=ALU.add))
                            op("dve", [s1, bi, t_], [t_], lambda e, t_=t_, s1=s1, cls=cls: e.scalar_tensor_tensor(
                                out=t_.t[:, 512:640], in0=s1.t[:, 0:128], scalar=bscale,
                                in1=bi.t[:, cls * 5 + 4, :], op0=ALU.mult, op1=ALU.add))
                            op("act", [t_], [pt], lambda e, t_=t_, pt=pt: e.activation(out=pt.t[:, 0:640], in_=t_.t[:, 0:640], func=AF.Exp))
                            chunks += [(2 + jb + j, j * 128) for j in range(5)]
                        op("act", [s1, pt], [pt], lambda e, s1=s1, pt=pt: e.activation(out=pt.t[:, 640:896], in_=s1.t[:, 128:384], func=AF.Exp, scale=bscale))
                        chunks += [(0, 640), (1, 768)]
                        while pend_post:
                            pend_post.pop(0)()
                        oc_ = bi_ * 128
                        for ci, (kc, col) in enumerate(chunks):
                            op("pe", [vtok, pt], [Ob], lambda e, kc=kc, col=col, ci=ci, pt=pt, oc_=oc_, Ob=Ob: e.matmul(
                                Ob.t[:, oc_:oc_ + 128], lhsT=vtok.t[:, kc, :], rhs=pt.t[:, col:col + 128], start=(ci == 0), stop=(ci == len(chunks) - 1)))
                        for ci, (kc, col) in enumerate(chunks):
                            op("pe", [onesb, pt], [Zb], lambda e, col=col, ci=ci, pt=pt, oc_=oc_, Zb=Zb: e.matmul(
                                Zb.t[:, oc_:oc_ + 128], lhsT=onesb.t[:], rhs=pt.t[:, col:col + 128], start=(ci == 0), stop=(ci == len(chunks) - 1)))
                        cur = nxt_s
                        if bi_ == len(grp) - 1:
                            def post(Ob=Ob, Zb=Zb, qn=128 * len(grp), q0=grp[0][0]):
                                op("act", [Zb], [f[0]], lambda e: e.activation(out=f[0].t[:, 0:qn], in_=Zb.t[:, 0:qn], func=AF.Ln))
                                op("act", [f[0]], [f[0]], lambda e: e.activation(out=f[0].t[:, 0:qn], in_=f[0].t[:, 0:qn], func=AF.Exp, scale=-1.0))
                                op("dve", [Ob, f[0]], [f[1]], lambda e: e.tensor_tensor(out=f[1].t[:, 0:qn], in0=Ob.t[:, 0:qn], in1=f[0].t[:, 0:qn], op=ALU.mult))
                                op("dve", [f[1], gT], [ms], lambda e: e.tensor_tensor(out=ms.t[:, q0:q0 + qn], in0=f[1].t[:, 0:qn], in1=gT.t[:, q0:q0 + qn], op=ALU.mult))
                            pend_post.append(post)
                    while pend_post:
                        pend_post.pop(0)()
                    c0 = 0 if need_ctx else 256
                    dma("sp", dpool[2 + h % 2], mrg[(12 + h) * 128:(13 + h) * 128, c0:T], ms.t[:, c0:T], [ms], [mrg_b])
                kb.barrier()
            if "stop_B" in debug:
                break

            with ExitStack() as ph:
                xcl = [kb.sb(ph, "cx%d" % i, [128, T], F32) for i in range(2)]
                gcl = [kb.sb(ph, "cg%d" % i, [128, T], BF16) for i in range(2)]
                rw = [kb.sb(ph, "crw%d" % i, [128, 4, 128], F32) for i in range(2)]
                u_l = [kb.sb(ph, "cu%d" % i, [128, T], F32) for i in range(2)]
                rr_l = [kb.sb(ph, "cr%d" % i, [128, T], F32) for i in range(2)]
                ii_l = [kb.sb(ph, "ci%d" % i, [128, T], F32) for i in range(2)]
                aa_l = [kb.sb(ph, "ca%d" % i, [128, T], F32) for i in range(2)]
                a2_l = [kb.sb(ph, "ca2%d" % i, [128, T], F32) for i in range(2)]
                bb_l = [kb.sb(ph, "cb%d" % i, [128, T], F32) for i in range(2)]
                hh_l = [[kb.sb(ph, "ch%d_%d" % (j, i), [128, T], F32) for i in range(2)] for j in range(2)]
                mst = [kb.sb(ph, "cmst%d" % i, [128, T], BF16) for i in range(2)]
                nsp = kb.sb(ph, "nsp", [128, 3, 16], F32)
                lo = PP_RGLAM + l * 16
                op("act", [pp], [nsp], lambda e: e.activation(out=nsp.t[:, 0, :], in_=pp.t[:, lo:lo + 16], func=AF.Exp, scale=-1.0))
                op("act", [nsp, epsc], [nsp], lambda e: e.activation(out=nsp.t[:, 0, :], in_=nsp.t[:, 0, :], func=AF.Ln, bias=epsc.t[:, 2:3]))
                op("dve", [nsp], [nsp], lambda e: e.tensor_scalar(out=nsp.t[:, 1, :], in0=nsp.t[:, 0, :], scalar1=-8.0, scalar2=None, op0=ALU.mult))
                op("dve", [nsp], [nsp], lambda e: e.tensor_scalar(out=nsp.t[:, 2, :], in0=nsp.t[:, 0, :], scalar1=-16.0, scalar2=None, op0=ALU.mult))

                def c_load(g):
                    i = g % 2
                    kb.dma_group("sp", dpool[i], [(xcl[i].t[:], xcT[g * 128:(g + 1) * 128, :], [xcT_b], [xcl[i]]),
                                                  (gcl[i].t[:], proj[(96 + g) * 128:(97 + g) * 128, :], [proj_b], [gcl[i]]),
                                                  (rw[i].t[:], rgw_in[l, g], [], [rw[i]])])

                def c_conv(g):
                    xc_ = xcl[g % 2]
                    u = u_l[g % 2]
                    cw = lambda j: pp.t[:, PP_CONVW + l * 32 + g * 4 + j:PP_CONVW + l * 32 + g * 4 + j + 1]
                    cb = pp.t[:, PP_CONVB + l * 8 + g:PP_CONVB + l * 8 + g + 1]
                    for (s0, sn) in ((0, 256), (256, 2048)):
                        op("dve", [xc_, pp], [u], lambda e, s0=s0, sn=sn: e.tensor_scalar(
                            out=u.t[:, s0:s0 + sn], in0=xc_.t[:, s0:s0 + sn], scalar1=cw(2), scalar2=cb, op0=ALU.mult, op1=ALU.add))
                        for (j, do, so, n) in ((0, 2, 0, sn - 2), (1, 1, 0, sn - 1), (3, 0, 1, sn - 1)):
                            op("dve", [xc_, pp, u], [u], lambda e, s0=s0, j=j, do=do, so=so, n=n: e.scalar_tensor_tensor(
                                out=u.t[:, s0 + do:s0 + do + n], in0=xc_.t[:, s0 + so:s0 + so + n], scalar=cw(j),
                                in1=u.t[:, s0 + do:s0 + do + n], op0=ALU.mult, op1=ALU.add))

                def c_dir(g, d):
                    rw_ = rw[g % 2]
                    u, rr, ii, aa, a2, bb, hh = u_l[g % 2], rr_l[g % 2], ii_l[g % 2], aa_l[g % 2], a2_l[g % 2], bb_l[g % 2], hh_l[g % 2]
                    pidx = (l * 2 + d) * 8 + g
                    ba = pp.t[:, PP_RGBA + pidx:PP_RGBA + pidx + 1]
                    bx = pp.t[:, PP_RGBX + pidx:PP_RGBX + pidx + 1]
                    for (t0, tn) in TOKCH:
                        bA, bX = PS.next(), PS.next()
                        op("pe", [rw_, u], [bA], lambda e, bA=bA, t0=t0, tn=tn: e.matmul(
                            bA.t[:, 0:tn], lhsT=rw_.t[:, d * 2, :], rhs=u.t[:, t0:t0 + tn], start=True, stop=True))
                        op("pe", [rw_, u], [bX], lambda e, bX=bX, t0=t0, tn=tn: e.matmul(
                            bX.t[:, 0:tn], lhsT=rw_.t[:, d * 2 + 1, :], rhs=u.t[:, t0:t0 + tn], start=True, stop=True))
                        op("act", [bA, pp], [rr], lambda e, bA=bA, t0=t0, tn=tn: e.activation(
                            out=rr.t[:, t0:t0 + tn], in_=bA.t[:, 0:tn], func=AF.Sigmoid, bias=ba))
                        op("act", [bX, pp], [ii], lambda e, bX=bX, t0=t0, tn=tn: e.activation(
                            out=ii.t[:, t0:t0 + tn], in_=bX.t[:, 0:tn], func=AF.Sigmoid, bias=bx))
                    op("act", [rr, nsp], [aa], lambda e: e.activation(out=aa.t[:], in_=rr.t[:], func=AF.Exp, scale=nsp.t[:, 1, d * 8 + g:d * 8 + g + 1]))
                    op("act", [rr, nsp], [a2], lambda e: e.activation(out=a2.t[:], in_=rr.t[:], func=AF.Exp, scale=nsp.t[:, 2, d * 8 + g:d * 8 + g + 1]))
                    op("act", [a2, epsc], [a2], lambda e: e.activation(out=a2.t[:], in_=a2.t[:], func=AF.Sqrt, scale=-1.0, bias=epsc.t[:, 2:3]))
                    op("dve", [ii, u], [bb], lambda e: e.tensor_tensor(out=bb.t[:], in0=ii.t[:], in1=u.t[:], op=ALU.mult))
                    op("dve", [bb, a2], [bb], lambda e: e.tensor_tensor(out=bb.t[:], in0=bb.t[:], in1=a2.t[:], op=ALU.mult))
                    h_ = hh[d]
                    if d == 0:
                        op("dve", [aa, bb], [h_], lambda e: e.tensor_tensor_scan(
                            out=h_.t[:, 0:256], data0=aa.t[:, 0:256], data1=bb.t[:, 0:256], initial=0.0, op0=ALU.mult, op1=ALU.add))
                        op("dve", [aa, bb, h_], [h_], lambda e: e.tensor_tensor_scan(
                            out=h_.t[:, 256:T], data0=aa.t[:, 256:T], data1=bb.t[:, 256:T], initial=h_.t[:, 255:256], op0=ALU.mult, op1=ALU.add))
                    else:
                        op("dve", [aa, bb], [h_], lambda e: e.tensor_tensor_scan(
                            out=h_.t[:, 0:256][:, ::-1], data0=aa.t[:, 0:256][:, ::-1], data1=bb.t[:, 0:256][:, ::-1], initial=0.0, op0=ALU.mult, op1=ALU.add))
                        op("dve", [aa, bb, h_], [h_], lambda e: e.tensor_tensor_scan(
                            out=h_.t[:, 256:T][:, ::-1], data0=aa.t[:, 256:T][:, ::-1], data1=bb.t[:, 256:T][:, ::-1], initial=h_.t[:, 0:1], op0=ALU.mult, op1=ALU.add))

                def c_final(g):
                    hh = hh_l[g % 2]
                    gc_ = gcl[g % 2]
                    ms = mst[g % 2]
                    op("dve", [hh[0], hh[1]], [hh[0]], lambda e: e.tensor_tensor(out=hh[0].t[:], in0=hh[0].t[:], in1=hh[1].t[:], op=ALU.add))
                    op("dve", [hh[0], gc_], [ms], lambda e: e.tensor_tensor(out=ms.t[:], in0=hh[0].t[:], in1=gc_.t[:], op=ALU.mult))
                    dma("sp", dpool[2 + g % 2], mrg[(24 + g) * 128:(25 + g) * 128, :], ms.t[:], [ms], [mrg_b])

                c_load(0)
                c_load(1)
                c_conv(0)
                for g in range(8):
                    c_dir(g, 0)
                    if g + 1 < 8:
                        c_conv(g + 1)
                    c_dir(g, 1)
                    c_final(g)
                    if g + 2 < 8:
                        c_load(g + 2)
                kb.barrier()
            if "stop_C" in debug:
                break

            with ExitStack() as ph:
                XTs = [kb.sb(ph, "MT%d" % q, [128, 4, T], BF16) for q in range(8)]
                wp = mk_wpool(ph)
                xr = [kb.sb(ph, "xr%d" % i, [128, T], F32) for i in range(2)]
                for q in range(8):
                    dma("sp", dpool[4 + q], XTs[q].t[:], mrg_v[:, q * 4:(q + 1) * 4, :], [mrg_b], [XTs[q]])
                tokch = TOKCH if need_ctx else LAT_TOKCH

                def pre_out(g):
                    i = g % 2
                    dma("sp", dpool[i], xr[i].t[:], xT[g * 128:(g + 1) * 128, :], [xT_b], [xr[i]])

                def evac_out(g, banks):
                    i = g % 2
                    for (bk, t0, tn) in banks:
                        m = 1 if t0 < 256 else 0
                        op("dve", [bk, modT, xr[i]], [xr[i]], lambda e, bk=bk, t0=t0, tn=tn, m=m, i=i: e.scalar_tensor_tensor(
                            out=xr[i].t[:, t0:t0 + tn], in0=bk.t[:, 0:tn], scalar=modT.t[:, 64 + g, m:m + 1],
                            in1=xr[i].t[:, t0:t0 + tn], op0=ALU.mult, op1=ALU.add))
                    dma("sp", dpool[2 + i], xT[g * 128:(g + 1) * 128, :], xr[i].t[:], [xr[i]], [xT_b])

                linear(wp, w_out[l], KC, lambda k, t0, tn: XTs[k // 4].t[:, k % 4, t0:t0 + tn], lambda k: [XTs[k // 4]], tokch, evac_out, pre=pre_out)
                kb.barrier()

        if nlayers == DEPTH and not any(k.startswith("stop") for k in debug):
            with ExitStack() as ph:
                xch = [kb.sb(ph, "fx%d" % i, [128, KC, 256], F32) for i in range(2)]
                sq = [kb.sb(ph, "fsq%d" % i, [128, 8, 128], F32) for i in range(2)]
                rstd = [kb.sb(ph, "frs%d" % i, [128, 128], F32) for i in range(2)]
                yn = [kb.sb(ph, "fy%d" % i, [128, 4, 128], F32) for i in range(2)]
                ost = [kb.sb(ph, "fo%d" % i, [128, D], F32) for i in range(2)]
                SQ, YN = Rot(sq), Rot(yn)
                bks = {}

                def f_dma(tt):
                    if tt % 2 != 0:
                        return
                    i = (tt // 2) % 2
                    t0 = 256 + tt * 128
                    dma("sp", dpool[i], xch[i].t[:], xT_v[:, :, t0:t0 + 256], [xT_b], [xch[i]])

                def f_stageA(tt):
                    i = tt % 2
                    xb = xch[(tt // 2) % 2]
                    o_ = (tt % 2) * 128
                    bk = PS.next()
                    bks[tt] = bk
                    for c8 in range(4):
                        s = SQ.next()
                        op("dve", [xb], [s], lambda e, s=s, c8=c8: e.tensor_tensor(
                            out=s.t[:], in0=xb.t[:, c8 * 8:(c8 + 1) * 8, o_:o_ + 128], in1=xb.t[:, c8 * 8:(c8 + 1) * 8, o_:o_ + 128], op=ALU.mult))
                        for j in range(8):
                            op("pe", [s, cf], [bk], lambda e, s=s, j=j, bk=bk, c8=c8: e.matmul(
                                bk.t[:, 0:128], lhsT=onesf, rhs=s.t[:, j, :], start=(c8 == 0 and j == 0), stop=(c8 == 3 and j == 7)))

                def f_stageC(tt):
                    rstd_from(bks[tt], 128, rstd[tt % 2], 1.0 / D, 0)

                def f_stageB(tt):
                    i = tt % 2
                    xb = xch[(tt // 2) % 2]
                    o_ = (tt % 2) * 128
                    r = rstd[i]
                    for c4 in range(8):
                        y = YN.next()
                        for j in range(4):
                            c = c4 * 4 + j
                            op("dve", [xb, r, pp], [y], lambda e, c=c, j=j, y=y, r=r: e.scalar_tensor_tensor(
                                out=y.t[:, j, :], in0=xb.t[:, c, o_:o_ + 128], scalar=pp.t[:, PP_FNW + c:PP_FNW + c + 1], in1=r.t[:], op0=ALU.mult, op1=ALU.mult))
                        bk2 = PS.next()
                        for j in range(4):
                            op("pe", [y, cf], [bk2], lambda e, j=j, y=y, bk2=bk2: e.transpose(
                                out=bk2.t[:, j * 128:(j + 1) * 128], in_=y.t[:, j, :], identity=ident))
                        op("act", [bk2], [ost[i]], lambda e, bk2=bk2, c4=c4, i=i: e.copy(out=ost[i].t[:, c4 * 512:(c4 + 1) * 512], in_=bk2.t[:, 0:512]))
                    dma("sp", dpool[2 + i], out_d[tt * 128:(tt + 1) * 128, :], ost[i].t[:], [ost[i]], [])

                NT_ = SEQ // 128
                f_dma(0)
                f_stageA(0)
                f_stageC(0)
                for tt in range(NT_):
                    if tt + 1 < NT_:
                        f_dma(tt + 1)
                    f_stageB(tt)
                    if tt + 1 < NT_:
                        f_stageA(tt + 1)
                        f_stageC(tt + 1)
                kb.barrier()

        kb.barrier()
    return nc


def _fm(v, n):
    return np.ascontiguousarray(v.reshape(n, 128).T)


def host_consts():
    cf = np.zeros((128, 384), np.float32)
    cf[:, 0:128] = np.eye(128, dtype=np.float32)
    R = np.zeros((128, 128), np.float32)
    for base in range(0, 128, 32):
        for i in range(16):
            R[base + i, base + i + 16] = -1.0
            R[base + i + 16, base + i] = 1.0
    cf[:, 128:256] = R.T
    cf[:, 256:384] = 1.0
    t = np.arange(SEQ)
    rows = (t // 64).astype(np.float32)
    cols = (t % 64).astype(np.float32)
    freqs = (10000.0 ** (-np.arange(16, dtype=np.float32) / 16)).astype(np.float32)
    rope = np.zeros((128, 2, SEQ), np.float32)
    for p in range(128):
        d = p % 64
        part = d // 32
        i = (d % 32) % 16
        pos = rows if part == 0 else cols
        ang = (pos * freqs[i]).astype(np.float32)
        rope[p, 0] = np.cos(ang)
        rope[p, 1] = np.sin(ang)
    return cf, rope


def host_pp(inputs, b):
    pp = np.zeros((128, PP_N), np.float32)
    cc = np.stack([_fm(inputs["c"][b], 32), _fm(inputs["c_ctx"], 32)], axis=-1)
    pp[:, PP_CC:PP_CC + 64] = cc.reshape(128, 64)
    for l in range(DEPTH):
        pp[:, PP_ADAB + l * 96:PP_ADAB + (l + 1) * 96] = _fm(inputs["ada_b"][l], 96)
        pp[:, PP_NORMW + l * 32:PP_NORMW + (l + 1) * 32] = _fm(inputs["norm_w"][l], 32)
        pp[:, PP_LAMQ + l * 256:PP_LAMQ + (l + 1) * 256] = inputs["lambda_qk"][l].reshape(1, 256)
        pp[:, PP_SUBLN + l] = inputs["subln_w"][l]
        for j in range(4):
            pp[:, PP_CONVW + l * 32 + j:PP_CONVW + (l + 1) * 32:4] = _fm(inputs["conv_w"][l, j], 8)
        pp[:, PP_CONVB + l * 8:PP_CONVB + (l + 1) * 8] = _fm(inputs["conv_b"][l], 8)
        for d in range(2):
            o = (l * 2 + d) * 8
            pp[:, PP_RGBA + o:PP_RGBA + o + 8] = _fm(inputs["rg_ba"][l, d], 8)
            pp[:, PP_RGBX + o:PP_RGBX + o + 8] = _fm(inputs["rg_bx"][l, d], 8)
            pp[:, PP_RGLAM + o:PP_RGLAM + o + 8] = _fm(inputs["rg_lambda"][l, d], 8)
    pp[:, PP_FNW:PP_FNW + 32] = _fm(inputs["final_norm_w"], 32)
    return pp


def host_rgw(inputs):
    rgw = np.zeros((DEPTH, 8, 128, 4, 128), np.float32)
    for l in range(DEPTH):
        for d in range(2):
            for ax, key in enumerate(("rg_wa", "rg_wx")):
                w = inputs[key][l, d]
                for g in range(8):
                    rgw[l, g, 0:64, d * 2 + ax, 0:64] = w[2 * g]
                    rgw[l, g, 64:128, d * 2 + ax, 64:128] = w[2 * g + 1]
    return rgw


def host_bias(inputs):
    out = np.full((DEPTH, 12, 128, 25, 128), -1e30, np.float32)
    kk = np.arange(128)
    qq = np.arange(128)
    for cls, r in enumerate((0, 2, 28, 30, 4)):
        jb = min(max((r - 4) // 2, 0), 11)
        q_row = r + qq // 64
        q_col = qq % 64
        r0 = np.clip(q_row - 4, 0, 24)
        c0 = np.clip(q_col - 8, 0, 48)
        for j in range(5):
            k_row = 2 * (jb + j) + kk // 64
            k_col = kk % 64
            valid = ((k_row[:, None] >= r0[None, :]) & (k_row[:, None] < r0[None, :] + 8) &
                     (k_col[:, None] >= c0[None, :]) & (k_col[:, None] < c0[None, :] + 16))
            dr = np.clip(k_row[:, None] - q_row[None, :] + 7, 0, 14)
            dc = np.clip(k_col[:, None] - q_col[None, :] + 15, 0, 30)
            for l in range(DEPTH):
                vals = inputs["rpb"][l][:, dr, dc]
                out[l, :, :, cls * 5 + j, :] = np.where(valid[None], vals, np.float32(-1e30))
    return out


def make_in_maps(inputs, cores):
    cf, rope = host_consts()
    rgw = host_rgw(inputs)
    biasb = host_bias(inputs)
    maps = []
    for b in cores:
        maps.append({
            "x": np.ascontiguousarray(inputs["x"][b]), "ctx": np.ascontiguousarray(inputs["ctx"][b]),
            "pp": host_pp(inputs, b), "cf": cf, "rope": rope,
            "ada_w": inputs["ada_w"], "w_in": inputs["w_in"], "w_out": inputs["w_out"],
            "rgw": rgw, "biasb": biasb,
        })
    return maps


def kernel(**inputs):
    inputs = {k: np.asarray(v) for k, v in inputs.items()}
    nc = build_program()
    maps = make_in_maps(inputs, list(range(8)))
    res = run_bass_kernel_spmd(nc, maps, core_ids=list(range(8)))
    return np.stack([r["out"] for r in res.results], axis=0).astype(np.float32)
```

```python
import math
from contextlib import ExitStack

import numpy as np
import ml_dtypes
import concourse.bass as bass
import concourse.mybir as mybir
from concourse.bass_utils import run_bass_kernel_spmd

F32 = mybir.dt.float32
BF16 = mybir.dt.bfloat16
ALU = mybir.AluOpType
AF = mybir.ActivationFunctionType
AX = mybir.AxisListType

D = 4096
SEQ = 2048
NCTX = 256
T = SEQ + NCTX
KC = D // 128
INW = 14336
DEPTH = 2
NG_IN = INW // 128
TOKCH = [(0, 256), (256, 512), (768, 512), (1280, 512), (1792, 512)]
LAT_TOKCH = TOKCH[1:]
NORM_EPS = 1e-6
SUBLN_EPS = 1e-5

PP_CC = 0
PP_ADAB = PP_CC + 64
PP_NORMW = PP_ADAB + 192
PP_FNW = PP_NORMW + 64
PP_LAMQ = PP_FNW + 32
PP_SUBLN = PP_LAMQ + 512
PP_CONVW = PP_SUBLN + 2
PP_CONVB = PP_CONVW + 64
PP_RGBA = PP_CONVB + 16
PP_RGBX = PP_RGBA + 32
PP_RGLAM = PP_RGBX + 32
PP_N = PP_RGLAM + 32


class Buf:
    __slots__ = ("t", "w", "r")

    def __init__(self, t):
        self.t = t
        self.w = None
        self.r = {}

    def __getitem__(self, k):
        return self.t[k]


class DSem:
    __slots__ = ("sem", "cnt")

    def __init__(self, sem):
        self.sem = sem
        self.cnt = 0


class KB:
    def __init__(self, nc, es):
        self.nc = nc
        self.es = es
        self.eng = {"pe": nc.tensor, "dve": nc.vector, "act": nc.scalar, "pool": nc.gpsimd, "sp": nc.sync}
        self.esem = {k: es.enter_context(nc.semaphore("e_" + k)) for k in ("pe", "dve", "act", "pool")}
        self.ecnt = {k: 0 for k in self.esem}
        self.waited = {k: {} for k in self.eng}
        self.dsems = []
        self.nsem = 0

    def dsem(self):
        s = DSem(self.es.enter_context(self.nc.semaphore("d%d" % self.nsem)))
        self.nsem += 1
        self.dsems.append(s)
        return s

    def sb(self, stack, name, shape, dt):
        self.nsb = getattr(self, "nsb", 0) + 1
        return Buf(stack.enter_context(self.nc.sbuf_tensor("s%d_%s" % (self.nsb, name), list(shape), dt)))

    def _deps(self, e, reads, writes):
        w = self.waited[e]
        need = {}
        own = self.esem.get(e)
        for b in reads:
            if b.w is not None:
                s, v = b.w
                if need.get(s, 0) < v:
                    need[s] = v
        for b in writes:
            if b.w is not None:
                s, v = b.w
                if need.get(s, 0) < v:
                    need[s] = v
            for s, v in b.r.items():
                if need.get(s, 0) < v:
                    need[s] = v
        for s, v in need.items():
            if e == "pe" and s is own:
                continue
            if w.get(s, 0) < v:
                self.eng[e].wait_ge(s, v)
                w[s] = v

    def _mark(self, tok, reads, writes):
        s, v = tok
        for b in reads:
            if b.r.get(s, 0) < v:
                b.r[s] = v
        for b in writes:
            b.w = tok
            b.r = {}

    def op(self, e, reads, writes, fn):
        self._deps(e, reads, writes)
        ins = fn(self.eng[e])
        self.ecnt[e] += 1
        tok = (self.esem[e], self.ecnt[e])
        ins.then_inc(tok[0], 1)
        self._mark(tok, reads, writes)
        return tok

    def dma(self, q, ds, out_ap, in_ap, reads, writes):
        self._deps(q, reads, writes)
        ins = self.eng[q].dma_start(out=out_ap, in_=in_ap)
        ds.cnt += 16
        ins.then_inc(ds.sem, 16)
        self._mark((ds.sem, ds.cnt), reads, writes)

    def dma_group(self, q, ds, items):
        for (o, i_, r, w) in items:
            self._deps(q, r, w)
        for (o, i_, r, w) in items:
            ins = self.eng[q].dma_start(out=o, in_=i_)
            ds.cnt += 16
            ins.then_inc(ds.sem, 16)
        tok = (ds.sem, ds.cnt)
        for (o, i_, r, w) in items:
            self._mark(tok, r, w)

    def barrier(self):
        for e in self.eng:
            w = self.waited[e]
            for k, s in self.esem.items():
                v = self.ecnt[k]
                if k == e and e == "pe":
                    continue
                if v > 0 and w.get(s, 0) < v:
                    self.eng[e].wait_ge(s, v)
                    w[s] = v
            for ds in self.dsems:
                if ds.cnt > 0 and w.get(ds.sem, 0) < ds.cnt:
                    self.eng[e].wait_ge(ds.sem, ds.cnt)
                    w[ds.sem] = ds.cnt


class Rot:
    def __init__(self, items):
        self.items = items
        self.i = 0

    def next(self):
        it = self.items[self.i % len(self.items)]
        self.i += 1
        return it


def build_program(debug=None, nlayers=DEPTH):
    debug = debug or set()
    nc = bass.Bass("TRN2", target_bir_lowering=False)

    def din(name, shape, dt=F32):
        return nc.dram_tensor(name, list(shape), dt, kind="ExternalInput").ap()

    def dscr(name, shape, dt=F32):
        kind = "ExternalOutput" if name in debug else "Internal"
        return nc.dram_tensor(name, list(shape), dt, kind=kind).ap()

    x_in = din("x", [SEQ, D])
    ctx_in = din("ctx", [NCTX, D])
    pp_in = din("pp", [128, PP_N])
    cf_in = din("cf", [128, 384])
    rope_in = din("rope", [128, 2, SEQ])
    ada_w = din("ada_w", [DEPTH, D, 3 * D])
    w_in = din("w_in", [DEPTH, D, INW])
    w_out = din("w_out", [DEPTH, D, D])
    rgw_in = din("rgw", [DEPTH, 8, 128, 4, 128])
    bias_in = din("biasb", [DEPTH, 12, 128, 25, 128])
    out_d = nc.dram_tensor("out", [SEQ, D], F32, kind="ExternalOutput").ap()

    xT = dscr("xT", [D, T])
    proj = dscr("proj", [13312, T], BF16)
    xcT = dscr("xcT", [1024, T])
    mrg = dscr("mrg", [D, T], BF16)
    xT_b, proj_b, xcT_b, mrg_b = Buf(xT), Buf(proj), Buf(xcT), Buf(mrg)
    xT_v = xT.rearrange("(c p) t -> p c t", p=128)
    mrg_v = mrg.rearrange("(c p) t -> p c t", p=128)

    es = ExitStack()
    with es:
        kb = KB(nc, es)
        op, dma = kb.op, kb.dma

        pp = kb.sb(es, "pp", [128, PP_N], F32)
        cf = kb.sb(es, "cf", [128, 384], F32)
        identb = kb.sb(es, "identb", [128, 128], BF16)
        rotTb = kb.sb(es, "rotTb", [128, 128], BF16)
        onesb = kb.sb(es, "onesb", [128, 128], BF16)
        modTs = [kb.sb(es, "modT%d" % i, [128, 96, 2], F32) for i in range(DEPTH)]
        Amods = [kb.sb(es, "Amod%d" % i, [128, KC, 2], F32) for i in range(DEPTH)]
        scb = kb.sb(es, "scb", [128, KC, 2], BF16)
        epsc = kb.sb(es, "epsc", [128, 4], F32)
        psb = [Buf(es.enter_context(nc.psum_tensor("ps%d" % i, [128, 512], F32))) for i in range(8)]
        PS = Rot(psb)
        d_misc = kb.dsem()
        NW = 3
        wsems = [kb.dsem() for _ in range(8)]
        dpool = [kb.dsem() for _ in range(12)]

        ident = cf.t[:, 0:128]
        onesf = cf.t[:, 256:384]
        kb.dma_group("sp", d_misc, [(pp.t[:], pp_in[:, :], [], [pp]), (cf.t[:], cf_in[:, :], [], [cf])])
        op("dve", [cf], [identb], lambda e: e.tensor_copy(out=identb.t[:], in_=cf.t[:, 0:128]))
        op("dve", [cf], [rotTb], lambda e: e.tensor_copy(out=rotTb.t[:], in_=cf.t[:, 128:256]))
        op("dve", [cf], [onesb], lambda e: e.tensor_copy(out=onesb.t[:], in_=cf.t[:, 256:384]))
        op("dve", [], [epsc], lambda e: e.memset(epsc.t[:, 0:1], NORM_EPS))
        op("dve", [epsc], [epsc], lambda e: e.memset(epsc.t[:, 1:2], SUBLN_EPS))
        op("dve", [epsc], [epsc], lambda e: e.memset(epsc.t[:, 2:3], 1.0))
        op("dve", [epsc], [epsc], lambda e: e.memset(epsc.t[:, 3:4], 0.0))

        def rstd_from(bk, n, dst, inv_n, eps_col):
            op("act", [bk, epsc], [dst], lambda e: e.activation(out=dst.t[:, 0:n], in_=bk.t[:, 0:n], func=AF.Ln,
                                                               scale=inv_n, bias=epsc.t[:, eps_col:eps_col + 1]))
            op("act", [dst], [dst], lambda e: e.activation(out=dst.t[:, 0:n], in_=dst.t[:, 0:n], func=AF.Exp, scale=-0.5))

        def mk_wpool(stack, n=NW):
            slots = [kb.sb(stack, "w%d" % i, [128, KC, 128], BF16) for i in range(n)]
            return {"slots": slots, "i": 0}

        def load_w(wp, wd, g):
            i = wp["i"] % len(wp["slots"])
            wp["i"] += 1
            src = wd.rearrange("(c p) n -> p c n", p=128)
            sl = wp["slots"][i]
            for q in range(4):
                dma("pool", wsems[i], sl.t[:, q * 8:(q + 1) * 8, :], src[:, q * 8:(q + 1) * 8, g * 128:(g + 1) * 128], [], [sl])
            return sl

        def gen_items(wp, items):
            pend = []
            nxt = 0
            n = len(items)
            prefetch = len(wp["slots"])
            for idx in range(n):
                while nxt < n and nxt <= idx + prefetch - 1:
                    pend.append(load_w(wp, items[nxt]["wd"], items[nxt]["g"]))
                    nxt += 1
                it = items[idx]
                if it.get("pre") is not None:
                    it["pre"](it["g"])
                ws = pend.pop(0)
                banks = []
                for (t0, tn) in it["tokch"]:
                    bk = PS.next()
                    for k in range(KC):
                        rb = it["rhs_bufs"](k) if callable(it["rhs_bufs"]) else it["rhs_bufs"]
                        op("pe", [ws] + rb, [bk], lambda e, k=k, bk=bk, t0=t0, tn=tn, ws=ws, it=it: e.matmul(
                            bk.t[:, 0:tn], lhsT=ws.t[:, k, :], rhs=it["rhs_fn"](k, t0, tn), start=(k == 0), stop=(k == KC - 1)))
                    banks.append((bk, t0, tn))
                it["evac"](it["g"], banks)
                yield

        def linear(wp, wd, ngroups, rhs_fn, rhs_bufs, tokch, evac, pre=None):
            for _ in gen_items(wp, [dict(wd=wd, g=g, rhs_fn=rhs_fn, rhs_bufs=rhs_bufs, tokch=tokch, evac=evac, pre=pre) for g in range(ngroups)]):
                pass

        def ada_items(l):
            def evac_mod(g, banks, l=l):
                bk, _, _ = banks[0]
                op("dve", [bk, pp], [modTs[l]], lambda e: e.tensor_scalar(
                    out=modTs[l].t[:, g, :], in0=bk.t[:, 0:2], scalar1=pp.t[:, PP_ADAB + l * 96 + g:PP_ADAB + l * 96 + g + 1],
                    scalar2=None, op0=ALU.add))
            return [dict(wd=ada_w[l], g=g, rhs_fn=lambda k, t0, tn: scb.t[:, k, 0:2], rhs_bufs=[scb], tokch=[(0, 2)], evac=evac_mod, pre=None)
                    for g in range(96)]

        def gen_ada_wide(l, stack, nslots, ngw=24):
            slots = [kb.sb(stack, "ww%d" % i, [128, KC, 512], BF16) for i in range(nslots)]
            src = ada_w[l].rearrange("(c p) n -> p c n", p=128)
            evac_mod = ada_items(l)[0]["evac"]

            def load(gw):
                sl = slots[gw % nslots]
                for q in range(4):
                    dma("pool", wsems[gw % nslots], sl.t[:, q * 8:(q + 1) * 8, :], src[:, q * 8:(q + 1) * 8, gw * 512:(gw + 1) * 512], [], [sl])
                return sl
            pend = []
            nxt = 0
            for gw in range(ngw):
                while nxt < ngw and nxt <= gw + nslots - 1:
                    pend.append(load(nxt))
                    nxt += 1
                ws = pend.pop(0)
                for j in range(4):
                    bk = PS.next()
                    for k in range(KC):
                        op("pe", [ws, scb], [bk], lambda e, k=k, bk=bk, j=j, ws=ws: e.matmul(
                            bk.t[:, 0:2], lhsT=ws.t[:, k, j * 128:(j + 1) * 128], rhs=scb.t[:, k, 0:2], start=(k == 0), stop=(k == KC - 1)))
                    evac_mod(gw * 4 + j, [(bk, 0, 2)])
                    yield

        def ada_finish(l):
            op("dve", [modTs[l], pp], [Amods[l]], lambda e: e.scalar_tensor_tensor(
                out=Amods[l].t[:], in0=modTs[l].t[:, 32:64, :], scalar=1.0,
                in1=pp.t[:, PP_NORMW + l * 32:PP_NORMW + (l + 1) * 32].unsqueeze(2).to_broadcast([128, KC, 2]),
                op0=ALU.add, op1=ALU.mult))

        op("act", [pp], [scb], lambda e: e.activation(out=scb.t[:], in_=pp.t[:, PP_CC:PP_CC + 64].rearrange("p (c t) -> p c t", t=2), func=AF.Silu))

        with ExitStack() as ph:
            xin = [kb.sb(ph, "xin%d" % i, [128, D], F32) for i in range(2)]
            xst = [kb.sb(ph, "xst%d" % i, [128, KC, 256], F32) for i in range(2)]
            ada_gen = gen_ada_wide(0, ph, 3, 24)
            for tt in range(T // 128):
                i = tt % 2
                src = ctx_in[tt * 128:(tt + 1) * 128, :] if tt < 2 else x_in[(tt - 2) * 128:(tt - 1) * 128, :]
                dma("sp", dpool[i], xin[i].t[:], src, [], [xin[i]])
                for c4 in range(KC // 4):
                    bk = PS.next()
                    for j in range(4):
                        c = c4 * 4 + j
                        op("pe", [xin[i], cf], [bk], lambda e, c=c, j=j, bk=bk, i=i: e.transpose(
                            out=bk.t[:, j * 128:(j + 1) * 128], in_=xin[i].t[:, c * 128:(c + 1) * 128], identity=ident))
                    xs = xst[(tt // 2) % 2]
                    o_ = (tt % 2) * 128
                    dst = xs.t[:, c4 * 4:(c4 + 1) * 4, o_:o_ + 128]
                    srcp = bk.t[:, :].rearrange("p (c t) -> p c t", c=4)
                    if c4 % 2 == 0:
                        op("act", [bk], [xs], lambda e, dst=dst, srcp=srcp: e.copy(out=dst, in_=srcp))
                    else:
                        op("dve", [bk], [xs], lambda e, dst=dst, srcp=srcp: e.tensor_copy(out=dst, in_=srcp))
                if tt % 2 == 1:
                    xs = xst[(tt // 2) % 2]
                    dma("sp", dpool[2 + (tt // 2) % 2], xT_v[:, :, (tt - 1) * 128:(tt + 1) * 128], xs.t[:], [xs], [xT_b])
                for _ in range(6):
                    next(ada_gen, None)
            for _ in ada_gen:
                pass
            ada_finish(0)
            kb.barrier()

        for l in range(nlayers):
            last = (l == DEPTH - 1)
            need_ctx = not last
            lam_init = 0.8 - 0.6 * math.exp(-0.3 * l)
            modT, Amod = modTs[l], Amods[l]
            if "modT" in debug:
                dbg = nc.dram_tensor("dbg_modT", [128, 192], F32, kind="ExternalOutput").ap()
                dma("sp", d_misc, dbg[:, :], modT.t[:].rearrange("p a b -> p (a b)"), [modT], [])
            if "stop_mod" in debug:
                break

            with ExitStack() as phx:
                XT = kb.sb(phx, "XT", [128, KC, T], BF16)
                with ExitStack() as ph:
                    xch = [kb.sb(ph, "xch%d" % i, [128, KC, 128], F32) for i in range(2)]
                    sq = [kb.sb(ph, "sq%d" % i, [128, 8, 128], F32) for i in range(2)]
                    rstd = [kb.sb(ph, "rstd%d" % i, [128, 128], F32) for i in range(2)]
                    tmp8 = [kb.sb(ph, "tmp8%d" % i, [128, 8, 128], F32) for i in range(2)]
                    SQ, TM = Rot(sq), Rot(tmp8)
                    bks = {}

                    def n_dma(tt):
                        i = tt % 2
                        dma("sp", dpool[i], xch[i].t[:], xT_v[:, :, tt * 128:(tt + 1) * 128], [xT_b], [xch[i]])

                    def n_stageA(tt):
                        i = tt % 2
                        bk = PS.next()
                        bks[tt] = bk
                        for c8 in range(4):
                            s = SQ.next()
                            op("dve", [xch[i]], [s], lambda e, s=s, c8=c8, i=i: e.tensor_tensor(
                                out=s.t[:], in0=xch[i].t[:, c8 * 8:(c8 + 1) * 8, :], in1=xch[i].t[:, c8 * 8:(c8 + 1) * 8, :], op=ALU.mult))
                            for j in range(8):
                                op("pe", [s, cf], [bk], lambda e, s=s, j=j, bk=bk, c8=c8: e.matmul(
                                    bk.t[:, 0:128], lhsT=onesf, rhs=s.t[:, j, :], start=(c8 == 0 and j == 0), stop=(c8 == 3 and j == 7)))

                    def n_stageC(tt):
                        rstd_from(bks[tt], 128, rstd[tt % 2], 1.0 / D, 0)

                    def n_stageB(tt):
                        i = tt % 2
                        m = 1 if tt < 2 else 0
                        r = rstd[i]
                        for c8 in range(4):
                            tm = TM.next()
                            op("dve", [xch[i], r], [tm], lambda e, r=r, tm=tm, i=i, c8=c8: e.tensor_tensor(
                                out=tm.t[:], in0=xch[i].t[:, c8 * 8:(c8 + 1) * 8, :], in1=r.t[:].unsqueeze(1).to_broadcast([128, 8, 128]), op=ALU.mult))
                            for j in range(8):
                                c = c8 * 8 + j
                                op("act", [tm, Amod, modT], [XT], lambda e, c=c, j=j, tm=tm, m=m, tt=tt: e.activation(
                                    out=XT.t[:, c, tt * 128:(tt + 1) * 128], in_=tm.t[:, j, :], func=AF.Identity,
                                    scale=Amod.t[:, c, m:m + 1], bias=modT.t[:, c, m:m + 1]))

                    NT_ = T // 128
                    n_dma(0)
                    n_stageA(0)
                    n_stageC(0)
                    for tt in range(NT_):
                        if tt + 1 < NT_:
                            n_dma(tt + 1)
                        n_stageB(tt)
                        if tt + 1 < NT_:
                            n_stageA(tt + 1)
                            n_stageC(tt + 1)
                    kb.barrier()
                if "XT" in debug:
                    dbg_xt = nc.dram_tensor("dbg_XT", [128, KC, T], BF16, kind="ExternalOutput").ap()
                    dma("sp", d_misc, dbg_xt[:, :, :], XT.t[:], [XT], [])
                if "stop_norm" in debug:
                    kb.barrier()
                    break

                with ExitStack() as ph:
                    wp = mk_wpool(ph, 4)
                    stg = [kb.sb(ph, "stg%d" % i, [128, T], BF16) for i in range(2)]
                    stgf = [kb.sb(ph, "stgf%d" % i, [128, T], F32) for i in range(1)]
                    st = {"i": 0}

                    def evac_in(g, banks):
                        fam = g // 12 if g < 96 else 8 + (g - 96) // 8
                        if fam == 8:
                            for (bk, t0, tn) in banks:
                                op("dve", [bk], [stgf[0]], lambda e, bk=bk, t0=t0, tn=tn: e.tensor_copy(out=stgf[0].t[:, t0:t0 + tn], in_=bk.t[:, 0:tn]))
                            r0 = (g - 96) * 128
                            dma("sp", dpool[2], xcT[r0:r0 + 128, :], stgf[0].t[:], [stgf[0]], [xcT_b])
                            return
                        i = st["i"] % 2
                        st["i"] += 1
                        gate = fam in (3, 7, 9)
                        for (bk, t0, tn) in banks:
                            if gate:
                                op("act", [bk], [stg[i]], lambda e, bk=bk, t0=t0, tn=tn, i=i: e.activation(out=stg[i].t[:, t0:t0 + tn], in_=bk.t[:, 0:tn], func=AF.Silu))
                            else:
                                op("dve", [bk], [stg[i]], lambda e, bk=bk, t0=t0, tn=tn, i=i: e.tensor_copy(out=stg[i].t[:, t0:t0 + tn], in_=bk.t[:, 0:tn]))
                        r0 = g * 128 if g < 96 else (g - 8) * 128
                        dma("sp", dpool[i], proj[r0:r0 + 128, :], stg[i].t[:], [stg[i]], [proj_b])

                    def fam_of(g):
                        return g // 12 if g < 96 else 8 + (g - 96) // 8
                    in_items = [dict(wd=w_in[l], g=g, rhs_fn=lambda k, t0, tn: XT.t[:, k, t0:t0 + tn], rhs_bufs=[XT],
                                     tokch=(LAT_TOKCH if (last and fam_of(g) in (0, 3, 4, 7, 9)) else TOKCH), evac=evac_in, pre=None)
                                for g in range(NG_IN)]
                    ad = []
                    if l + 1 < nlayers:
                        ad += ada_items(l + 1)[0:64]
                    if l > 0:
                        ad += ada_items(l)[64:96]
                    items = []
                    for idx in range(NG_IN):
                        items.append(in_items[idx])
                        if idx < len(ad):
                            items.append(ad[idx])
                    items += ad[NG_IN:]
                    for _ in gen_items(wp, items):
                        pass
                    if l + 1 < nlayers:
                        ada_finish(l + 1)
                    kb.barrier()
            if "stop_proj" in debug:
                break

            with ExitStack() as ph:
                rope = kb.sb(ph, "rope", [128, 2, SEQ], F32)
                dma("sp", d_misc, rope.t[:], rope_in[:, :, :], [], [rope])
                lt = kb.sb(ph, "lamtmp", [128, 136], F32)
                nlam = kb.sb(ph, "nlam", [128, 1], F32)
                sw = kb.sb(ph, "sw", [128, 1], F32)
                lq = pp.t[:, PP_LAMQ + l * 256:PP_LAMQ + (l + 1) * 256]
                op("dve", [pp], [lt], lambda e: e.tensor_tensor(out=lt.t[:, 0:64], in0=lq[:, 0:64], in1=lq[:, 64:128], op=ALU.mult))
                op("dve", [pp, lt], [lt], lambda e: e.tensor_tensor(out=lt.t[:, 64:128], in0=lq[:, 128:192], in1=lq[:, 192:256], op=ALU.mult))
                op("dve", [lt], [lt], lambda e: e.reduce_sum(out=lt.t[:, 128:129], in_=lt.t[:, 0:64], axis=AX.X))
                op("dve", [lt], [lt], lambda e: e.reduce_sum(out=lt.t[:, 129:130], in_=lt.t[:, 64:128], axis=AX.X))
                op("act", [lt], [lt], lambda e: e.activation(out=lt.t[:, 130:132], in_=lt.t[:, 128:130], func=AF.Exp))
                op("dve", [lt], [lt], lambda e: e.tensor_tensor(out=lt.t[:, 132:133], in0=lt.t[:, 131:132], in1=lt.t[:, 130:131], op=ALU.subtract))
                op("dve", [lt], [nlam], lambda e: e.tensor_scalar(out=nlam.t[:], in0=lt.t[:, 132:133], scalar1=-lam_init, scalar2=None, op0=ALU.add))
                op("dve", [pp], [sw], lambda e: e.tensor_scalar(out=sw.t[:], in0=pp.t[:, PP_SUBLN + l:PP_SUBLN + l + 1], scalar1=1.0 - lam_init, scalar2=None, op0=ALU.mult))

                ld = [[kb.sb(ph, "ald%d_%d" % (i, j), [128, T], BF16) for j in range(4)] for i in range(2)]
                qrs = [kb.sb(ph, "qr%d" % i, [128, T], BF16) for i in range(2)]
                krs = [kb.sb(ph, "kr%d" % i, [128, T], BF16) for i in range(2)]
                vtoks = [kb.sb(ph, "vtok%d" % i, [128, 18, 128], BF16) for i in range(2)]
                mst = [kb.sb(ph, "amst%d" % i, [128, T], BF16) for i in range(2)]
                pts = [kb.sb(ph, "pt%d" % i, [128, 2, 2, 256], BF16) for i in range(3)]
                f = [kb.sb(ph, "af%d" % i, [128, 512], F32) for i in range(8)]
                fpost = [[kb.sb(ph, "afp%d_%d" % (i, j), [128, 256], F32) for j in range(2)] for i in range(2)]
                SC = Rot([(psb[4], psb[5]), (psb[6], psb[7])])
                OB = Rot([psb[0], psb[1]])
                ZB = psb[2]
                AUX = psb[3]
                PTS = Rot(pts)
                ZS = Rot([kb.sb(ph, "azs%d" % i, [128, 512], BF16) for i in range(3)])
                qscale = 64 ** -0.5

                def a_load(h):
                    i = h % 2
                    kb.dma_group("sp", dpool[i], [(ld[i][j].t[:], proj[(j * 12 + h) * 128:(j * 12 + h + 1) * 128, :], [proj_b], [ld[i][j]]) for j in range(4)])

                def gen_prologue(h):
                    qT, kT, vT, gT = ld[h % 2]
                    qr, kr, vtok = qrs[h % 2], krs[h % 2], vtoks[h % 2]
                    for (src, dst) in ((qT, qr), (kT, kr)):
                        op("dve", [src], [dst], lambda e, src=src, dst=dst: e.tensor_copy(out=dst.t[:, 0:256], in_=src.t[:, 0:256]))
                        for ci, (t0, tn) in enumerate(LAT_TOKCH):
                            bk = AUX
                            op("pe", [src, rotTb], [bk], lambda e, bk=bk, src=src, t0=t0: e.matmul(
                                bk.t[:, 0:512], lhsT=rotTb.t[:], rhs=src.t[:, t0:t0 + 512], start=True, stop=True))
                            op("dve", [src, rope], [f[6]], lambda e, src=src, t0=t0: e.tensor_tensor(
                                out=f[6].t[:], in0=src.t[:, t0:t0 + 512], in1=rope.t[:, 0, t0 - 256:t0 + 256], op=ALU.mult))
                            op("dve", [bk, rope], [f[7]], lambda e, bk=bk, t0=t0: e.tensor_tensor(
                                out=f[7].t[:], in0=bk.t[:, 0:512], in1=rope.t[:, 1, t0 - 256:t0 + 256], op=ALU.mult))
                            op("dve", [f[6], f[7]], [dst], lambda e, dst=dst, t0=t0: e.tensor_tensor(
                                out=dst.t[:, t0:t0 + 512], in0=f[6].t[:], in1=f[7].t[:], op=ALU.add))
                            yield
                    for k0 in range(0, 18, 8):
                        nk = min(8, 18 - k0)
                        bk = AUX
                        bkb = bk.t[:, :].bitcast(BF16)
                        for j in range(nk):
                            op("pe", [vT, identb], [bk], lambda e, j=j, k0=k0, bkb=bkb: e.transpose(
                                out=bkb[:, j * 128:(j + 1) * 128], in_=vT.t[:, (k0 + j) * 128:(k0 + j + 1) * 128], identity=identb.t[:]))
                        op("dve", [bk], [vtok], lambda e, bkb=bkb, k0=k0, nk=nk, vtok=vtok: e.tensor_copy(
                            out=vtok.t[:, k0:k0 + nk, :], in_=bkb[:, 0:nk * 128].rearrange("p (k d) -> p k d", k=nk)))
                        yield

                a_load(0)
                for _ in gen_prologue(0):
                    pass
                pending = []
                qranges = [(256 + 256 * i, 18) for i in range(8)]
                if need_ctx:
                    qranges.append((0, 2))
                tasks = [(h, qi, q0, nkc, p) for h in range(12) for qi, (q0, nkc) in enumerate(qranges) for p in range(nkc // 2)]
                it_ob = {}
                fcount = [0]

                def scores(task):
                    h, qi, q0, nkc, p = task
                    qr, kr = qrs[h % 2], krs[h % 2]
                    bA, bB = SC.next()
                    for slot in range(2):
                        kc = 2 * p + slot
                        for n, bkd in enumerate((bA, bB)):
                            op("pe", [kr, qr], [bkd], lambda e, n=n, kc=kc, bkd=bkd, slot=slot: e.matmul(
                                bkd.t[:, slot * 256:(slot + 1) * 256], lhsT=kr.t[n * 64:(n + 1) * 64, kc * 128:(kc + 1) * 128],
                                rhs=qr.t[n * 64:(n + 1) * 64, q0:q0 + 256], start=True, stop=True))
                    return (bA, bB)

                def drain_pending():
                    while pending:
                        pp2 = pending.pop(0)
                        pp2(0)
                        pp2(1)

                def finish_iteration(task):
                    h, qi, q0, nkc, p = task
                    Ob = it_ob[(h, qi)]
                    gT = ld[h % 2][3]
                    ms = mst[h % 2]
                    fo, fq = fpost[fcount[0] % 2]
                    fcount[0] += 1
                    op("dve", [ZB], [f[4]], lambda e: e.tensor_copy(out=f[4].t[:], in_=ZB.t[:, 0:512]))
                    op("dve", [f[4]], [f[0]], lambda e: e.reciprocal(out=f[0].t[:], in_=f[4].t[:]))
                    op("dve", [Ob, f[0]], [f[2]], lambda e: e.tensor_tensor(out=f[2].t[:], in0=Ob.t[:, 0:512], in1=f[0].t[:], op=ALU.mult))
                    op("dve", [f[2], nlam], [fo], lambda e: e.scalar_tensor_tensor(
                        out=fo.t[:], in0=f[2].t[:, 256:512], scalar=nlam.t[:, 0:1], in1=f[2].t[:, 0:256], op0=ALU.mult, op1=ALU.add))
                    op("dve", [fo], [fq], lambda e: e.tensor_tensor(out=fq.t[:], in0=fo.t[:], in1=fo.t[:], op=ALU.mult))
                    st = {}

                    def post2(stage):
                        if stage == 0:
                            if "done0" in st:
                                return
                            st["done0"] = True
                            op("pe", [fq, cf], [AUX], lambda e: e.matmul(AUX.t[:, 0:256], lhsT=onesf, rhs=fq.t[:], start=True, stop=True))
                            return
                        rstd_from(AUX, 256, f[3], 1.0 / 128, 1)
                        op("dve", [fo, f[3]], [fo], lambda e: e.tensor_tensor(out=fo.t[:], in0=fo.t[:], in1=f[3].t[:, 0:256], op=ALU.mult))
                        op("dve", [fo, sw, gT], [ms], lambda e: e.scalar_tensor_tensor(
                            out=ms.t[:, q0:q0 + 256], in0=fo.t[:], scalar=sw.t[:, 0:1], in1=gT.t[:, q0:q0 + 256], op0=ALU.mult, op1=ALU.mult))
                        if qi == len(qranges) - 1:
                            c0 = 0 if need_ctx else 256
                            dma("sp", dpool[2 + h % 2], mrg[h * 128:(h + 1) * 128, c0:T], ms.t[:, c0:T], [ms], [mrg_b])
                    pending.append(post2)

                def emit_z(zp):
                    zs_, task_ = zp
                    h_, qi_, q0_, nkc_, p_ = task_
                    last = (p_ == nkc_ // 2 - 1)
                    op("pe", [onesb, zs_], [ZB], lambda e: e.matmul(ZB.t[:, 0:512], lhsT=onesb.t[:], rhs=zs_.t[:], start=(p_ == 0), stop=last))
                    if last:
                        finish_iteration(task_)

                pro = iter(())
                zprev = None
                cur = scores(tasks[0])
                for k, task in enumerate(tasks):
                    h, qi, q0, nkc, p = task
                    npair = nkc // 2
                    vtok = vtoks[h % 2]
                    if p == 0:
                        it_ob[(h, qi)] = OB.next()
                    Ob = it_ob[(h, qi)]
                    if k + 1 < len(tasks):
                        if tasks[k + 1][0] != h:
                            for _ in pro:
                                pass
                        nxt_s = scores(tasks[k + 1])
                    else:
                        nxt_s = None
                    pt = PTS.next()
                    for n in range(2):
                        op("act", [cur[n]], [pt], lambda e, n=n, cur=cur, pt=pt: e.activation(
                            out=pt.t[:, :, n, :], in_=cur[n].t[:, 0:512].rearrange("p (s q) -> p s q", s=2), func=AF.Exp, scale=qscale))
                    zs = ZS.next()
                    op("pool", [pt], [zs], lambda e, pt=pt, zs=zs: e.tensor_tensor(
                        out=zs.t[:], in0=pt.t[:, 0, :, :].rearrange("p n q -> p (n q)"),
                        in1=pt.t[:, 1, :, :].rearrange("p n q -> p (n q)"), op=ALU.add))
                    for slot in range(2):
                        kc = 2 * p + slot
                        rhs = pt.t[:, slot, :, :].rearrange("p n q -> p (n q)")
                        op("pe", [vtok, pt], [Ob], lambda e, kc=kc, rhs=rhs: e.matmul(
                            Ob.t[:, 0:512], lhsT=vtok.t[:, kc, :], rhs=rhs, start=(kc == 0), stop=(kc == nkc - 1)))
                    if zprev is not None:
                        emit_z(zprev)
                    zprev = (zs, task)
                    cur = nxt_s
                    if p == 0 and qi == 1 and h + 1 < 12:
                        a_load(h + 1)
                    if p == 0 and qi == 2:
                        pro = gen_prologue(h + 1) if h + 1 < 12 else iter(())
                    if npair >= 9:
                        if p in (5, 6):
                            next(pro, None)
                        if p == 7 and pending:
                            pending[0](0)
                        if p == 8:
                            drain_pending()
                emit_z(zprev)
                drain_pending()
                kb.barrier()
            if "stop_A" in debug:
                break

            with ExitStack() as ph:
                ld = [[kb.sb(ph, "bld%d_%d" % (i, j), [128, T], BF16) for j in range(4)] for i in range(2)]
                bias = [kb.sb(ph, "bbias%d" % i, [128, 25, 128], F32) for i in range(2)]
                vtok = kb.sb(ph, "bvtok", [128, 18, 128], BF16)
                mst = [kb.sb(ph, "bmst%d" % i, [128, T], BF16) for i in range(2)]
                pts = [kb.sb(ph, "bpt%d" % i, [128, 896], BF16) for i in range(3)]
                tb = [kb.sb(ph, "btb%d" % i, [128, 640], F32) for i in range(2)]
                f = [kb.sb(ph, "bf%d" % i, [128, 512], F32) for i in range(3)]
                SC = Rot([(psb[4], psb[5]), (psb[6], psb[7])])
                OZ = Rot([(psb[0], psb[1]), (psb[2], psb[3])])
                PTS, TB = Rot(pts), Rot(tb)
                bscale = 128 ** -0.5

                def b_load(h):
                    i = h % 2
                    kb.dma_group("sp", dpool[i], [(ld[i][j].t[:], proj[((4 + j) * 12 + h) * 128:((4 + j) * 12 + h + 1) * 128, :], [proj_b], [ld[i][j]]) for j in range(4)]
                                 + [(bias[i].t[:], bias_in[l, h], [], [bias[i]])])

                b_load(0)
                for h in range(12):
                    if h + 1 < 12:
                        b_load(h + 1)
                    qT, kT, vT, gT = ld[h % 2]
                    bi = bias[h % 2]
                    for k0 in range(0, 18, 8):
                        nk = min(8, 18 - k0)
                        bk = SC.next()[0]
                        bkb = bk.t[:, :].bitcast(BF16)
                        for j in range(nk):
                            op("pe", [vT, identb], [bk], lambda e, j=j, k0=k0, bkb=bkb: e.transpose(
                                out=bkb[:, j * 128:(j + 1) * 128], in_=vT.t[:, (k0 + j) * 128:(k0 + j + 1) * 128], identity=identb.t[:]))
                        op("act", [bk], [vtok], lambda e, bkb=bkb, k0=k0, nk=nk: e.copy(
                            out=vtok.t[:, k0:k0 + nk, :], in_=bkb[:, 0:nk * 128].rearrange("p (k d) -> p k d", k=nk)))
                    ms = mst[h % 2]
                    groups = [[(256 + 128 * (g4 * 4 + j), 2 * (g4 * 4 + j)) for j in range(4)] for g4 in range(4)]
                    if need_ctx:
                        groups.append([(0, None), (128, None)])
                    def scores(blk):
                        qt0, r = blk
                        s0, s1 = SC.next()
                        qa = qT.t[:, qt0:qt0 + 128]
                        if r is not None:
                            jb = min(max((r - 4) // 2, 0), 11)
                            for j in range(5):
                                kt0 = (2 + jb + j) * 128
                                bkd = s0 if j < 4 else s1
                                col = (j % 4) * 128
                                op("pe", [kT, qT], [bkd], lambda e, kt0=kt0, bkd=bkd, col=col: e.matmul(
                                    bkd.t[:, col:col + 128], lhsT=kT.t[:, kt0:kt0 + 128], rhs=qa, start=True, stop=True))
                        for j in range(2):
                            col = 128 + j * 128
                            op("pe", [kT, qT], [s1], lambda e, j=j, col=col: e.matmul(
                                s1.t[:, col:col + 128], lhsT=kT.t[:, j * 128:(j + 1) * 128], rhs=qa, start=True, stop=True))
                        return (s0, s1)

                    blocks = [(gi, bi_, blk) for gi, grp in enumerate(groups) for bi_, blk in enumerate(grp)]
                    pend_post = []
                    cur = scores(blocks[0][2])
                    for idx, (gi, bi_, blk) in enumerate(blocks):
                        grp = groups[gi]
                        if bi_ == 0:
                            Ob, Zb = OZ.next()
                        nxt_s = scores(blocks[idx + 1][2]) if idx + 1 < len(blocks) else None
                        qt0, r = blk
                        s0, s1 = cur
                        pt = PTS.next()
                        chunks = []
                        if r is not None:
                            jb = min(max((r - 4) // 2, 0), 11)
                            cls = {0: 0, 2: 1, 28: 2, 30: 3}.get(r, 4)
                            t_ = TB.next()
                            op("dve", [s0, bi], [t_], lambda e, t_=t_, s0=s0, cls=cls: e.scalar_tensor_tensor(
                                out=t_.t[:, 0:512], in0=s0.t[:, 0:512], scalar=bscale,
                                in1=bi.t[:, cls * 5:cls * 5 + 4, :].rearrange("p a b -> p (a b)"), op0=ALU.mult, op1=ALU.add))
                            op("dve", [s1, bi, t_], [t_], lambda e, t_=t_, s1=s1, cls=cls: e.scalar_tensor_tensor(
                                out=t_.t[:, 512:640], in0=s1.t[:, 0:128], scalar=bscale,
                                in1=bi.t[:, cls * 5 + 4, :], op0=ALU.mult, op1=ALU.add))
                            op("act", [t_], [pt], lambda e, t_=t_, pt=pt: e.activation(out=pt.t[:, 0:640], in_=t_.t[:, 0:640], func=AF.Exp))
                            chunks += [(2 + jb + j, j * 128) for j in range(5)]
                        op("act", [s1, pt], [pt], lambda e, s1=s1, pt=pt: e.activation(out=pt.t[:, 640:896], in_=s1.t[:, 128:384], func=AF.Exp, scale=bscale))
                        chunks += [(0, 640), (1, 768)]
                        while pend_post:
                            pend_post.pop(0)()
                        oc_ = bi_ * 128
                        for ci, (kc, col) in enumerate(chunks):
                            op("pe", [vtok, pt], [Ob], lambda e, kc=kc, col=col, ci=ci, pt=pt, oc_=oc_, Ob=Ob: e.matmul(
                                Ob.t[:, oc_:oc_ + 128], lhsT=vtok.t[:, kc, :], rhs=pt.t[:, col:col + 128], start=(ci == 0), stop=(ci == len(chunks) - 1)))
                        for ci, (kc, col) in enumerate(chunks):
                            op("pe", [onesb, pt], [Zb], lambda e, col=col, ci=ci, pt=pt, oc_=oc_, Zb=Zb: e.matmul(
                                Zb.t[:, oc_:oc_ + 128], lhsT=onesb.t[:], rhs=pt.t[:, col:col + 128], start=(ci == 0), stop=(ci == len(chunks) - 1)))
                        cur = nxt_s
                        if bi_ == len(grp) - 1:
                            def post(Ob=Ob, Zb=Zb, qn=128 * len(grp), q0=grp[0][0]):
                                op("act", [Zb], [f[0]], lambda e: e.activation(out=f[0].t[:, 0:qn], in_=Zb.t[:, 0:qn], func=AF.Ln))
                                op("act", [f[0]], [f[0]], lambda e: e.activation(out=f[0].t[:, 0:qn], in_=f[0].t[:, 0:qn], func=AF.Exp, scale=-1.0))
                                op("dve", [Ob, f[0]], [f[1]], lambda e: e.tensor_tensor(out=f[1].t[:, 0:qn], in0=Ob.t[:, 0:qn], in1=f[0].t[:, 0:qn], op=ALU.mult))
                                op("dve", [f[1], gT], [ms], lambda e: e.tensor_tensor(out=ms.t[:, q0:q0 + qn], in0=f[1].t[:, 0:qn], in1=gT.t[:, q0:q0 + qn], op=ALU.mult))
                            pend_post.append(post)
                    while pend_post:
                        pend_post.pop(0)()
                    c0 = 0 if need_ctx else 256
                    dma("sp", dpool[2 + h % 2], mrg[(12 + h) * 128:(13 + h) * 128, c0:T], ms.t[:, c0:T], [ms], [mrg_b])
                kb.barrier()
            if "stop_B" in debug:
                break

            with ExitStack() as ph:
                xcl = [kb.sb(ph, "cx%d" % i, [128, T], F32) for i in range(2)]
                gcl = [kb.sb(ph, "cg%d" % i, [128, T], BF16) for i in range(2)]
                rw = [kb.sb(ph, "crw%d" % i, [128, 4, 128], F32) for i in range(2)]
                u_l = [kb.sb(ph, "cu%d" % i, [128, T], F32) for i in range(2)]
                rr_l = [kb.sb(ph, "cr%d" % i, [128, T], F32) for i in range(2)]
                ii_l = [kb.sb(ph, "ci%d" % i, [128, T], F32) for i in range(2)]
                aa_l = [kb.sb(ph, "ca%d" % i, [128, T], F32) for i in range(2)]
                a2_l = [kb.sb(ph, "ca2%d" % i, [128, T], F32) for i in range(2)]
                bb_l = [kb.sb(ph, "cb%d" % i, [128, T], F32) for i in range(2)]
                hh_l = [[kb.sb(ph, "ch%d_%d" % (j, i), [128, T], F32) for i in range(2)] for j in range(2)]
                mst = [kb.sb(ph, "cmst%d" % i, [128, T], BF16) for i in range(2)]
                nsp = kb.sb(ph, "nsp", [128, 3, 16], F32)
                lo = PP_RGLAM + l * 16
                op("act", [pp], [nsp], lambda e: e.activation(out=nsp.t[:, 0, :], in_=pp.t[:, lo:lo + 16], func=AF.Exp, scale=-1.0))
                op("act", [nsp, epsc], [nsp], lambda e: e.activation(out=nsp.t[:, 0, :], in_=nsp.t[:, 0, :], func=AF.Ln, bias=epsc.t[:, 2:3]))
                op("dve", [nsp], [nsp], lambda e: e.tensor_scalar(out=nsp.t[:, 1, :], in0=nsp.t[:, 0, :], scalar1=-8.0, scalar2=None, op0=ALU.mult))
                op("dve", [nsp], [nsp], lambda e: e.tensor_scalar(out=nsp.t[:, 2, :], in0=nsp.t[:, 0, :], scalar1=-16.0, scalar2=None, op0=ALU.mult))

                def c_load(g):
                    i = g % 2
                    kb.dma_group("sp", dpool[i], [(xcl[i].t[:], xcT[g * 128:(g + 1) * 128, :], [xcT_b], [xcl[i]]),
                                                  (gcl[i].t[:], proj[(96 + g) * 128:(97 + g) * 128, :], [proj_b], [gcl[i]]),
                                                  (rw[i].t[:], rgw_in[l, g], [], [rw[i]])])

                def c_conv(g):
                    xc_ = xcl[g % 2]
                    u = u_l[g % 2]
                    cw = lambda j: pp.t[:, PP_CONVW + l * 32 + g * 4 + j:PP_CONVW + l * 32 + g * 4 + j + 1]
                    cb = pp.t[:, PP_CONVB + l * 8 + g:PP_CONVB + l * 8 + g + 1]
                    for (s0, sn) in ((0, 256), (256, 2048)):
                        op("dve", [xc_, pp], [u], lambda e, s0=s0, sn=sn: e.tensor_scalar(
                            out=u.t[:, s0:s0 + sn], in0=xc_.t[:, s0:s0 + sn], scalar1=cw(2), scalar2=cb, op0=ALU.mult, op1=ALU.add))
                        for (j, do, so, n) in ((0, 2, 0, sn - 2), (1, 1, 0, sn - 1), (3, 0, 1, sn - 1)):
                            op("dve", [xc_, pp, u], [u], lambda e, s0=s0, j=j, do=do, so=so, n=n: e.scalar_tensor_tensor(
                                out=u.t[:, s0 + do:s0 + do + n], in0=xc_.t[:, s0 + so:s0 + so + n], scalar=cw(j),
                                in1=u.t[:, s0 + do:s0 + do + n], op0=ALU.mult, op1=ALU.add))

                def c_dir(g, d):
                    rw_ = rw[g % 2]
                    u, rr, ii, aa, a2, bb, hh = u_l[g % 2], rr_l[g % 2], ii_l[g % 2], aa_l[g % 2], a2_l[g % 2], bb_l[g % 2], hh_l[g % 2]
                    pidx = (l * 2 + d) * 8 + g
                    ba = pp.t[:, PP_RGBA + pidx:PP_RGBA + pidx + 1]
                    bx = pp.t[:, PP_RGBX + pidx:PP_RGBX + pidx + 1]
                    for (t0, tn) in TOKCH:
                        bA, bX = PS.next(), PS.next()
                        op("pe", [rw_, u], [bA], lambda e, bA=bA, t0=t0, tn=tn: e.matmul(
                            bA.t[:, 0:tn], lhsT=rw_.t[:, d * 2, :], rhs=u.t[:, t0:t0 + tn], start=True, stop=True))
                        op("pe", [rw_, u], [bX], lambda e, bX=bX, t0=t0, tn=tn: e.matmul(
                            bX.t[:, 0:tn], lhsT=rw_.t[:, d * 2 + 1, :], rhs=u.t[:, t0:t0 + tn], start=True, stop=True))
                        op("act", [bA, pp], [rr], lambda e, bA=bA, t0=t0, tn=tn: e.activation(
                            out=rr.t[:, t0:t0 + tn], in_=bA.t[:, 0:tn], func=AF.Sigmoid, bias=ba))
                        op("act", [bX, pp], [ii], lambda e, bX=bX, t0=t0, tn=tn: e.activation(
                            out=ii.t[:, t0:t0 + tn], in_=bX.t[:, 0:tn], func=AF.Sigmoid, bias=bx))
                    op("act", [rr, nsp], [aa], lambda e: e.activation(out=aa.t[:], in_=rr.t[:], func=AF.Exp, scale=nsp.t[:, 1, d * 8 + g:d * 8 + g + 1]))
                    op("act", [rr, nsp], [a2], lambda e: e.activation(out=a2.t[:], in_=rr.t[:], func=AF.Exp, scale=nsp.t[:, 2, d * 8 + g:d * 8 + g + 1]))
                    op("act", [a2, epsc], [a2], lambda e: e.activation(out=a2.t[:], in_=a2.t[:], func=AF.Sqrt, scale=-1.0, bias=epsc.t[:, 2:3]))
                    op("dve", [ii, u], [bb], lambda e: e.tensor_tensor(out=bb.t[:], in0=ii.t[:], in1=u.t[:], op=ALU.mult))
                    op("dve", [bb, a2], [bb], lambda e: e.tensor_tensor(out=bb.t[:], in0=bb.t[:], in1=a2.t[:], op=ALU.mult))
                    h_ = hh[d]
                    if d == 0:
                        op("dve", [aa, bb], [h_], lambda e: e.tensor_tensor_scan(
                            out=h_.t[:, 0:256], data0=aa.t[:, 0:256], data1=bb.t[:, 0:256], initial=0.0, op0=ALU.mult, op1=ALU.add))
                        op("dve", [aa, bb, h_], [h_], lambda e: e.tensor_tensor_scan(
                            out=h_.t[:, 256:T], data0=aa.t[:, 256:T], data1=bb.t[:, 256:T], initial=h_.t[:, 255:256], op0=ALU.mult, op1=ALU.add))
                    else:
                        op("dve", [aa, bb], [h_], lambda e: e.tensor_tensor_scan(
                            out=h_.t[:, 0:256][:, ::-1], data0=aa.t[:, 0:256][:, ::-1], data1=bb.t[:, 0:256][:, ::-1], initial=0.0, op0=ALU.mult, op1=ALU.add))
                        op("dve", [aa, bb, h_], [h_], lambda e: e.tensor_tensor_scan(
                            out=h_.t[:, 256:T][:, ::-1], data0=aa.t[:, 256:T][:, ::-1], data1=bb.t[:, 256:T][:, ::-1], initial=h_.t[:, 0:1], op0=ALU.mult, op1=ALU.add))

                def c_final(g):
                    hh = hh_l[g % 2]
                    gc_ = gcl[g % 2]
                    ms = mst[g % 2]
                    op("dve", [hh[0], hh[1]], [hh[0]], lambda e: e.tensor_tensor(out=hh[0].t[:], in0=hh[0].t[:], in1=hh[1].t[:], op=ALU.add))
                    op("dve", [hh[0], gc_], [ms], lambda e: e.tensor_tensor(out=ms.t[:], in0=hh[0].t[:], in1=gc_.t[:], op=ALU.mult))
                    dma("sp", dpool[2 + g % 2], mrg[(24 + g) * 128:(25 + g) * 128, :], ms.t[:], [ms], [mrg_b])

                c_load(0)
                c_load(1)
                c_conv(0)
                for g in range(8):
                    c_dir(g, 0)
                    if g + 1 < 8:
                        c_conv(g + 1)
                    c_dir(g, 1)
                    c_final(g)
                    if g + 2 < 8:
                        c_load(g + 2)
                kb.barrier()
            if "stop_C" in debug:
                break

            with ExitStack() as ph:
                XTs = [kb.sb(ph, "MT%d" % q, [128, 4, T], BF16) for q in range(8)]
                wp = mk_wpool(ph)
                xr = [kb.sb(ph, "xr%d" % i, [128, T], F32) for i in range(2)]
                for q in range(8):
                    dma("sp", dpool[4 + q], XTs[q].t[:], mrg_v[:, q * 4:(q + 1) * 4, :], [mrg_b], [XTs[q]])
                tokch = TOKCH if need_ctx else LAT_TOKCH

                def pre_out(g):
                    i = g % 2
                    dma("sp", dpool[i], xr[i].t[:], xT[g * 128:(g + 1) * 128, :], [xT_b], [xr[i]])

                def evac_out(g, banks):
                    i = g % 2
                    for (bk, t0, tn) in banks:
                        m = 1 if t0 < 256 else 0
                        op("dve", [bk, modT, xr[i]], [xr[i]], lambda e, bk=bk, t0=t0, tn=tn, m=m, i=i: e.scalar_tensor_tensor(
                            out=xr[i].t[:, t0:t0 + tn], in0=bk.t[:, 0:tn], scalar=modT.t[:, 64 + g, m:m + 1],
                            in1=xr[i].t[:, t0:t0 + tn], op0=ALU.mult, op1=ALU.add))
                    dma("sp", dpool[2 + i], xT[g * 128:(g + 1) * 128, :], xr[i].t[:], [xr[i]], [xT_b])

                linear(wp, w_out[l], KC, lambda k, t0, tn: XTs[k // 4].t[:, k % 4, t0:t0 + tn], lambda k: [XTs[k // 4]], tokch, evac_out, pre=pre_out)
                kb.barrier()

        if nlayers == DEPTH and not any(k.startswith("stop") for k in debug):
            with ExitStack() as ph:
                xch = [kb.sb(ph, "fx%d" % i, [128, KC, 256], F32) for i in range(2)]
                sq = [kb.sb(ph, "fsq%d" % i, [128, 8, 128], F32) for i in range(2)]
                rstd = [kb.sb(ph, "frs%d" % i, [128, 128], F32) for i in range(2)]
                yn = [kb.sb(ph, "fy%d" % i, [128, 4, 128], F32) for i in range(2)]
                ost = [kb.sb(ph, "fo%d" % i, [128, D], F32) for i in range(2)]
                SQ, YN = Rot(sq), Rot(yn)
                bks = {}

                def f_dma(tt):
                    if tt % 2 != 0:
                        return
                    i = (tt // 2) % 2
                    t0 = 256 + tt * 128
                    dma("sp", dpool[i], xch[i].t[:], xT_v[:, :, t0:t0 + 256], [xT_b], [xch[i]])

                def f_stageA(tt):
                    i = tt % 2
                    xb = xch[(tt // 2) % 2]
                    o_ = (tt % 2) * 128
                    bk = PS.next()
                    bks[tt] = bk
                    for c8 in range(4):
                        s = SQ.next()
                        op("dve", [xb], [s], lambda e, s=s, c8=c8: e.tensor_tensor(
                            out=s.t[:], in0=xb.t[:, c8 * 8:(c8 + 1) * 8, o_:o_ + 128], in1=xb.t[:, c8 * 8:(c8 + 1) * 8, o_:o_ + 128], op=ALU.mult))
                        for j in range(8):
                            op("pe", [s, cf], [bk], lambda e, s=s, j=j, bk=bk, c8=c8: e.matmul(
                                bk.t[:, 0:128], lhsT=onesf, rhs=s.t[:, j, :], start=(c8 == 0 and j == 0), stop=(c8 == 3 and j == 7)))

                def f_stageC(tt):
                    rstd_from(bks[tt], 128, rstd[tt % 2], 1.0 / D, 0)

                def f_stageB(tt):
                    i = tt % 2
                    xb = xch[(tt // 2) % 2]
                    o_ = (tt % 2) * 128
                    r = rstd[i]
                    for c4 in range(8):
                        y = YN.next()
                        for j in range(4):
                            c = c4 * 4 + j
                            op("dve", [xb, r, pp], [y], lambda e, c=c, j=j, y=y, r=r: e.scalar_tensor_tensor(
                                out=y.t[:, j, :], in0=xb.t[:, c, o_:o_ + 128], scalar=pp.t[:, PP_FNW + c:PP_FNW + c + 1], in1=r.t[:], op0=ALU.mult, op1=ALU.mult))
                        bk2 = PS.next()
                        for j in range(4):
                            op("pe", [y, cf], [bk2], lambda e, j=j, y=y, bk2=bk2: e.transpose(
                                out=bk2.t[:, j * 128:(j + 1) * 128], in_=y.t[:, j, :], identity=ident))
                        op("act", [bk2], [ost[i]], lambda e, bk2=bk2, c4=c4, i=i: e.copy(out=ost[i].t[:, c4 * 512:(c4 + 1) * 512], in_=bk2.t[:, 0:512]))
                    dma("sp", dpool[2 + i], out_d[tt * 128:(tt + 1) * 128, :], ost[i].t[:], [ost[i]], [])

                NT_ = SEQ // 128
                f_dma(0)
                f_stageA(0)
                f_stageC(0)
                for tt in range(NT_):
                    if tt + 1 < NT_:
                        f_dma(tt + 1)
                    f_stageB(tt)
                    if tt + 1 < NT_:
                        f_stageA(tt + 1)
                        f_stageC(tt + 1)
                kb.barrier()

        kb.barrier()
    return nc


def _fm(v, n):
    return np.ascontiguousarray(v.reshape(n, 128).T)


def host_consts():
    cf = np.zeros((128, 384), np.float32)
    cf[:, 0:128] = np.eye(128, dtype=np.float32)
    R = np.zeros((128, 128), np.float32)
    for base in range(0, 128, 32):
        for i in range(16):
            R[base + i, base + i + 16] = -1.0
            R[base + i + 16, base + i] = 1.0
    cf[:, 128:256] = R.T
    cf[:, 256:384] = 1.0
    t = np.arange(SEQ)
    rows = (t // 64).astype(np.float32)
    cols = (t % 64).astype(np.float32)
    freqs = (10000.0 ** (-np.arange(16, dtype=np.float32) / 16)).astype(np.float32)
    rope = np.zeros((128, 2, SEQ), np.float32)
    for p in range(128):
        d = p % 64
        part = d // 32
        i = (d % 32) % 16
        pos = rows if part == 0 else cols
        ang = (pos * freqs[i]).astype(np.float32)
        rope[p, 0] = np.cos(ang)
        rope[p, 1] = np.sin(ang)
    return cf, rope


def host_pp(inputs, b):
    pp = np.zeros((128, PP_N), np.float32)
    cc = np.stack([_fm(inputs["c"][b], 32), _fm(inputs["c_ctx"], 32)], axis=-1)
    pp[:, PP_CC:PP_CC + 64] = cc.reshape(128, 64)
    for l in range(DEPTH):
        pp[:, PP_ADAB + l * 96:PP_ADAB + (l + 1) * 96] = _fm(inputs["ada_b"][l], 96)
        pp[:, PP_NORMW + l * 32:PP_NORMW + (l + 1) * 32] = _fm(inputs["norm_w"][l], 32)
        pp[:, PP_LAMQ + l * 256:PP_LAMQ + (l + 1) * 256] = inputs["lambda_qk"][l].reshape(1, 256)
        pp[:, PP_SUBLN + l] = inputs["subln_w"][l]
        for j in range(4):
            pp[:, PP_CONVW + l * 32 + j:PP_CONVW + (l + 1) * 32:4] = _fm(inputs["conv_w"][l, j], 8)
        pp[:, PP_CONVB + l * 8:PP_CONVB + (l + 1) * 8] = _fm(inputs["conv_b"][l], 8)
        for d in range(2):
            o = (l * 2 + d) * 8
            pp[:, PP_RGBA + o:PP_RGBA + o + 8] = _fm(inputs["rg_ba"][l, d], 8)
            pp[:, PP_RGBX + o:PP_RGBX + o + 8] = _fm(inputs["rg_bx"][l, d], 8)
            pp[:, PP_RGLAM + o:PP_RGLAM + o + 8] = _fm(inputs["rg_lambda"][l, d], 8)
    pp[:, PP_FNW:PP_FNW + 32] = _fm(inputs["final_norm_w"], 32)
    return pp


def host_rgw(inputs):
    rgw = np.zeros((DEPTH, 8, 128, 4, 128), np.float32)
    for l in range(DEPTH):
        for d in range(2):
            for ax, key in enumerate(("rg_wa", "rg_wx")):
                w = inputs[key][l, d]
                for g in range(8):
                    rgw[l, g, 0:64, d * 2 + ax, 0:64] = w[2 * g]
                    rgw[l, g, 64:128, d * 2 + ax, 64:128] = w[2 * g + 1]
    return rgw


def host_bias(inputs):
    out = np.full((DEPTH, 12, 128, 25, 128), -1e30, np.float32)
    kk = np.arange(128)
    qq = np.arange(128)
    for cls, r in enumerate((0, 2, 28, 30, 4)):
        jb = min(max((r - 4) // 2, 0), 11)
        q_row = r + qq // 64
        q_col = qq % 64
        r0 = np.clip(q_row - 4, 0, 24)
        c0 = np.clip(q_col - 8, 0, 48)
        for j in range(5):
            k_row = 2 * (jb + j) + kk // 64
            k_col = kk % 64
            valid = ((k_row[:, None] >= r0[None, :]) & (k_row[:, None] < r0[None, :] + 8) &
                     (k_col[:, None] >= c0[None, :]) & (k_col[:, None] < c0[None, :] + 16))
            dr = np.clip(k_row[:, None] - q_row[None, :] + 7, 0, 14)
            dc = np.clip(k_col[:, None] - q_col[None, :] + 15, 0, 30)
            for l in range(DEPTH):
                vals = inputs["rpb"][l][:, dr, dc]
                out[l, :, :, cls * 5 + j, :] = np.where(valid[None], vals, np.float32(-1e30))
    return out


def make_in_maps(inputs, cores):
    cf, rope = host_consts()
    rgw = host_rgw(inputs)
    biasb = host_bias(inputs)
    maps = []
    for b in cores:
        maps.append({
            "x": np.ascontiguousarray(inputs["x"][b]), "ctx": np.ascontiguousarray(inputs["ctx"][b]),
            "pp": host_pp(inputs, b), "cf": cf, "rope": rope,
            "ada_w": inputs["ada_w"], "w_in": inputs["w_in"], "w_out": inputs["w_out"],
            "rgw": rgw, "biasb": biasb,
        })
    return maps


def kernel(**inputs):
    inputs = {k: np.asarray(v) for k, v in inputs.items()}
    nc = build_program()
    maps = make_in_maps(inputs, list(range(8)))
    res = run_bass_kernel_spmd(nc, maps, core_ids=list(range(8)))
    return np.stack([r["out"] for r in res.results], axis=0).astype(np.float32)
```

```python
import math
from contextlib import ExitStack

import numpy as np
import ml_dtypes
import concourse.bass as bass
import concourse.mybir as mybir
from concourse.bass_utils import run_bass_kernel_spmd

F32 = mybir.dt.float32
BF16 = mybir.dt.bfloat16
ALU = mybir.AluOpType
AF = mybir.ActivationFunctionType
AX = mybir.AxisListType

D = 4096
SEQ = 2048
NCTX = 256
T = SEQ + NCTX
KC = D // 128
INW = 14336
DEPTH = 2
NG_IN = INW // 128
TOKCH = [(0, 256), (256, 512), (768, 512), (1280, 512), (1792, 512)]
LAT_TOKCH = TOKCH[1:]
NORM_EPS = 1e-6
SUBLN_EPS = 1e-5

PP_CC = 0
PP_ADAB = PP_CC + 64
PP_NORMW = PP_ADAB + 192
PP_FNW = PP_NORMW + 64
PP_LAMQ = PP_FNW + 32
PP_SUBLN = PP_LAMQ + 512
PP_CONVW = PP_SUBLN + 2
PP_CONVB = PP_CONVW + 64
PP_RGBA = PP_CONVB + 16
PP_RGBX = PP_RGBA + 32
PP_RGLAM = PP_RGBX + 32
PP_N = PP_RGLAM + 32


class Buf:
    __slots__ = ("t", "w", "r")

    def __init__(self, t):
        self.t = t
        self.w = None
        self.r = {}

    def __getitem__(self, k):
        return self.t[k]


class DSem:
    __slots__ = ("sem", "cnt")

    def __init__(self, sem):
        self.sem = sem
        self.cnt = 0


class KB:
    def __init__(self, nc, es):
        self.nc = nc
        self.es = es
        self.eng = {"pe": nc.tensor, "dve": nc.vector, "act": nc.scalar, "pool": nc.gpsimd, "sp": nc.sync}
        self.esem = {k: es.enter_context(nc.semaphore("e_" + k)) for k in ("pe", "dve", "act", "pool")}
        self.ecnt = {k: 0 for k in self.esem}
        self.waited = {k: {} for k in self.eng}
        self.dsems = []
        self.nsem = 0

    def dsem(self):
        s = DSem(self.es.enter_context(self.nc.semaphore("d%d" % self.nsem)))
        self.nsem += 1
        self.dsems.append(s)
        return s

    def sb(self, stack, name, shape, dt):
        self.nsb = getattr(self, "nsb", 0) + 1
        return Buf(stack.enter_context(self.nc.sbuf_tensor("s%d_%s" % (self.nsb, name), list(shape), dt)))

    def _deps(self, e, reads, writes):
        w = self.waited[e]
        need = {}
        own = self.esem.get(e)
        for b in reads:
            if b.w is not None:
                s, v = b.w
                if need.get(s, 0) < v:
                    need[s] = v
        for b in writes:
            if b.w is not None:
                s, v = b.w
                if need.get(s, 0) < v:
                    need[s] = v
            for s, v in b.r.items():
                if need.get(s, 0) < v:
                    need[s] = v
        for s, v in need.items():
            if e == "pe" and s is own:
                continue
            if w.get(s, 0) < v:
                self.eng[e].wait_ge(s, v)
                w[s] = v

    def _mark(self, tok, reads, writes):
        s, v = tok
        for b in reads:
            if b.r.get(s, 0) < v:
                b.r[s] = v
        for b in writes:
            b.w = tok
            b.r = {}

    def op(self, e, reads, writes, fn):
        self._deps(e, reads, writes)
        ins = fn(self.eng[e])
        self.ecnt[e] += 1
        tok = (self.esem[e], self.ecnt[e])
        ins.then_inc(tok[0], 1)
        self._mark(tok, reads, writes)
        return tok

    def dma(self, q, ds, out_ap, in_ap, reads, writes):
        self._deps(q, reads, writes)
        ins = self.eng[q].dma_start(out=out_ap, in_=in_ap)
        ds.cnt += 16
        ins.then_inc(ds.sem, 16)
        self._mark((ds.sem, ds.cnt), reads, writes)

    def dma_group(self, q, ds, items):
        for (o, i_, r, w) in items:
            self._deps(q, r, w)
        for (o, i_, r, w) in items:
            ins = self.eng[q].dma_start(out=o, in_=i_)
            ds.cnt += 16
            ins.then_inc(ds.sem, 16)
        tok = (ds.sem, ds.cnt)
        for (o, i_, r, w) in items:
            self._mark(tok, r, w)

    def barrier(self):
        for e in self.eng:
            w = self.waited[e]
            for k, s in self.esem.items():
                v = self.ecnt[k]
                if k == e and e == "pe":
                    continue
                if v > 0 and w.get(s, 0) < v:
                    self.eng[e].wait_ge(s, v)
                    w[s] = v
            for ds in self.dsems:
                if ds.cnt > 0 and w.get(ds.sem, 0) < ds.cnt:
                    self.eng[e].wait_ge(ds.sem, ds.cnt)
                    w[ds.sem] = ds.cnt


class Rot:
    def __init__(self, items):
        self.items = items
        self.i = 0

    def next(self):
        it = self.items[self.i % len(self.items)]
        self.i += 1
        return it


def build_program(debug=None, nlayers=DEPTH):
    debug = debug or set()
    nc = bass.Bass("TRN2", target_bir_lowering=False)

    def din(name, shape, dt=F32):
        return nc.dram_tensor(name, list(shape), dt, kind="ExternalInput").ap()

    def dscr(name, shape, dt=F32):
        kind = "ExternalOutput" if name in debug else "Internal"
        return nc.dram_tensor(name, list(shape), dt, kind=kind).ap()

    x_in = din("x", [SEQ, D])
    ctx_in = din("ctx", [NCTX, D])
    pp_in = din("pp", [128, PP_N])
    cf_in = din("cf", [128, 384])
    rope_in = din("rope", [128, 2, SEQ])
    ada_w = din("ada_w", [DEPTH, D, 3 * D])
    w_in = din("w_in", [DEPTH, D, INW])
    w_out = din("w_out", [DEPTH, D, D])
    rgw_in = din("rgw", [DEPTH, 8, 128, 4, 128])
    bias_in = din("biasb", [DEPTH, 12, 128, 25, 128])
    out_d = nc.dram_tensor("out", [SEQ, D], F32, kind="ExternalOutput").ap()

    xT = dscr("xT", [D, T])
    proj = dscr("proj", [13312, T], BF16)
    xcT = dscr("xcT", [1024, T])
    mrg = dscr("mrg", [D, T], BF16)
    xT_b, proj_b, xcT_b, mrg_b = Buf(xT), Buf(proj), Buf(xcT), Buf(mrg)
    xT_v = xT.rearrange("(c p) t -> p c t", p=128)
    mrg_v = mrg.rearrange("(c p) t -> p c t", p=128)

    es = ExitStack()
    with es:
        kb = KB(nc, es)
        op, dma = kb.op, kb.dma

        pp = kb.sb(es, "pp", [128, PP_N], F32)
        cf = kb.sb(es, "cf", [128, 384], F32)
        identb = kb.sb(es, "identb", [128, 128], BF16)
        rotTb = kb.sb(es, "rotTb", [128, 128], BF16)
        onesb = kb.sb(es, "onesb", [128, 128], BF16)
        modTs = [kb.sb(es, "modT%d" % i, [128, 96, 2], F32) for i in range(DEPTH)]
        Amods = [kb.sb(es, "Amod%d" % i, [128, KC, 2], F32) for i in range(DEPTH)]
        scb = kb.sb(es, "scb", [128, KC, 2], BF16)
        epsc = kb.sb(es, "epsc", [128, 4], F32)
        psb = [Buf(es.enter_context(nc.psum_tensor("ps%d" % i, [128, 512], F32))) for i in range(8)]
        PS = Rot(psb)
        d_misc = kb.dsem()
        NW = 3
        wsems = [kb.dsem() for _ in range(8)]
        dpool = [kb.dsem() for _ in range(12)]

        ident = cf.t[:, 0:128]
        onesf = cf.t[:, 256:384]
        kb.dma_group("sp", d_misc, [(pp.t[:], pp_in[:, :], [], [pp]), (cf.t[:], cf_in[:, :], [], [cf])])
        op("dve", [cf], [identb], lambda e: e.tensor_copy(out=identb.t[:], in_=cf.t[:, 0:128]))
        op("dve", [cf], [rotTb], lambda e: e.tensor_copy(out=rotTb.t[:], in_=cf.t[:, 128:256]))
        op("dve", [cf], [onesb], lambda e: e.tensor_copy(out=onesb.t[:], in_=cf.t[:, 256:384]))
        op("dve", [], [epsc], lambda e: e.memset(epsc.t[:, 0:1], NORM_EPS))
        op("dve", [epsc], [epsc], lambda e: e.memset(epsc.t[:, 1:2], SUBLN_EPS))
        op("dve", [epsc], [epsc], lambda e: e.memset(epsc.t[:, 2:3], 1.0))
        op("dve", [epsc], [epsc], lambda e: e.memset(epsc.t[:, 3:4], 0.0))

        def rstd_from(bk, n, dst, inv_n, eps_col):
            op("act", [bk, epsc], [dst], lambda e: e.activation(out=dst.t[:, 0:n], in_=bk.t[:, 0:n], func=AF.Ln,
                                                               scale=inv_n, bias=epsc.t[:, eps_col:eps_col + 1]))
            op("act", [dst], [dst], lambda e: e.activation(out=dst.t[:, 0:n], in_=dst.t[:, 0:n], func=AF.Exp, scale=-0.5))

        def mk_wpool(stack, n=NW):
            slots = [kb.sb(stack, "w%d" % i, [128, KC, 128], BF16) for i in range(n)]
            return {"slots": slots, "i": 0}

        def load_w(wp, wd, g):
            i = wp["i"] % len(wp["slots"])
            wp["i"] += 1
            src = wd.rearrange("(c p) n -> p c n", p=128)
            sl = wp["slots"][i]
            for q in range(4):
                dma("pool", wsems[i], sl.t[:, q * 8:(q + 1) * 8, :], src[:, q * 8:(q + 1) * 8, g * 128:(g + 1) * 128], [], [sl])
            return sl

        def gen_items(wp, items):
            pend = []
            nxt = 0
            n = len(items)
            prefetch = len(wp["slots"])
            for idx in range(n):
                while nxt < n and nxt <= idx + prefetch - 1:
                    pend.append(load_w(wp, items[nxt]["wd"], items[nxt]["g"]))
                    nxt += 1
                it = items[idx]
                if it.get("pre") is not None:
                    it["pre"](it["g"])
                ws = pend.pop(0)
                banks = []
                for (t0, tn) in it["tokch"]:
                    bk = PS.next()
                    for k in range(KC):
                        rb = it["rhs_bufs"](k) if callable(it["rhs_bufs"]) else it["rhs_bufs"]
                        if it.get("swap"):
                            op("pe", [ws] + rb, [bk], lambda e, k=k, bk=bk, ws=ws, it=it: e.matmul(
                                bk.t[0:2, 0:128], lhsT=it["rhs_fn"](k, 0, 2), rhs=ws.t[:, k, :], start=(k == 0), stop=(k == KC - 1)))
                            continue
                        op("pe", [ws] + rb, [bk], lambda e, k=k, bk=bk, t0=t0, tn=tn, ws=ws, it=it: e.matmul(
                            bk.t[:, 0:tn], lhsT=ws.t[:, k, :], rhs=it["rhs_fn"](k, t0, tn), start=(k == 0), stop=(k == KC - 1)))
                    banks.append((bk, t0, tn))
                it["evac"](it["g"], banks)
                yield

        def linear(wp, wd, ngroups, rhs_fn, rhs_bufs, tokch, evac, pre=None):
            for _ in gen_items(wp, [dict(wd=wd, g=g, rhs_fn=rhs_fn, rhs_bufs=rhs_bufs, tokch=tokch, evac=evac, pre=pre) for g in range(ngroups)]):
                pass

        def ada_items(l):
            def evac_mod(g, banks, l=l):
                bk, _, _ = banks[0]
                op("dve", [bk, pp], [modTs[l]], lambda e: e.tensor_scalar(
                    out=modTs[l].t[:, g, :], in0=bk.t[:, 0:2], scalar1=pp.t[:, PP_ADAB + l * 96 + g:PP_ADAB + l * 96 + g + 1],
                    scalar2=None, op0=ALU.add))
            return [dict(wd=ada_w[l], g=g, rhs_fn=lambda k, t0, tn: scb.t[:, k, 0:2], rhs_bufs=[scb], tokch=[(0, 2)], evac=evac_mod, pre=None)
                    for g in range(96)]

        def gen_ada_wide(l, stack, nslots, ngw=24):
            slots = [kb.sb(stack, "ww%d" % i, [128, KC, 512], BF16) for i in range(nslots)]
            src = ada_w[l].rearrange("(c p) n -> p c n", p=128)
            evac_mod = ada_items(l)[0]["evac"]

            def load(gw):
                sl = slots[gw % nslots]
                for q in range(4):
                    dma("pool", wsems[gw % nslots], sl.t[:, q * 8:(q + 1) * 8, :], src[:, q * 8:(q + 1) * 8, gw * 512:(gw + 1) * 512], [], [sl])
                return sl
            pend = []
            nxt = 0
            for gw in range(ngw):
                while nxt < ngw and nxt <= gw + nslots - 1:
                    pend.append(load(nxt))
                    nxt += 1
                ws = pend.pop(0)
                for j in range(4):
                    bk = PS.next()
                    for k in range(KC):
                        op("pe", [ws, scb], [bk], lambda e, k=k, bk=bk, j=j, ws=ws: e.matmul(
                            bk.t[:, 0:2], lhsT=ws.t[:, k, j * 128:(j + 1) * 128], rhs=scb.t[:, k, 0:2], start=(k == 0), stop=(k == KC - 1)))
                    evac_mod(gw * 4 + j, [(bk, 0, 2)])
                    yield

        adarow = Rot([kb.sb(es, "adarow%d" % i, [2, 128], F32) for i in range(2)])

        def ada_items_swapped(l):
            def evac_sw(g, banks, l=l):
                bk, _, _ = banks[0]
                row = adarow.next()
                op("dve", [bk], [row], lambda e: e.tensor_copy(out=row.t[0:2, :], in_=bk.t[0:2, 0:128]))
                bk2 = PS.next()
                op("pe", [row, cf], [bk2], lambda e: e.transpose(out=bk2.t[:, 0:2], in_=row.t[0:2, :], identity=cf.t[0:2, 0:2]))
                op("dve", [bk2, pp], [modTs[l]], lambda e: e.tensor_scalar(
                    out=modTs[l].t[:, g, :], in0=bk2.t[:, 0:2], scalar1=pp.t[:, PP_ADAB + l * 96 + g:PP_ADAB + l * 96 + g + 1],
                    scalar2=None, op0=ALU.add))
            return [dict(wd=ada_w[l], g=g, rhs_fn=lambda k, t0, tn: scb.t[:, k, 0:2], rhs_bufs=[scb], tokch=[(0, 2)], evac=evac_sw, pre=None, swap=True)
                    for g in range(96)]

        def ada_finish(l):
            op("dve", [modTs[l], pp], [Amods[l]], lambda e: e.scalar_tensor_tensor(
                out=Amods[l].t[:], in0=modTs[l].t[:, 32:64, :], scalar=1.0,
                in1=pp.t[:, PP_NORMW + l * 32:PP_NORMW + (l + 1) * 32].unsqueeze(2).to_broadcast([128, KC, 2]),
                op0=ALU.add, op1=ALU.mult))

        op("act", [pp], [scb], lambda e: e.activation(out=scb.t[:], in_=pp.t[:, PP_CC:PP_CC + 64].rearrange("p (c t) -> p c t", t=2), func=AF.Silu))

        with ExitStack() as ph:
            xin = [kb.sb(ph, "xin%d" % i, [128, D], F32) for i in range(2)]
            xst = [kb.sb(ph, "xst%d" % i, [128, KC, 256], F32) for i in range(2)]
            ada_gen = gen_ada_wide(0, ph, 3, 24)
            for tt in range(T // 128):
                i = tt % 2
                src = ctx_in[tt * 128:(tt + 1) * 128, :] if tt < 2 else x_in[(tt - 2) * 128:(tt - 1) * 128, :]
                dma("sp", dpool[i], xin[i].t[:], src, [], [xin[i]])
                for c4 in range(KC // 4):
                    bk = PS.next()
                    for j in range(4):
                        c = c4 * 4 + j
                        op("pe", [xin[i], cf], [bk], lambda e, c=c, j=j, bk=bk, i=i: e.transpose(
                            out=bk.t[:, j * 128:(j + 1) * 128], in_=xin[i].t[:, c * 128:(c + 1) * 128], identity=ident))
                    xs = xst[(tt // 2) % 2]
                    o_ = (tt % 2) * 128
                    dst = xs.t[:, c4 * 4:(c4 + 1) * 4, o_:o_ + 128]
                    srcp = bk.t[:, :].rearrange("p (c t) -> p c t", c=4)
                    if c4 % 2 == 0:
                        op("act", [bk], [xs], lambda e, dst=dst, srcp=srcp: e.copy(out=dst, in_=srcp))
                    else:
                        op("dve", [bk], [xs], lambda e, dst=dst, srcp=srcp: e.tensor_copy(out=dst, in_=srcp))
                if tt % 2 == 1:
                    xs = xst[(tt // 2) % 2]
                    dma("sp", dpool[2 + (tt // 2) % 2], xT_v[:, :, (tt - 1) * 128:(tt + 1) * 128], xs.t[:], [xs], [xT_b])
                for _ in range(6):
                    next(ada_gen, None)
            for _ in ada_gen:
                pass
            ada_finish(0)
            kb.barrier()

        for l in range(nlayers):
            last = (l == DEPTH - 1)
            need_ctx = not last
            lam_init = 0.8 - 0.6 * math.exp(-0.3 * l)
            modT, Amod = modTs[l], Amods[l]
            if "modT" in debug:
                dbg = nc.dram_tensor("dbg_modT", [128, 192], F32, kind="ExternalOutput").ap()
                dma("sp", d_misc, dbg[:, :], modT.t[:].rearrange("p a b -> p (a b)"), [modT], [])
            if "stop_mod" in debug:
                break

            with ExitStack() as phx:
                XT = kb.sb(phx, "XT", [128, KC, T], BF16)
                with ExitStack() as ph:
                    xch = [kb.sb(ph, "xch%d" % i, [128, KC, 128], F32) for i in range(2)]
                    sq = [kb.sb(ph, "sq%d" % i, [128, 8, 128], F32) for i in range(2)]
                    rstd = [kb.sb(ph, "rstd%d" % i, [128, 128], F32) for i in range(2)]
                    tmp8 = [kb.sb(ph, "tmp8%d" % i, [128, 8, 128], F32) for i in range(2)]
                    SQ, TM = Rot(sq), Rot(tmp8)
                    bks = {}

                    def n_dma(tt):
                        i = tt % 2
                        dma("sp", dpool[i], xch[i].t[:], xT_v[:, :, tt * 128:(tt + 1) * 128], [xT_b], [xch[i]])

                    def n_stageA(tt):
                        i = tt % 2
                        bk = PS.next()
                        bks[tt] = bk
                        for c8 in range(4):
                            s = SQ.next()
                            op("dve", [xch[i]], [s], lambda e, s=s, c8=c8, i=i: e.tensor_tensor(
                                out=s.t[:], in0=xch[i].t[:, c8 * 8:(c8 + 1) * 8, :], in1=xch[i].t[:, c8 * 8:(c8 + 1) * 8, :], op=ALU.mult))
                            for j in range(8):
                                op("pe", [s, cf], [bk], lambda e, s=s, j=j, bk=bk, c8=c8: e.matmul(
                                    bk.t[:, 0:128], lhsT=onesf, rhs=s.t[:, j, :], start=(c8 == 0 and j == 0), stop=(c8 == 3 and j == 7)))

                    def n_stageC(tt):
                        rstd_from(bks[tt], 128, rstd[tt % 2], 1.0 / D, 0)

                    def n_stageB(tt):
                        i = tt % 2
                        m = 1 if tt < 2 else 0
                        r = rstd[i]
                        for c8 in range(4):
                            tm = TM.next()
                            op("dve", [xch[i], r], [tm], lambda e, r=r, tm=tm, i=i, c8=c8: e.tensor_tensor(
                                out=tm.t[:], in0=xch[i].t[:, c8 * 8:(c8 + 1) * 8, :], in1=r.t[:].unsqueeze(1).to_broadcast([128, 8, 128]), op=ALU.mult))
                            for j in range(8):
                                c = c8 * 8 + j
                                op("act", [tm, Amod, modT], [XT], lambda e, c=c, j=j, tm=tm, m=m, tt=tt: e.activation(
                                    out=XT.t[:, c, tt * 128:(tt + 1) * 128], in_=tm.t[:, j, :], func=AF.Identity,
                                    scale=Amod.t[:, c, m:m + 1], bias=modT.t[:, c, m:m + 1]))

                    NT_ = T // 128
                    n_dma(0)
                    n_stageA(0)
                    n_stageC(0)
                    for tt in range(NT_):
                        if tt + 1 < NT_:
                            n_dma(tt + 1)
                        n_stageB(tt)
                        if tt + 1 < NT_:
                            n_stageA(tt + 1)
                            n_stageC(tt + 1)
                    kb.barrier()
                if "XT" in debug:
                    dbg_xt = nc.dram_tensor("dbg_XT", [128, KC, T], BF16, kind="ExternalOutput").ap()
                    dma("sp", d_misc, dbg_xt[:, :, :], XT.t[:], [XT], [])
                if "stop_norm" in debug:
                    kb.barrier()
                    break

                with ExitStack() as ph:
                    wp = mk_wpool(ph, 4)
                    stg = [kb.sb(ph, "stg%d" % i, [128, T], BF16) for i in range(2)]
                    stgf = [kb.sb(ph, "stgf%d" % i, [128, T], F32) for i in range(1)]
                    st = {"i": 0}

                    def evac_in(g, banks):
                        fam = g // 12 if g < 96 else 8 + (g - 96) // 8
                        if fam == 8:
                            for (bk, t0, tn) in banks:
                                op("dve", [bk], [stgf[0]], lambda e, bk=bk, t0=t0, tn=tn: e.tensor_copy(out=stgf[0].t[:, t0:t0 + tn], in_=bk.t[:, 0:tn]))
                            r0 = (g - 96) * 128
                            dma("sp", dpool[2], xcT[r0:r0 + 128, :], stgf[0].t[:], [stgf[0]], [xcT_b])
                            return
                        i = st["i"] % 2
                        st["i"] += 1
                        gate = fam in (3, 7, 9)
                        for (bk, t0, tn) in banks:
                            if gate:
                                op("act", [bk], [stg[i]], lambda e, bk=bk, t0=t0, tn=tn, i=i: e.activation(out=stg[i].t[:, t0:t0 + tn], in_=bk.t[:, 0:tn], func=AF.Silu))
                            else:
                                op("dve", [bk], [stg[i]], lambda e, bk=bk, t0=t0, tn=tn, i=i: e.tensor_copy(out=stg[i].t[:, t0:t0 + tn], in_=bk.t[:, 0:tn]))
                        r0 = g * 128 if g < 96 else (g - 8) * 128
                        dma("sp", dpool[i], proj[r0:r0 + 128, :], stg[i].t[:], [stg[i]], [proj_b])

                    def fam_of(g):
                        return g // 12 if g < 96 else 8 + (g - 96) // 8
                    in_items = [dict(wd=w_in[l], g=g, rhs_fn=lambda k, t0, tn: XT.t[:, k, t0:t0 + tn], rhs_bufs=[XT],
                                     tokch=(LAT_TOKCH if (last and fam_of(g) in (0, 3, 4, 7, 9)) else TOKCH), evac=evac_in, pre=None)
                                for g in range(NG_IN)]
                    ad = []
                    if l + 1 < nlayers:
                        ad += ada_items_swapped(l + 1)[0:64]
                    if l > 0:
                        ad += ada_items_swapped(l)[64:96]
                    items = []
                    for idx in range(NG_IN):
                        items.append(in_items[idx])
                        if idx < len(ad):
                            items.append(ad[idx])
                    items += ad[NG_IN:]
                    for _ in gen_items(wp, items):
                        pass
                    if l + 1 < nlayers:
                        ada_finish(l + 1)
                    kb.barrier()
            if "stop_proj" in debug:
                break

            with ExitStack() as ph:
                rope = kb.sb(ph, "rope", [128, 2, SEQ], F32)
                dma("sp", d_misc, rope.t[:], rope_in[:, :, :], [], [rope])
                lt = kb.sb(ph, "lamtmp", [128, 136], F32)
                nlam = kb.sb(ph, "nlam", [128, 1], F32)
                sw = kb.sb(ph, "sw", [128, 1], F32)
                lq = pp.t[:, PP_LAMQ + l * 256:PP_LAMQ + (l + 1) * 256]
                op("dve", [pp], [lt], lambda e: e.tensor_tensor(out=lt.t[:, 0:64], in0=lq[:, 0:64], in1=lq[:, 64:128], op=ALU.mult))
                op("dve", [pp, lt], [lt], lambda e: e.tensor_tensor(out=lt.t[:, 64:128], in0=lq[:, 128:192], in1=lq[:, 192:256], op=ALU.mult))
                op("dve", [lt], [lt], lambda e: e.reduce_sum(out=lt.t[:, 128:129], in_=lt.t[:, 0:64], axis=AX.X))
                op("dve", [lt], [lt], lambda e: e.reduce_sum(out=lt.t[:, 129:130], in_=lt.t[:, 64:128], axis=AX.X))
                op("act", [lt], [lt], lambda e: e.activation(out=lt.t[:, 130:132], in_=lt.t[:, 128:130], func=AF.Exp))
                op("dve", [lt], [lt], lambda e: e.tensor_tensor(out=lt.t[:, 132:133], in0=lt.t[:, 131:132], in1=lt.t[:, 130:131], op=ALU.subtract))
                op("dve", [lt], [nlam], lambda e: e.tensor_scalar(out=nlam.t[:], in0=lt.t[:, 132:133], scalar1=-lam_init, scalar2=None, op0=ALU.add))
                op("dve", [pp], [sw], lambda e: e.tensor_scalar(out=sw.t[:], in0=pp.t[:, PP_SUBLN + l:PP_SUBLN + l + 1], scalar1=1.0 - lam_init, scalar2=None, op0=ALU.mult))

                ld = [[kb.sb(ph, "ald%d_%d" % (i, j), [128, T], BF16) for j in range(4)] for i in range(2)]
                qrs = [kb.sb(ph, "qr%d" % i, [128, T], BF16) for i in range(2)]
                krs = [kb.sb(ph, "kr%d" % i, [128, T], BF16) for i in range(2)]
                vtoks = [kb.sb(ph, "vtok%d" % i, [128, 18, 128], BF16) for i in range(2)]
                mst = [kb.sb(ph, "amst%d" % i, [128, T], BF16) for i in range(2)]
                pts = [kb.sb(ph, "pt%d" % i, [128, 2, 2, 256], BF16) for i in range(3)]
                f = [kb.sb(ph, "af%d" % i, [128, 512], F32) for i in range(8)]
                fpost = [[kb.sb(ph, "afp%d_%d" % (i, j), [128, 256], F32) for j in range(2)] for i in range(2)]
                SC = Rot([(psb[4], psb[5]), (psb[6], psb[7])])
                OB = Rot([psb[0], psb[1]])
                ZB = psb[2]
                AUX = psb[3]
                PTS = Rot(pts)
                ZS = Rot([kb.sb(ph, "azs%d" % i, [128, 512], BF16) for i in range(3)])
                qscale = 64 ** -0.5

                def a_load(h):
                    i = h % 2
                    kb.dma_group("sp", dpool[i], [(ld[i][j].t[:], proj[(j * 12 + h) * 128:(j * 12 + h + 1) * 128, :], [proj_b], [ld[i][j]]) for j in range(4)])

                def gen_prologue(h):
                    qT, kT, vT, gT = ld[h % 2]
                    qr, kr, vtok = qrs[h % 2], krs[h % 2], vtoks[h % 2]
                    for (src, dst) in ((qT, qr), (kT, kr)):
                        op("dve", [src], [dst], lambda e, src=src, dst=dst: e.tensor_copy(out=dst.t[:, 0:256], in_=src.t[:, 0:256]))
                        for ci, (t0, tn) in enumerate(LAT_TOKCH):
                            bk = AUX
                            op("pe", [src, rotTb], [bk], lambda e, bk=bk, src=src, t0=t0: e.matmul(
                                bk.t[:, 0:512], lhsT=rotTb.t[:], rhs=src.t[:, t0:t0 + 512], start=True, stop=True))
                            op("dve", [src, rope], [f[6]], lambda e, src=src, t0=t0: e.tensor_tensor(
                                out=f[6].t[:], in0=src.t[:, t0:t0 + 512], in1=rope.t[:, 0, t0 - 256:t0 + 256], op=ALU.mult))
                            op("dve", [bk, rope], [f[7]], lambda e, bk=bk, t0=t0: e.tensor_tensor(
                                out=f[7].t[:], in0=bk.t[:, 0:512], in1=rope.t[:, 1, t0 - 256:t0 + 256], op=ALU.mult))
                            op("dve", [f[6], f[7]], [dst], lambda e, dst=dst, t0=t0: e.tensor_tensor(
                                out=dst.t[:, t0:t0 + 512], in0=f[6].t[:], in1=f[7].t[:], op=ALU.add))
                            yield
                    for k0 in range(0, 18, 8):
                        nk = min(8, 18 - k0)
                        bk = AUX
                        bkb = bk.t[:, :].bitcast(BF16)
                        for j in range(nk):
                            op("pe", [vT, identb], [bk], lambda e, j=j, k0=k0, bkb=bkb: e.transpose(
                                out=bkb[:, j * 128:(j + 1) * 128], in_=vT.t[:, (k0 + j) * 128:(k0 + j + 1) * 128], identity=identb.t[:]))
                        op("dve", [bk], [vtok], lambda e, bkb=bkb, k0=k0, nk=nk, vtok=vtok: e.tensor_copy(
                            out=vtok.t[:, k0:k0 + nk, :], in_=bkb[:, 0:nk * 128].rearrange("p (k d) -> p k d", k=nk)))
                        yield

                a_load(0)
                for _ in gen_prologue(0):
                    pass
                pending = []
                qranges = [(256 + 256 * i, 18) for i in range(8)]
                if need_ctx:
                    qranges.append((0, 2))
                tasks = [(h, qi, q0, nkc, p) for h in range(12) for qi, (q0, nkc) in enumerate(qranges) for p in range(nkc // 2)]
                it_ob = {}
                fcount = [0]

                def scores(task):
                    h, qi, q0, nkc, p = task
                    qr, kr = qrs[h % 2], krs[h % 2]
                    bA, bB = SC.next()
                    for slot in range(2):
                        kc = 2 * p + slot
                        for n, bkd in enumerate((bA, bB)):
                            op("pe", [kr, qr], [bkd], lambda e, n=n, kc=kc, bkd=bkd, slot=slot: e.matmul(
                                bkd.t[:, slot * 256:(slot + 1) * 256], lhsT=kr.t[n * 64:(n + 1) * 64, kc * 128:(kc + 1) * 128],
                                rhs=qr.t[n * 64:(n + 1) * 64, q0:q0 + 256], start=True, stop=True))
                    return (bA, bB)

                def drain_pending():
                    while pending:
                        pp2 = pending.pop(0)
                        pp2(0)
                        pp2(1)

                def finish_iteration(task):
                    h, qi, q0, nkc, p = task
                    Ob = it_ob[(h, qi)]
                    gT = ld[h % 2][3]
                    ms = mst[h % 2]
                    fo, fq = fpost[fcount[0] % 2]
                    fcount[0] += 1
                    op("dve", [ZB], [f[4]], lambda e: e.tensor_copy(out=f[4].t[:], in_=ZB.t[:, 0:512]))
                    op("dve", [f[4]], [f[0]], lambda e: e.reciprocal(out=f[0].t[:], in_=f[4].t[:]))
                    op("dve", [Ob, f[0]], [f[2]], lambda e: e.tensor_tensor(out=f[2].t[:], in0=Ob.t[:, 0:512], in1=f[0].t[:], op=ALU.mult))
                    op("dve", [f[2], nlam], [fo], lambda e: e.scalar_tensor_tensor(
                        out=fo.t[:], in0=f[2].t[:, 256:512], scalar=nlam.t[:, 0:1], in1=f[2].t[:, 0:256], op0=ALU.mult, op1=ALU.add))
                    op("dve", [fo], [fq], lambda e: e.tensor_tensor(out=fq.t[:], in0=fo.t[:], in1=fo.t[:], op=ALU.mult))
                    st = {}

                    def post2(stage):
                        if stage == 0:
                            if "done0" in st:
                                return
                            st["done0"] = True
                            op("pe", [fq, cf], [AUX], lambda e: e.matmul(AUX.t[:, 0:256], lhsT=onesf, rhs=fq.t[:], start=True, stop=True))
                            return
                        rstd_from(AUX, 256, f[3], 1.0 / 128, 1)
                        op("dve", [fo, f[3]], [fo], lambda e: e.tensor_tensor(out=fo.t[:], in0=fo.t[:], in1=f[3].t[:, 0:256], op=ALU.mult))
                        op("dve", [fo, sw, gT], [ms], lambda e: e.scalar_tensor_tensor(
                            out=ms.t[:, q0:q0 + 256], in0=fo.t[:], scalar=sw.t[:, 0:1], in1=gT.t[:, q0:q0 + 256], op0=ALU.mult, op1=ALU.mult))
                        if qi == len(qranges) - 1:
                            c0 = 0 if need_ctx else 256
                            dma("sp", dpool[2 + h % 2], mrg[h * 128:(h + 1) * 128, c0:T], ms.t[:, c0:T], [ms], [mrg_b])
                    pending.append(post2)

                def emit_z(zp):
                    zs_, task_ = zp
                    h_, qi_, q0_, nkc_, p_ = task_
                    last = (p_ == nkc_ // 2 - 1)
                    op("pe", [onesb, zs_], [ZB], lambda e: e.matmul(ZB.t[:, 0:512], lhsT=onesb.t[:], rhs=zs_.t[:], start=(p_ == 0), stop=last))
                    if last:
                        finish_iteration(task_)

                pro = iter(())
                zprev = None
                cur = scores(tasks[0])
                for k, task in enumerate(tasks):
                    h, qi, q0, nkc, p = task
                    npair = nkc // 2
                    vtok = vtoks[h % 2]
                    if p == 0:
                        it_ob[(h, qi)] = OB.next()
                    Ob = it_ob[(h, qi)]
                    if k + 1 < len(tasks):
                        if tasks[k + 1][0] != h:
                            for _ in pro:
                                pass
                        nxt_s = scores(tasks[k + 1])
                    else:
                        nxt_s = None
                    pt = PTS.next()
                    for n in range(2):
                        op("act", [cur[n]], [pt], lambda e, n=n, cur=cur, pt=pt: e.activation(
                            out=pt.t[:, :, n, :], in_=cur[n].t[:, 0:512].rearrange("p (s q) -> p s q", s=2), func=AF.Exp, scale=qscale))
                    zs = ZS.next()
                    op("pool", [pt], [zs], lambda e, pt=pt, zs=zs: e.tensor_tensor(
                        out=zs.t[:], in0=pt.t[:, 0, :, :].rearrange("p n q -> p (n q)"),
                        in1=pt.t[:, 1, :, :].rearrange("p n q -> p (n q)"), op=ALU.add))
                    for slot in range(2):
                        kc = 2 * p + slot
                        rhs = pt.t[:, slot, :, :].rearrange("p n q -> p (n q)")
                        op("pe", [vtok, pt], [Ob], lambda e, kc=kc, rhs=rhs: e.matmul(
                            Ob.t[:, 0:512], lhsT=vtok.t[:, kc, :], rhs=rhs, start=(kc == 0), stop=(kc == nkc - 1)))
                    if zprev is not None:
                        emit_z(zprev)
                    zprev = (zs, task)
                    cur = nxt_s
                    if p == 0 and qi == 1 and h + 1 < 12:
                        a_load(h + 1)
                    if p == 0 and qi == 2:
                        pro = gen_prologue(h + 1) if h + 1 < 12 else iter(())
                    if npair >= 9:
                        if p in (5, 6):
                            next(pro, None)
                        if p == 7 and pending:
                            pending[0](0)
                        if p == 8:
                            drain_pending()
                emit_z(zprev)
                drain_pending()
                kb.barrier()
            if "stop_A" in debug:
                break

            with ExitStack() as ph:
                ld = [[kb.sb(ph, "bld%d_%d" % (i, j), [128, T], BF16) for j in range(4)] for i in range(2)]
                bias = [kb.sb(ph, "bbias%d" % i, [128, 25, 128], F32) for i in range(2)]
                vtok = kb.sb(ph, "bvtok", [128, 18, 128], BF16)
                mst = [kb.sb(ph, "bmst%d" % i, [128, T], BF16) for i in range(2)]
                pts = [kb.sb(ph, "bpt%d" % i, [128, 896], BF16) for i in range(3)]
                tb = [kb.sb(ph, "btb%d" % i, [128, 640], F32) for i in range(2)]
                f = [kb.sb(ph, "bf%d" % i, [128, 512], F32) for i in range(3)]
                SC = Rot([(psb[4], psb[5]), (psb[6], psb[7])])
                OZ = Rot([(psb[0], psb[1]), (psb[2], psb[3])])
                PTS, TB = Rot(pts), Rot(tb)
                bscale = 128 ** -0.5

                def b_load(h):
                    i = h % 2
                    kb.dma_group("sp", dpool[i], [(ld[i][j].t[:], proj[((4 + j) * 12 + h) * 128:((4 + j) * 12 + h + 1) * 128, :], [proj_b], [ld[i][j]]) for j in range(4)]
                                 + [(bias[i].t[:], bias_in[l, h], [], [bias[i]])])

                b_load(0)
                for h in range(12):
                    if h + 1 < 12:
                        b_load(h + 1)
                    qT, kT, vT, gT = ld[h % 2]
                    bi = bias[h % 2]
                    for k0 in range(0, 18, 8):
                        nk = min(8, 18 - k0)
                        bk = SC.next()[0]
                        bkb = bk.t[:, :].bitcast(BF16)
                        for j in range(nk):
                            op("pe", [vT, identb], [bk], lambda e, j=j, k0=k0, bkb=bkb: e.transpose(
                                out=bkb[:, j * 128:(j + 1) * 128], in_=vT.t[:, (k0 + j) * 128:(k0 + j + 1) * 128], identity=identb.t[:]))
                        op("act", [bk], [vtok], lambda e, bkb=bkb, k0=k0, nk=nk: e.copy(
                            out=vtok.t[:, k0:k0 + nk, :], in_=bkb[:, 0:nk * 128].rearrange("p (k d) -> p k d", k=nk)))
                    ms = mst[h % 2]
                    groups = [[(256 + 128 * (g4 * 4 + j), 2 * (g4 * 4 + j)) for j in range(4)] for g4 in range(4)]
                    if need_ctx:
                        groups.append([(0, None), (128, None)])
                    def scores(blk):
                        qt0, r = blk
                        s0, s1 = SC.next()
                        qa = qT.t[:, qt0:qt0 + 128]
                        if r is not None:
                            jb = min(max((r - 4) // 2, 0), 11)
                            for j in range(5):
                                kt0 = (2 + jb + j) * 128
                                bkd = s0 if j < 4 else s1
                                col = (j % 4) * 128
                                op("pe", [kT, qT], [bkd], lambda e, kt0=kt0, bkd=bkd, col=col: e.matmul(
                                    bkd.t[:, col:col + 128], lhsT=kT.t[:, kt0:kt0 + 128], rhs=qa, start=True, stop=True))
                        for j in range(2):
                            col = 128 + j * 128
                            op("pe", [kT, qT], [s1], lambda e, j=j, col=col: e.matmul(
                                s1.t[:, col:col + 128], lhsT=kT.t[:, j * 128:(j + 1) * 128], rhs=qa, start=True, stop=True))
                        return (s0, s1)

                    blocks = [(gi, bi_, blk) for gi, grp in enumerate(groups) for bi_, blk in enumerate(grp)]
                    pend_post = []
                    cur = scores(blocks[0][2])
                    for idx, (gi, bi_, blk) in enumerate(blocks):
                        grp = groups[gi]
                        if bi_ == 0:
                            Ob, Zb = OZ.next()
                        nxt_s = scores(blocks[idx + 1][2]) if idx + 1 < len(blocks) else None
                        qt0, r = blk
                        s0, s1 = cur
                        pt = PTS.next()
                        chunks = []
                        if r is not None:
                            jb = min(max((r - 4) // 2, 0), 11)
                            cls = {0: 0, 2: 1, 28: 2, 30: 3}.get(r, 4)
                            t_ = TB.next()
                            op("dve", [s0, bi], [t_], lambda e, t_=t_, s0=s0, cls=cls: e.scalar_tensor_tensor(
                                out=t_.t[:, 0:512], in0=s0.t[:, 0:512], scalar=bscale,
                                in1=bi.t[:, cls * 5:cls * 5 + 4, :].rearrange("p a b -> p (a b)"), op0=ALU.mult, op1=ALU.add))
                            op("dve", [s1, bi, t_], [t_], lambda e, t_=t_, s1=s1, cls=cls: e.scalar_tensor_tensor(
                                out=t_.t[:, 512:640], in0=s1.t[:, 0:128], scalar=bscale,
                                in1=bi.t[:, cls * 5 + 4, :], op0=ALU.mult, op1=ALU.add))
                            op("act", [t_], [pt], lambda e, t_=t_, pt=pt: e.activation(out=pt.t[:, 0:640], in_=t_.t[:, 0:640], func=AF.Exp))
                            chunks += [(2 + jb + j, j * 128) for j in range(5)]
                        op("act", [s1, pt], [pt], lambda e, s1=s1, pt=pt: e.activation(out=pt.t[:, 640:896], in_=s1.t[:, 128:384], func=AF.Exp, scale=bscale))
                        chunks += [(0, 640), (1, 768)]
                        while pend_post:
                            pend_post.pop(0)()
                        oc_ = bi_ * 128
                        for ci, (kc, col) in enumerate(chunks):
                            op("pe", [vtok, pt], [Ob], lambda e, kc=kc, col=col, ci=ci, pt=pt, oc_=oc_, Ob=Ob: e.matmul(
                                Ob.t[:, oc_:oc_ + 128], lhsT=vtok.t[:, kc, :], rhs=pt.t[:, col:col + 128], start=(ci == 0), stop=(ci == len(chunks) - 1)))
                        for ci, (kc, col) in enumerate(chunks):
                            op("pe", [onesb, pt], [Zb], lambda e, col=col, ci=ci, pt=pt, oc_=oc_, Zb=Zb: e.matmul(
                                Zb.t[:, oc_:oc_ + 128], lhsT=onesb.t[:], rhs=pt.t[:, col:col + 128], start=(ci == 0), stop=(ci == len(chunks) - 1)))
                        cur = nxt_s
                        if bi_ == len(grp) - 1:
                            def post(Ob=Ob, Zb=Zb, qn=128 * len(grp), q0=grp[0][0]):
                                op("act", [Zb], [f[0]], lambda e: e.activation(out=f[0].t[:, 0:qn], in_=Zb.t[:, 0:qn], func=AF.Ln))
                                op("act", [f[0]], [f[0]], lambda e: e.activation(out=f[0].t[:, 0:qn], in_=f[0].t[:, 0:qn], func=AF.Exp, scale=-1.0))
                                op("dve", [Ob, f[0]], [f[1]], lambda e: e.tensor_tensor(out=f[1].t[:, 0:qn], in0=Ob.t[:, 0:qn], in1=f[0].t[:, 0:qn], op=ALU.mult))
                                op("dve", [f[1], gT], [ms], lambda e: e.tensor_tensor(out=ms.t[:, q0:q0 + qn], in0=f[1].t[:, 0:qn], in1=gT.t[:, q0:q0 + qn], op=ALU.mult))
                            pend_post.append(post)
                    while pend_post:
                        pend_post.pop(0)()
                    c0 = 0 if need_ctx else 256
                    dma("sp", dpool[2 + h % 2], mrg[(12 + h) * 128:(13 + h) * 128, c0:T], ms.t[:, c0:T], [ms], [mrg_b])
                kb.barrier()
            if "stop_B" in debug:
                break

            with ExitStack() as ph:
                xcl = [kb.sb(ph, "cx%d" % i, [128, T], F32) for i in range(2)]
                gcl = [kb.sb(ph, "cg%d" % i, [128, T], BF16) for i in range(2)]
                rw = [kb.sb(ph, "crw%d" % i, [128, 4, 128], F32) for i in range(2)]
                u_l = [kb.sb(ph, "cu%d" % i, [128, T], F32) for i in range(2)]
                rr_l = [kb.sb(ph, "cr%d" % i, [128, T], F32) for i in range(2)]
                ii_l = [kb.sb(ph, "ci%d" % i, [128, T], F32) for i in range(2)]
                aa_l = [kb.sb(ph, "ca%d" % i, [128, T], F32) for i in range(2)]
                a2_l = [kb.sb(ph, "ca2%d" % i, [128, T], F32) for i in range(2)]
                bb_l = [kb.sb(ph, "cb%d" % i, [128, T], F32) for i in range(2)]
                hh_l = [[kb.sb(ph, "ch%d_%d" % (j, i), [128, T], F32) for i in range(2)] for j in range(2)]
                mst = [kb.sb(ph, "cmst%d" % i, [128, T], BF16) for i in range(2)]
                nsp = kb.sb(ph, "nsp", [128, 3, 16], F32)
                lo = PP_RGLAM + l * 16
                op("act", [pp], [nsp], lambda e: e.activation(out=nsp.t[:, 0, :], in_=pp.t[:, lo:lo + 16], func=AF.Exp, scale=-1.0))
                op("act", [nsp, epsc], [nsp], lambda e: e.activation(out=nsp.t[:, 0, :], in_=nsp.t[:, 0, :], func=AF.Ln, bias=epsc.t[:, 2:3]))
                op("dve", [nsp], [nsp], lambda e: e.tensor_scalar(out=nsp.t[:, 1, :], in0=nsp.t[:, 0, :], scalar1=-8.0, scalar2=None, op0=ALU.mult))
                op("dve", [nsp], [nsp], lambda e: e.tensor_scalar(out=nsp.t[:, 2, :], in0=nsp.t[:, 0, :], scalar1=-16.0, scalar2=None, op0=ALU.mult))

                def c_load(g):
                    i = g % 2
                    kb.dma_group("sp", dpool[i], [(xcl[i].t[:], xcT[g * 128:(g + 1) * 128, :], [xcT_b], [xcl[i]]),
                                                  (gcl[i].t[:], proj[(96 + g) * 128:(97 + g) * 128, :], [proj_b], [gcl[i]]),
                                                  (rw[i].t[:], rgw_in[l, g], [], [rw[i]])])

                def c_conv(g):
                    xc_ = xcl[g % 2]
                    u = u_l[g % 2]
                    cw = lambda j: pp.t[:, PP_CONVW + l * 32 + g * 4 + j:PP_CONVW + l * 32 + g * 4 + j + 1]
                    cb = pp.t[:, PP_CONVB + l * 8 + g:PP_CONVB + l * 8 + g + 1]
                    for (s0, sn) in ((0, 256), (256, 2048)):
                        op("dve", [xc_, pp], [u], lambda e, s0=s0, sn=sn: e.tensor_scalar(
                            out=u.t[:, s0:s0 + sn], in0=xc_.t[:, s0:s0 + sn], scalar1=cw(2), scalar2=cb, op0=ALU.mult, op1=ALU.add))
                        for (j, do, so, n) in ((0, 2, 0, sn - 2), (1, 1, 0, sn - 1), (3, 0, 1, sn - 1)):
                            op("dve", [xc_, pp, u], [u], lambda e, s0=s0, j=j, do=do, so=so, n=n: e.scalar_tensor_tensor(
                                out=u.t[:, s0 + do:s0 + do + n], in0=xc_.t[:, s0 + so:s0 + so + n], scalar=cw(j),
                                in1=u.t[:, s0 + do:s0 + do + n], op0=ALU.mult, op1=ALU.add))

                def c_dir(g, d):
                    rw_ = rw[g % 2]
                    u, rr, ii, aa, a2, bb, hh = u_l[g % 2], rr_l[g % 2], ii_l[g % 2], aa_l[g % 2], a2_l[g % 2], bb_l[g % 2], hh_l[g % 2]
                    pidx = (l * 2 + d) * 8 + g
                    ba = pp.t[:, PP_RGBA + pidx:PP_RGBA + pidx + 1]
                    bx = pp.t[:, PP_RGBX + pidx:PP_RGBX + pidx + 1]
                    for (t0, tn) in TOKCH:
                        bA, bX = PS.next(), PS.next()
                        op("pe", [rw_, u], [bA], lambda e, bA=bA, t0=t0, tn=tn: e.matmul(
                            bA.t[:, 0:tn], lhsT=rw_.t[:, d * 2, :], rhs=u.t[:, t0:t0 + tn], start=True, stop=True))
                        op("pe", [rw_, u], [bX], lambda e, bX=bX, t0=t0, tn=tn: e.matmul(
                            bX.t[:, 0:tn], lhsT=rw_.t[:, d * 2 + 1, :], rhs=u.t[:, t0:t0 + tn], start=True, stop=True))
                        op("act", [bA, pp], [rr], lambda e, bA=bA, t0=t0, tn=tn: e.activation(
                            out=rr.t[:, t0:t0 + tn], in_=bA.t[:, 0:tn], func=AF.Sigmoid, bias=ba))
                        op("act", [bX, pp], [ii], lambda e, bX=bX, t0=t0, tn=tn: e.activation(
                            out=ii.t[:, t0:t0 + tn], in_=bX.t[:, 0:tn], func=AF.Sigmoid, bias=bx))
                    op("act", [rr, nsp], [aa], lambda e: e.activation(out=aa.t[:], in_=rr.t[:], func=AF.Exp, scale=nsp.t[:, 1, d * 8 + g:d * 8 + g + 1]))
                    op("act", [rr, nsp], [a2], lambda e: e.activation(out=a2.t[:], in_=rr.t[:], func=AF.Exp, scale=nsp.t[:, 2, d * 8 + g:d * 8 + g + 1]))
                    op("act", [a2, epsc], [a2], lambda e: e.activation(out=a2.t[:], in_=a2.t[:], func=AF.Sqrt, scale=-1.0, bias=epsc.t[:, 2:3]))
                    op("dve", [ii, u], [bb], lambda e: e.tensor_tensor(out=bb.t[:], in0=ii.t[:], in1=u.t[:], op=ALU.mult))
                    op("dve", [bb, a2], [bb], lambda e: e.tensor_tensor(out=bb.t[:], in0=bb.t[:], in1=a2.t[:], op=ALU.mult))
                    h_ = hh[d]
                    if d == 0:
                        op("dve", [aa, bb], [h_], lambda e: e.tensor_tensor_scan(
                            out=h_.t[:, 0:256], data0=aa.t[:, 0:256], data1=bb.t[:, 0:256], initial=0.0, op0=ALU.mult, op1=ALU.add))
                        op("dve", [aa, bb, h_], [h_], lambda e: e.tensor_tensor_scan(
                            out=h_.t[:, 256:T], data0=aa.t[:, 256:T], data1=bb.t[:, 256:T], initial=h_.t[:, 255:256], op0=ALU.mult, op1=ALU.add))
                    else:
                        op("dve", [aa, bb], [h_], lambda e: e.tensor_tensor_scan(
                            out=h_.t[:, 0:256][:, ::-1], data0=aa.t[:, 0:256][:, ::-1], data1=bb.t[:, 0:256][:, ::-1], initial=0.0, op0=ALU.mult, op1=ALU.add))
                        op("dve", [aa, bb, h_], [h_], lambda e: e.tensor_tensor_scan(
                            out=h_.t[:, 256:T][:, ::-1], data0=aa.t[:, 256:T][:, ::-1], data1=bb.t[:, 256:T][:, ::-1], initial=h_.t[:, 0:1], op0=ALU.mult, op1=ALU.add))

                def c_final(g):
                    hh = hh_l[g % 2]
                    gc_ = gcl[g % 2]
                    ms = mst[g % 2]
                    op("dve", [hh[0], hh[1]], [hh[0]], lambda e: e.tensor_tensor(out=hh[0].t[:], in0=hh[0].t[:], in1=hh[1].t[:], op=ALU.add))
                    op("dve", [hh[0], gc_], [ms], lambda e: e.tensor_tensor(out=ms.t[:], in0=hh[0].t[:], in1=gc_.t[:], op=ALU.mult))
                    dma("sp", dpool[2 + g % 2], mrg[(24 + g) * 128:(25 + g) * 128, :], ms.t[:], [ms], [mrg_b])

                c_load(0)
                c_load(1)
                c_conv(0)
                for g in range(8):
                    c_dir(g, 0)
                    if g + 1 < 8:
                        c_conv(g + 1)
                    c_dir(g, 1)
                    c_final(g)
                    if g + 2 < 8:
                        c_load(g + 2)
                kb.barrier()
            if "stop_C" in debug:
                break

            with ExitStack() as ph:
                XTs = [kb.sb(ph, "MT%d" % q, [128, 4, T], BF16) for q in range(8)]
                wp = mk_wpool(ph)
                xr = [kb.sb(ph, "xr%d" % i, [128, T], F32) for i in range(2)]
                for q in range(8):
                    dma("sp", dpool[4 + q], XTs[q].t[:], mrg_v[:, q * 4:(q + 1) * 4, :], [mrg_b], [XTs[q]])
                tokch = TOKCH if need_ctx else LAT_TOKCH

                def pre_out(g):
                    i = g % 2
                    dma("sp", dpool[i], xr[i].t[:], xT[g * 128:(g + 1) * 128, :], [xT_b], [xr[i]])

                def evac_out(g, banks):
                    i = g % 2
                    for (bk, t0, tn) in banks:
                        m = 1 if t0 < 256 else 0
                        op("dve", [bk, modT, xr[i]], [xr[i]], lambda e, bk=bk, t0=t0, tn=tn, m=m, i=i: e.scalar_tensor_tensor(
                            out=xr[i].t[:, t0:t0 + tn], in0=bk.t[:, 0:tn], scalar=modT.t[:, 64 + g, m:m + 1],
                            in1=xr[i].t[:, t0:t0 + tn], op0=ALU.mult, op1=ALU.add))
                    dma("sp", dpool[2 + i], xT[g * 128:(g + 1) * 128, :], xr[i].t[:], [xr[i]], [xT_b])

                linear(wp, w_out[l], KC, lambda k, t0, tn: XTs[k // 4].t[:, k % 4, t0:t0 + tn], lambda k: [XTs[k // 4]], tokch, evac_out, pre=pre_out)
                kb.barrier()

        if nlayers == DEPTH and not any(k.startswith("stop") for k in debug):
            with ExitStack() as ph:
                xch = [kb.sb(ph, "fx%d" % i, [128, KC, 256], F32) for i in range(2)]
                sq = [kb.sb(ph, "fsq%d" % i, [128, 8, 128], F32) for i in range(2)]
                rstd = [kb.sb(ph, "frs%d" % i, [128, 128], F32) for i in range(2)]
                yn = [kb.sb(ph, "fy%d" % i, [128, 4, 128], F32) for i in range(2)]
                ost = [kb.sb(ph, "fo%d" % i, [128, D], F32) for i in range(2)]
                SQ, YN = Rot(sq), Rot(yn)
                bks = {}

                def f_dma(tt):
                    if tt % 2 != 0:
                        return
                    i = (tt // 2) % 2
                    t0 = 256 + tt * 128
                    dma("sp", dpool[i], xch[i].t[:], xT_v[:, :, t0:t0 + 256], [xT_b], [xch[i]])

                def f_stageA(tt):
                    i = tt % 2
                    xb = xch[(tt // 2) % 2]
                    o_ = (tt % 2) * 128
                    bk = PS.next()
                    bks[tt] = bk
                    for c8 in range(4):
                        s = SQ.next()
                        op("dve", [xb], [s], lambda e, s=s, c8=c8: e.tensor_tensor(
                            out=s.t[:], in0=xb.t[:, c8 * 8:(c8 + 1) * 8, o_:o_ + 128], in1=xb.t[:, c8 * 8:(c8 + 1) * 8, o_:o_ + 128], op=ALU.mult))
                        for j in range(8):
                            op("pe", [s, cf], [bk], lambda e, s=s, j=j, bk=bk, c8=c8: e.matmul(
                                bk.t[:, 0:128], lhsT=onesf, rhs=s.t[:, j, :], start=(c8 == 0 and j == 0), stop=(c8 == 3 and j == 7)))

                def f_stageC(tt):
                    rstd_from(bks[tt], 128, rstd[tt % 2], 1.0 / D, 0)

                def f_stageB(tt):
                    i = tt % 2
                    xb = xch[(tt // 2) % 2]
                    o_ = (tt % 2) * 128
                    r = rstd[i]
                    for c4 in range(8):
                        y = YN.next()
                        for j in range(4):
                            c = c4 * 4 + j
                            op("dve", [xb, r, pp], [y], lambda e, c=c, j=j, y=y, r=r: e.scalar_tensor_tensor(
                                out=y.t[:, j, :], in0=xb.t[:, c, o_:o_ + 128], scalar=pp.t[:, PP_FNW + c:PP_FNW + c + 1], in1=r.t[:], op0=ALU.mult, op1=ALU.mult))
                        bk2 = PS.next()
                        for j in range(4):
                            op("pe", [y, cf], [bk2], lambda e, j=j, y=y, bk2=bk2: e.transpose(
                                out=bk2.t[:, j * 128:(j + 1) * 128], in_=y.t[:, j, :], identity=ident))
                        op("act", [bk2], [ost[i]], lambda e, bk2=bk2, c4=c4, i=i: e.copy(out=ost[i].t[:, c4 * 512:(c4 + 1) * 512], in_=bk2.t[:, 0:512]))
                    dma("sp", dpool[2 + i], out_d[tt * 128:(tt + 1) * 128, :], ost[i].t[:], [ost[i]], [])

                NT_ = SEQ // 128
                f_dma(0)
                f_stageA(0)
                f_stageC(0)
                for tt in range(NT_):
                    if tt + 1 < NT_:
                        f_dma(tt + 1)
                    f_stageB(tt)
                    if tt + 1 < NT_:
                        f_stageA(tt + 1)
                        f_stageC(tt + 1)
                kb.barrier()

        kb.barrier()
    return nc


def _fm(v, n):
    return np.ascontiguousarray(v.reshape(n, 128).T)


def host_consts():
    cf = np.zeros((128, 384), np.float32)
    cf[:, 0:128] = np.eye(128, dtype=np.float32)
    R = np.zeros((128, 128), np.float32)
    for base in range(0, 128, 32):
        for i in range(16):
            R[base + i, base + i + 16] = -1.0
            R[base + i + 16, base + i] = 1.0
    cf[:, 128:256] = R.T
    cf[:, 256:384] = 1.0
    t = np.arange(SEQ)
    rows = (t // 64).astype(np.float32)
    cols = (t % 64).astype(np.float32)
    freqs = (10000.0 ** (-np.arange(16, dtype=np.float32) / 16)).astype(np.float32)
    rope = np.zeros((128, 2, SEQ), np.float32)
    for p in range(128):
        d = p % 64
        part = d // 32
        i = (d % 32) % 16
        pos = rows if part == 0 else cols
        ang = (pos * freqs[i]).astype(np.float32)
        rope[p, 0] = np.cos(ang)
        rope[p, 1] = np.sin(ang)
    return cf, rope


def host_pp(inputs, b):
    pp = np.zeros((128, PP_N), np.float32)
    cc = np.stack([_fm(inputs["c"][b], 32), _fm(inputs["c_ctx"], 32)], axis=-1)
    pp[:, PP_CC:PP_CC + 64] = cc.reshape(128, 64)
    for l in range(DEPTH):
        pp[:, PP_ADAB + l * 96:PP_ADAB + (l + 1) * 96] = _fm(inputs["ada_b"][l], 96)
        pp[:, PP_NORMW + l * 32:PP_NORMW + (l + 1) * 32] = _fm(inputs["norm_w"][l], 32)
        pp[:, PP_LAMQ + l * 256:PP_LAMQ + (l + 1) * 256] = inputs["lambda_qk"][l].reshape(1, 256)
        pp[:, PP_SUBLN + l] = inputs["subln_w"][l]
        for j in range(4):
            pp[:, PP_CONVW + l * 32 + j:PP_CONVW + (l + 1) * 32:4] = _fm(inputs["conv_w"][l, j], 8)
        pp[:, PP_CONVB + l * 8:PP_CONVB + (l + 1) * 8] = _fm(inputs["conv_b"][l], 8)
        for d in range(2):
            o = (l * 2 + d) * 8
            pp[:, PP_RGBA + o:PP_RGBA + o + 8] = _fm(inputs["rg_ba"][l, d], 8)
            pp[:, PP_RGBX + o:PP_RGBX + o + 8] = _fm(inputs["rg_bx"][l, d], 8)
            pp[:, PP_RGLAM + o:PP_RGLAM + o + 8] = _fm(inputs["rg_lambda"][l, d], 8)
    pp[:, PP_FNW:PP_FNW + 32] = _fm(inputs["final_norm_w"], 32)
    return pp


def host_rgw(inputs):
    rgw = np.zeros((DEPTH, 8, 128, 4, 128), np.float32)
    for l in range(DEPTH):
        for d in range(2):
            for ax, key in enumerate(("rg_wa", "rg_wx")):
                w = inputs[key][l, d]
                for g in range(8):
                    rgw[l, g, 0:64, d * 2 + ax, 0:64] = w[2 * g]
                    rgw[l, g, 64:128, d * 2 + ax, 64:128] = w[2 * g + 1]
    return rgw


def host_bias(inputs):
    out = np.full((DEPTH, 12, 128, 25, 128), -1e30, np.float32)
    kk = np.arange(128)
    qq = np.arange(128)
    for cls, r in enumerate((0, 2, 28, 30, 4)):
        jb = min(max((r - 4) // 2, 0), 11)
        q_row = r + qq // 64
        q_col = qq % 64
        r0 = np.clip(q_row - 4, 0, 24)
        c0 = np.clip(q_col - 8, 0, 48)
        for j in range(5):
            k_row = 2 * (jb + j) + kk // 64
            k_col = kk % 64
            valid = ((k_row[:, None] >= r0[None, :]) & (k_row[:, None] < r0[None, :] + 8) &
                     (k_col[:, None] >= c0[None, :]) & (k_col[:, None] < c0[None, :] + 16))
            dr = np.clip(k_row[:, None] - q_row[None, :] + 7, 0, 14)
            dc = np.clip(k_col[:, None] - q_col[None, :] + 15, 0, 30)
            for l in range(DEPTH):
                vals = inputs["rpb"][l][:, dr, dc]
                out[l, :, :, cls * 5 + j, :] = np.where(valid[None], vals, np.float32(-1e30))
    return out


def make_in_maps(inputs, cores):
    cf, rope = host_consts()
    rgw = host_rgw(inputs)
    biasb = host_bias(inputs)
    maps = []
    for b in cores:
        maps.append({
            "x": np.ascontiguousarray(inputs["x"][b]), "ctx": np.ascontiguousarray(inputs["ctx"][b]),
            "pp": host_pp(inputs, b), "cf": cf, "rope": rope,
            "ada_w": inputs["ada_w"], "w_in": inputs["w_in"], "w_out": inputs["w_out"],
            "rgw": rgw, "biasb": biasb,
        })
    return maps


def kernel(**inputs):
    inputs = {k: np.asarray(v) for k, v in inputs.items()}
    nc = build_program()
    maps = make_in_maps(inputs, list(range(8)))
    res = run_bass_kernel_spmd(nc, maps, core_ids=list(range(8)))
    return np.stack([r["out"] for r in res.results], axis=0).astype(np.float32)
```
